# Optimizing a Trainium2 kernel written in Bass

```python
import math
import jax, jax.numpy as jnp
from jax import lax
import numpy as np

D_MODEL = 1024
BATCH = 8
SEQ = 4096
DEPTH = 2

HEAD_DIM = 64
N_HEADS = D_MODEL // HEAD_DIM
N_HEADS_A = N_HEADS // 4
N_HEADS_B = N_HEADS // 4
N_HEADS_C = N_HEADS - N_HEADS_A - N_HEADS_B
WIDTH_A = N_HEADS_A * HEAD_DIM
WIDTH_B = N_HEADS_B * HEAD_DIM
WIDTH_C = N_HEADS_C * HEAD_DIM
DIFF_DIM = HEAD_DIM // 2
ROPE_THETA = 500000.0
ROPE_FRACTION = 4
Q_BLOCK = 128
MOBA_BLOCK = 256
MOBA_TOPK = 3
MOBA_Q_CHUNK = 32
DILATED_PATTERNS = ((128, 1), (512, 4), (2048, 16))
SW_BLOCK = 128
D_FF = ((8 * D_MODEL // 3 + 255) // 256) * 256
CONV_WIDTH = 3
EPS = 1e-6
NEG_INF = -1e30

kernel_name = 'hybrid_diff_moba_dilated_block'


def rms_norm(x, g):
    xf = x.astype(jnp.float32)
    y = xf * lax.rsqrt(jnp.mean(xf * xf, axis=-1, keepdims=True) + EPS)
    return (y * g.astype(jnp.float32)).astype(x.dtype)


def partial_rope(x, positions):
    d = x.shape[-1]
    rot = d // ROPE_FRACTION
    half = rot // 2
    inv = jnp.power(jnp.float32(ROPE_THETA), -jnp.arange(half, dtype=jnp.float32) * 2.0 / rot)
    ang = positions.astype(jnp.float32)[..., None] * inv
    ang = ang.reshape((x.shape[0],) + (1,) * (x.ndim - 3) + ang.shape[1:])
    cos = jnp.cos(ang).astype(x.dtype)
    sin = jnp.sin(ang).astype(x.dtype)
    x1 = x[..., :half]
    x2 = x[..., half:rot]
    return jnp.concatenate([x1 * cos - x2 * sin, x2 * cos + x1 * sin, x[..., rot:]], axis=-1)


def to_heads(t, h):
    b, s, _ = t.shape
    return t.reshape(b, s, h, -1).transpose(0, 2, 1, 3)


def from_heads(t):
    b, h, s, d = t.shape
    return t.transpose(0, 2, 1, 3).reshape(b, s, h * d)


def differential_attention(q, k, v, positions, lam, g_head, lam_init):
    b, s, _ = q.shape
    h = N_HEADS_A
    q = partial_rope(q.reshape(b, s, h, 2, DIFF_DIM).transpose(0, 2, 3, 1, 4), positions)
    k = partial_rope(k.reshape(b, s, h, 2, DIFF_DIM).transpose(0, 2, 3, 1, 4), positions)
    vf = to_heads(v, h).astype(jnp.float32)
    nq = s // Q_BLOCK
    qb = q.reshape(b, h, 2, nq, Q_BLOCK, DIFF_DIM).transpose(3, 0, 1, 2, 4, 5)
    key_pos = jnp.arange(s)
    scale = DIFF_DIM ** -0.5

    def block(args):
        qi, i = args
        sc = jnp.einsum('bhcqd,bhckd->bhcqk', qi, k).astype(jnp.float32) * scale
        q_pos = i * Q_BLOCK + jnp.arange(Q_BLOCK)
        sc = jnp.where(key_pos[None, :] <= q_pos[:, None], sc, NEG_INF)
        p = jax.nn.softmax(sc, axis=-1)
        a = p[:, :, 0] - lam * p[:, :, 1]
        return jnp.einsum('bhqk,bhkd->bhqd', a, vf)

    o = lax.map(block, (qb, jnp.arange(nq)))
    o = o.transpose(1, 2, 0, 3, 4).reshape(b, h, s, HEAD_DIM)
    o = rms_norm(o, g_head) * (1.0 - lam_init)
    return from_heads(o)


def moba_attention(q, k, v, positions):
    b, s, _ = q.shape
    h = N_HEADS_B
    q = partial_rope(to_heads(q, h), positions)
    k = partial_rope(to_heads(k, h), positions)
    v = to_heads(v, h)
    nb = -(-s // MOBA_BLOCK)
    sp = nb * MOBA_BLOCK
    pad = ((0, 0), (0, 0), (0, sp - s), (0, 0))
    q, k, v = jnp.pad(q, pad), jnp.pad(k, pad), jnp.pad(v, pad)
    n_sel = max(1, min(MOBA_TOPK, nb - 1))
    kb = k.reshape(b, h, nb, MOBA_BLOCK, HEAD_DIM)
    vb = v.reshape(b, h, nb, MOBA_BLOCK, HEAD_DIM)
    k_mean = jnp.mean(kb.astype(jnp.float32), axis=3)
    gate = jnp.einsum('bhsd,bhnd->bhsn', q.astype(jnp.float32), k_mean)
    q_blk = jnp.arange(sp) // MOBA_BLOCK
    gate = jnp.where(jnp.arange(nb)[None, :] < q_blk[:, None], gate, NEG_INF)
    _, sel = lax.top_k(gate, n_sel)
    valid = jnp.arange(n_sel)[None, :] < q_blk[:, None]
    c = MOBA_Q_CHUNK
    nc = sp // c
    qc = q.reshape(b, h, nc, c, HEAD_DIM).transpose(2, 0, 1, 3, 4)
    selc = sel.reshape(b, h, nc, c, n_sel).transpose(2, 0, 1, 3, 4)
    validc = valid.reshape(nc, c, n_sel)
    b_idx = jnp.arange(b)[:, None, None, None]
    h_idx = jnp.arange(h)[None, :, None, None]
    scale = HEAD_DIM ** -0.5

    def chunk(args):
        qi, si, vi, ci = args
        kg = kb[b_idx, h_idx, si]
        vg = vb[b_idx, h_idx, si].astype(jnp.float32)
        q_pos = ci * c + jnp.arange(c)
        start = (ci * c) // MOBA_BLOCK * MOBA_BLOCK
        k_own = lax.dynamic_slice_in_dim(k, start, MOBA_BLOCK, axis=2)
        v_own = lax.dynamic_slice_in_dim(v, start, MOBA_BLOCK, axis=2).astype(jnp.float32)
        s_sel = jnp.einsum('bhqd,bhqnkd->bhqnk', qi, kg).astype(jnp.float32) * scale
        s_sel = jnp.where(vi[None, None, :, :, None], s_sel, NEG_INF).reshape(b, h, c, n_sel * MOBA_BLOCK)
        s_own = jnp.einsum('bhqd,bhkd->bhqk', qi, k_own).astype(jnp.float32) * scale
        k_pos = start + jnp.arange(MOBA_BLOCK)
        s_own = jnp.where(k_pos[None, :] <= q_pos[:, None], s_own, NEG_INF)
        p = jax.nn.softmax(jnp.concatenate([s_sel, s_own], axis=-1), axis=-1)
        p_sel = p[..., :n_sel * MOBA_BLOCK].reshape(b, h, c, n_sel, MOBA_BLOCK)
        p_own = p[..., n_sel * MOBA_BLOCK:]
        return (jnp.einsum('bhqnk,bhqnkd->bhqd', p_sel, vg)
                + jnp.einsum('bhqk,bhkd->bhqd', p_own, v_own))

    o = lax.map(chunk, (qc, selc, validc, jnp.arange(nc)))
    o = o.transpose(1, 2, 0, 3, 4).reshape(b, h, sp, HEAD_DIM)[:, :, :s]
    return from_heads(o)


def dilated_branch(q, k, v, window, dil):
    b, h, s, d = q.shape
    span = dil * SW_BLOCK
    sp = -(-s // span) * span
    m_len = sp // dil
    nb = m_len // SW_BLOCK
    w_sub = window // dil

    def sub(t):
        t = jnp.pad(t, ((0, 0), (0, 0), (0, sp - s), (0, 0)))
        return t.reshape(b, h, m_len, dil, d).transpose(0, 1, 3, 2, 4).reshape(b, h, dil, nb, SW_BLOCK, d)

    def band(t):
        prev = jnp.pad(t, ((0, 0), (0, 0), (0, 0), (1, 0), (0, 0), (0, 0)))[:, :, :, :-1]
        return jnp.concatenate([prev, t], axis=4)

    qs = sub(q)
    kw = band(sub(k))
    vw = band(sub(v)).astype(jnp.float32)
    sc = jnp.einsum('bhrnqd,bhrnkd->bhrnqk', qs, kw).astype(jnp.float32) * (d ** -0.5)
    qi = jnp.arange(SW_BLOCK)[:, None]
    kj = jnp.arange(2 * SW_BLOCK)[None, :] - SW_BLOCK
    dist = qi - kj
    blk = jnp.arange(nb)[:, None, None]
    mask = (dist >= 0) & (dist <= w_sub) & (blk * SW_BLOCK + kj >= 0)
    sc = jnp.where(mask, sc, NEG_INF)
    mx = jnp.max(sc, axis=-1, keepdims=True)
    e = jnp.exp(sc - mx)
    den = jnp.sum(e, axis=-1)
    o = jnp.einsum('bhrnqk,bhrnkd->bhrnqd', e, vw) / den[..., None]
    lse = mx[..., 0] + jnp.log(den)
    o = o.reshape(b, h, dil, m_len, d).transpose(0, 1, 3, 2, 4).reshape(b, h, sp, d)[:, :, :s]
    lse = lse.reshape(b, h, dil, m_len).transpose(0, 1, 3, 2).reshape(b, h, sp)[:, :, :s]
    return o, lse


def dilated_attention(q, k, v, positions):
    h = N_HEADS_C
    q = partial_rope(to_heads(q, h), positions)
    k = partial_rope(to_heads(k, h), positions)
    v = to_heads(v, h)
    outs, lses = [], []
    for window, dil in DILATED_PATTERNS:
        o, lse = dilated_branch(q, k, v, window, dil)
        outs.append(o)
        lses.append(lse)
    wts = jax.nn.softmax(jnp.stack(lses, axis=0), axis=0)
    o = jnp.sum(wts[..., None] * jnp.stack(outs, axis=0), axis=0)
    return from_heads(o)


def conv_gated_mlp(h, w_up, conv_w, conv_b, w_down):
    u = h @ w_up
    u = lax.conv_general_dilated(u, conv_w[:, None, :].astype(u.dtype), window_strides=(1,),
                                 padding=[(CONV_WIDTH - 1, 0)],
                                 dimension_numbers=('NWC', 'WIO', 'NWC'),
                                 feature_group_count=u.shape[-1]) + conv_b
    gate, val = jnp.split(u, 2, axis=-1)
    return (jax.nn.silu(gate) * val) @ w_down


def setup_inputs(seed: int = 0) -> dict:
    key = jax.random.key(seed)
    ks = jax.random.split(key, 16)
    nrm = jax.random.normal
    return {
        'x': nrm(ks[0], (BATCH, SEQ, D_MODEL), jnp.float32),
        'positions': jnp.broadcast_to(jnp.arange(SEQ, dtype=jnp.int32)[None, :], (BATCH, SEQ)),
        'g_mix': 1.0 + 0.02 * nrm(ks[1], (DEPTH, D_MODEL), jnp.float32),
        'w_in': nrm(ks[2], (DEPTH, D_MODEL, 3 * D_MODEL), jnp.float32) * D_MODEL ** -0.5,
        'w_out': nrm(ks[3], (DEPTH, D_MODEL, D_MODEL), jnp.float32) * D_MODEL ** -0.5,
        'lambda_q1': 0.1 * nrm(ks[4], (DEPTH, DIFF_DIM), jnp.float32),
        'lambda_k1': 0.1 * nrm(ks[5], (DEPTH, DIFF_DIM), jnp.float32),
        'lambda_q2': 0.1 * nrm(ks[6], (DEPTH, DIFF_DIM), jnp.float32),
        'lambda_k2': 0.1 * nrm(ks[7], (DEPTH, DIFF_DIM), jnp.float32),
        'g_diff': 1.0 + 0.02 * nrm(ks[8], (DEPTH, HEAD_DIM), jnp.float32),
        'g_ffn': 1.0 + 0.02 * nrm(ks[9], (DEPTH, D_MODEL), jnp.float32),
        'w_up': nrm(ks[10], (DEPTH, D_MODEL, 2 * D_FF), jnp.float32) * D_MODEL ** -0.5,
        'conv_w': nrm(ks[11], (DEPTH, CONV_WIDTH, 2 * D_FF), jnp.float32) * CONV_WIDTH ** -0.5,
        'conv_b': 0.01 * nrm(ks[12], (DEPTH, 2 * D_FF), jnp.float32),
        'w_down': nrm(ks[13], (DEPTH, D_FF, D_MODEL), jnp.float32) * D_FF ** -0.5,
        'g_final': 1.0 + 0.02 * nrm(ks[14], (D_MODEL,), jnp.float32),
    }


def reference(x, positions, g_mix, w_in, w_out, lambda_q1, lambda_k1, lambda_q2, lambda_k2,
              g_diff, g_ffn, w_up, conv_w, conv_b, w_down, g_final):
    widths = [WIDTH_A] * 3 + [WIDTH_B] * 3 + [WIDTH_C] * 3
    cuts = []
    acc = 0
    for wd in widths[:-1]:
        acc += wd
        cuts.append(acc)
    for layer in range(DEPTH):
        h = rms_norm(x, g_mix[layer])
        qkv = h @ w_in[layer]
        qa, ka, va, qb, kb, vb, qc, kc, vc = jnp.split(qkv, cuts, axis=-1)
        lam_init = 0.8 - 0.6 * math.exp(-0.3 * layer)
        lam = (jnp.exp(jnp.sum(lambda_q1[layer].astype(jnp.float32) * lambda_k1[layer].astype(jnp.float32)))
               - jnp.exp(jnp.sum(lambda_q2[layer].astype(jnp.float32) * lambda_k2[layer].astype(jnp.float32)))
               + lam_init)
        o_a = differential_attention(qa, ka, va, positions, lam, g_diff[layer], lam_init).astype(x.dtype)
        o_b = moba_attention(qb, kb, vb, positions).astype(x.dtype)
        o_c = dilated_attention(qc, kc, vc, positions).astype(x.dtype)
        mix = jnp.concatenate([o_a, o_b, o_c], axis=-1)
        x = x + mix @ w_out[layer]
        h2 = rms_norm(x, g_ffn[layer])
        x = x + conv_gated_mlp(h2, w_up[layer], conv_w[layer], conv_b[layer], w_down[layer]).astype(x.dtype)
    return rms_norm(x, g_final)
```

```python
import math
import os
from contextlib import ExitStack

import numpy as np
import ml_dtypes
import concourse.bass as bass
import concourse.mybir as mybir
from concourse.bass_utils import run_bass_kernel_spmd

F32 = mybir.dt.float32
BF16 = mybir.dt.bfloat16
I32 = mybir.dt.int32
ALU = mybir.AluOpType
AF = mybir.ActivationFunctionType

D = 1024
DFF = 2816
NFF = 2 * DFF
NCH_FF = NFF // 128
DEPTH = 2
EPS = 1e-6
THETA = 500000.0
NEGM = -30000.0
TA = 512
TC = 256


class Buf:
    def __init__(self, name, sem=None):
        self.name = name
        self.sem = sem
        self.writers = []
        self.readers = []
        self.excl = False


class _Rec:
    def __init__(self):
        self.call = None

    def __getattr__(self, name):
        def f(*a, **k):
            self.call = (name, a, k)
            return self
        return f


class Sched:
    def __init__(self, nc, es):
        self.nc = nc
        self.es = es
        self.names = ["tensor", "vector", "scalar", "gpsimd", "sync"]
        self.sem = {}
        self.cnt = {}
        self.isdma = {}
        for n in self.names:
            self.sem[n] = es.enter_context(nc.semaphore("e_" + n))
            self.cnt[n] = 0
            self.isdma[n] = False
        self.prog = {n: [] for n in self.names}
        self.waited = {n: {} for n in self.names}
        self.nslot = 0

    def buf(self, name, dma=False):
        b = Buf(name)
        if dma:
            self.nslot += 1
            k = "d%d" % self.nslot
            self.sem[k] = self.es.enter_context(self.nc.semaphore(k))
            self.cnt[k] = 0
            self.isdma[k] = True
            b.sem = k
        return b

    def _waits(self, eng, deps):
        for (k, v) in deps:
            if k == "tensor" and eng == "tensor":
                continue
            if self.isdma[k]:
                v = max(v, self.cnt[k])
            if self.waited[eng].get(k, 0) >= v:
                continue
            self.waited[eng][k] = v
            sem = self.sem[k]
            self.prog[eng].append(lambda e, sem=sem, v=v: e.wait_ge(sem, v))

    def _deps(self, reads, writes):
        deps = []
        for b in reads:
            deps += b.writers
            if b.excl:
                deps += b.readers
        for b in writes:
            deps += b.writers + b.readers
        return deps

    def _commit(self, tok, reads, writes):
        for b in reads:
            b.readers.append(tok)
        for b in writes:
            b.writers = [tok]
            b.readers = []

    def op(self, eng, fn, reads=(), writes=()):
        self._waits(eng, self._deps(reads, writes))
        self.cnt[eng] += 1
        sem = self.sem[eng]
        rec = _Rec()
        fn(rec)
        name, a, k = rec.call
        self.prog[eng].append(lambda e, name=name, a=a, k=k, sem=sem: getattr(e, name)(*a, **k).then_inc(sem, 1))
        tok = (eng, self.cnt[eng])
        self._commit(tok, reads, writes)
        return tok

    def dma(self, queue, sbuf, out, in_, reads=(), writes=(), **kw):
        self._waits(queue, self._deps(reads, writes))
        k = sbuf.sem
        self.cnt[k] += 16
        sem = self.sem[k]
        self.prog[queue].append(
            lambda e, sem=sem, out=out, in_=in_, kw=kw: e.dma_start(out=out, in_=in_, **kw).then_inc(sem, 16))
        tok = (k, self.cnt[k])
        self._commit(tok, reads, writes)
        return tok

    def wait_all(self, eng, bufs):
        deps = []
        for b in bufs:
            deps += b.writers + b.readers
        self._waits(eng, deps)

    def final_wait(self, eng="sync"):
        for k in list(self.sem.keys()):
            if self.cnt[k] > 0 and k != eng:
                v = self.cnt[k]
                if self.waited[eng].get(k, 0) >= v:
                    continue
                self.waited[eng][k] = v
                sem = self.sem[k]
                self.prog[eng].append(lambda e, sem=sem, v=v: e.wait_ge(sem, v))

    def emit(self):
        nc = self.nc
        with nc.Block() as block:
            @block.tensor
            def _(e):
                for f in self.prog["tensor"]:
                    f(e)

            @block.vector
            def _(e):
                for f in self.prog["vector"]:
                    f(e)

            @block.scalar
            def _(e):
                for f in self.prog["scalar"]:
                    f(e)

            @block.gpsimd
            def _(e):
                for f in self.prog["gpsimd"]:
                    f(e)

            @block.sync
            def _(e):
                for f in self.prog["sync"]:
                    f(e)


CB_IDENT = 0
CB_TRI = 128
CB_OWN4 = 256
CB_PREV4 = 768
CB_ONES = 1280
CB_PERMA = 1408
CB_PERMB = 1536
CB_N = 1664
CF_INV = 0
CF_EPS = 4
CF_M0 = 5
CF_M1 = 6
CF_MASKTAB = 8
CF_OWNTAB = 8 + 256
CF_N = 8 + 512


def _host_consts():
    cb = np.zeros((128, CB_N), np.float32)
    cb[:, CB_IDENT:CB_IDENT + 128] = np.eye(128)
    p = np.arange(128)[:, None]
    f = np.arange(128)[None, :]
    tri = np.where(p <= f, 0.0, NEGM)
    prev = np.where(p >= f, 0.0, NEGM)
    cb[:, CB_TRI:CB_TRI + 128] = tri
    cb[:, CB_OWN4:CB_OWN4 + 512] = np.tile(tri, (1, 4))
    cb[:, CB_PREV4:CB_PREV4 + 512] = np.tile(prev, (1, 4))
    cb[:, CB_ONES:CB_ONES + 128] = 1.0
    cf = np.zeros((128, CF_N), np.float32)
    permA = np.zeros((128, 128), np.float32)
    permB = np.zeros((128, 128), np.float32)
    for m in range(128):
        r = m % 32
        if r < 4:
            permA[m + 4, m] = 1.0
            cf[m, CF_INV + 0] = THETA ** (-(r) / 4.0)
            cf[m, CF_INV + 1] = -(THETA ** (-(r) / 4.0))
        elif r < 8:
            permA[m - 4, m] = 1.0
            cf[m, CF_INV + 0] = THETA ** (-(r - 4) / 4.0)
            cf[m, CF_INV + 1] = THETA ** (-(r - 4) / 4.0)
        r = m % 64
        if r < 8:
            permB[m + 8, m] = 1.0
            cf[m, CF_INV + 2] = THETA ** (-(r) / 8.0)
            cf[m, CF_INV + 3] = -(THETA ** (-(r) / 8.0))
        elif r < 16:
            permB[m - 8, m] = 1.0
            cf[m, CF_INV + 2] = THETA ** (-(r - 8) / 8.0)
            cf[m, CF_INV + 3] = THETA ** (-(r - 8) / 8.0)
    cb[:, CB_PERMA:CB_PERMA + 128] = permA
    cb[:, CB_PERMB:CB_PERMB + 128] = permB
    cf[:, CF_EPS] = EPS
    cf[:, CF_M0] = ((np.arange(128) % 64) < 32).astype(np.float32)
    cf[:, CF_M1] = ((np.arange(128) % 64) >= 32).astype(np.float32)
    for qb in range(16):
        for j in range(16):
            cf[:, CF_MASKTAB + qb * 16 + j] = 0.0 if j < qb else -1e30
            cf[:, CF_OWNTAB + qb * 16 + j] = 1.0 if j == qb else 0.0
    return cb.astype(ml_dtypes.bfloat16), cf


def _kind(S):
    k = np.zeros((16, S), np.float32)
    for j in range(S // 256):
        k[j, 256 * j:256 * (j + 1)] = 1.0
    return k.astype(ml_dtypes.bfloat16)


PC_GMIX = 0
PC_GFFN = 16
PC_GFIN = 32
PC_CONVW = 40
PC_CONVB = 40 + 264
PC_GDIFF = 40 + 264 + 88
PC_N = PC_GDIFF + 2


def build(S=4096, layers=(0, 1), final=True, first=True, dbg=None):
    NT_A = S // TA
    NT_C = S // TC
    NKT = S // 128
    nc = bass.Bass("TRN2", target_bir_lowering=False)
    dt_in = lambda n, sh, dt: nc.dram_tensor(n, sh, dt, kind="ExternalInput").ap()
    xT = dt_in("xT", [D, S], F32)
    pos = dt_in("pos", [1, S], I32)
    w_in = dt_in("w_in", [DEPTH, D, 3 * D], F32)
    w_out = dt_in("w_out", [DEPTH, D, D], F32)
    w_up = dt_in("w_up", [DEPTH, D, NFF], F32)
    w_down = dt_in("w_down", [DEPTH, DFF, D], F32)
    pcols = dt_in("pcols", [128, PC_N], F32)
    lam_in = dt_in("lam_in", [1, DEPTH * 4 * 32], F32)
    cb_in = dt_in("cb", [128, CB_N], BF16)
    cf_in = dt_in("cf", [128, CF_N], F32)
    kind_in = dt_in("kind", [16, S], BF16)
    outT = nc.dram_tensor("outT", [D, S], F32, kind="ExternalOutput").ap()
    scr = lambda n, sh, dt: nc.dram_tensor(n, sh, dt).ap()
    qA_d = scr("qA_d", [8, 64, S], BF16)
    kA_d = scr("kA_d", [4, 64, S], BF16)
    qB_d = scr("qB_d", [4, 80, S], BF16)
    kB_d = scr("kB_d", [4, 64, S], BF16)
    qC_d = scr("qC_d", [8, 64, S], BF16)
    kC_d = scr("kC_d", [8, 64, S], BF16)
    vaug_d = scr("vaug_d", [S, 16, 128], BF16)
    mixT_d = scr("mixT_d", [D, S], BF16)
    xres_d = scr("xres_d", [D, S], F32)
    tabs_d = scr("tabs_d", [4, 128, S], BF16)

    with ExitStack() as es:
        S_ = Sched(nc, es)
        sbt = lambda n, sh, dt: es.enter_context(nc.sbuf_tensor(n, sh, dt))
        R = sbt("R", [128, 75392], BF16)
        WO = sbt("WO", [128, 8, D], BF16)
        T = sbt("T", [128, 16448], BF16)
        cb = sbt("cbs", [128, CB_N], BF16)
        cf = sbt("cfs", [128, CF_N], F32)
        pc = sbt("pcs", [128, PC_N], F32)
        lamv = sbt("lamv", [128, DEPTH * 4 * 32], F32)
        small = sbt("small", [128, 64], F32)
        kmean = sbt("kmean", [128, 2, 16], F32)
        psbig = es.enter_context(nc.psum_tensor("psbig", [128, 4096], F32))
        pss = [psbig[:, 512 * i:512 * (i + 1)] for i in range(8)]
        PS = [S_.buf("ps%d" % i) for i in range(8)]
        for b_ in PS:
            b_.excl = True

        B_cb = S_.buf("cb", dma=True)
        B_cf = S_.buf("cf", dma=True)
        B_pc = S_.buf("pc", dma=True)
        B_lamv = S_.buf("lamv", dma=True)
        B_small = S_.buf("small")
        B_kmean = S_.buf("kmean")
        B_halo = [S_.buf("halo%d" % i) for i in range(NCH_FF)]
        B_vstg = [S_.buf("vstg%d" % i, dma=True) for i in range(2)]
        B_qA = [S_.buf("qA%d" % i) for i in range(8)]
        B_kA = [S_.buf("kA%d" % i) for i in range(4)]
        B_qB = [S_.buf("qB%d" % i) for i in range(4)]
        B_kB = [S_.buf("kB%d" % i) for i in range(4)]
        B_qC = [S_.buf("qC%d" % i) for i in range(8)]
        B_kC = [S_.buf("kC%d" % i) for i in range(8)]
        B_vaug = S_.buf("vaug")
        B_mixT = S_.buf("mixT")
        B_xres = S_.buf("xres")
        B_tabs = S_.buf("tabs")
        B_tabs_t = [S_.buf("tabs%d" % i) for i in range(NT_A)]
        B_R = S_.buf("Rarena")
        B_T = S_.buf("Tarena")

        ident = cb[:, CB_IDENT:CB_IDENT + 128]
        ones_bf = cb[:, CB_ONES:CB_ONES + 128]

        def Rv(off, n, dt=BF16, parts=slice(0, 128)):
            a = R[parts, off:off + n]
            return a if dt == BF16 else a.bitcast(dt)

        def Tv(off, n, dt=BF16, parts=slice(0, 128)):
            a = T[parts, off:off + n]
            return a if dt == BF16 else a.bitcast(dt)

        S_.dma("sync", B_cb, cb[:], cb_in, writes=[B_cb])
        S_.dma("sync", B_cf, cf[:], cf_in, writes=[B_cf])
        S_.dma("sync", B_pc, pc[:], pcols, writes=[B_pc])
        S_.dma("sync", B_lamv, lamv[:], bass.AP(lam_in.tensor, 0, [[0, 128], [1, DEPTH * 4 * 32]]), writes=[B_lamv])
        S_.op("gpsimd", lambda e: e.memset(kmean[:], 0.0), writes=[B_kmean])
        S_.op("gpsimd", lambda e: e.memset(small[:], 0.0), writes=[B_small])

        def col(off, parts=slice(0, 128)):
            return pc[parts, off:off + 1]

        def ccol(off, parts=slice(0, 128)):
            return cf[parts, off:off + 1]

        def lam_setup(l):
            lam_init = 0.8 - 0.6 * math.exp(-0.3 * l)
            base = l * 128
            prod = small[:, 32:64]
            for i, (a, b) in enumerate(((0, 1), (2, 3))):
                S_.op("vector", lambda e, a=a, b=b: e.tensor_tensor(
                    out=prod, in0=lamv[:, base + 32 * a:base + 32 * a + 32],
                    in1=lamv[:, base + 32 * b:base + 32 * b + 32], op=ALU.mult),
                    reads=[B_lamv], writes=[B_small])
                S_.op("vector", lambda e, i=i: e.tensor_reduce(
                    out=small[:, 16 + i:17 + i], in_=prod, axis=mybir.AxisListType.X, op=ALU.add),
                    reads=[B_small], writes=[B_small])
            S_.op("scalar", lambda e: e.activation(out=small[:, 18:20], in_=small[:, 16:18], func=AF.Exp),
                  reads=[B_small], writes=[B_small])
            S_.op("vector", lambda e: e.tensor_tensor(out=small[:, 4 * l:4 * l + 1], in0=small[:, 18:19],
                                                     in1=small[:, 19:20], op=ALU.subtract),
                  reads=[B_small], writes=[B_small])
            S_.op("vector", lambda e: e.tensor_scalar(out=small[:, 4 * l + 1:4 * l + 2], in0=small[:, 4 * l:4 * l + 1],
                                                     scalar1=lam_init, scalar2=-1.0, op0=ALU.add, op1=ALU.mult),
                  reads=[B_small], writes=[B_small])
            S_.op("vector", lambda e: e.tensor_scalar(out=small[:, 4 * l + 2:4 * l + 3],
                                                     in0=pc[:, PC_GDIFF + l:PC_GDIFF + l + 1],
                                                     scalar1=1.0 - lam_init, scalar2=None, op0=ALU.mult),
                  reads=[B_pc, B_small], writes=[B_small])

        for l in layers:
            lam_setup(l)

        TBS = {}

        def tables_init():
            pos_i = Tv(0, 2 * TA, I32)
            ang = Tv(2 * TA, 2 * TA, F32)
            kk = Tv(4 * TA, 2 * TA, F32)
            tb = [Tv(6 * TA + i * TA, TA, BF16) for i in range(2)]
            Bp = S_.buf("pos_i", dma=True)
            Ba = S_.buf("ang")
            Bk = S_.buf("kk")
            Btb = [S_.buf("tb%d" % i, dma=True) for i in range(2)]
            TWO_PI = 2 * np.pi
            c1 = float(np.float32(6.28125))
            c2 = float(np.float32(TWO_PI - 6.28125))
            c3 = float(TWO_PI - 6.28125 - float(np.float32(TWO_PI - 6.28125)))
            MAGIC = 12582912.0
            TBS["posf"] = Tv(8 * TA, 2 * TA, F32)
            TBS["Bpf"] = S_.buf("posf")
            TBS.update(dict(pos_i=pos_i, ang=ang, kk=kk, tb=tb, Bp=Bp, Ba=Ba, Bk=Bk, Btb=Btb, TWO_PI=TWO_PI,
                            c1=c1, c2=c2, c3=c3, MAGIC=MAGIC, n=0))
            return [Bp, Ba, Bk, TBS["Bpf"]] + Btb

        def tables_tile(t):
            pos_i, ang, kk, tb, Bp, Ba, Bk, Btb = (TBS[k_] for k_ in ("pos_i", "ang", "kk", "tb", "Bp", "Ba", "Bk", "Btb"))
            TWO_PI, c1, c2, c3, MAGIC = (TBS[k_] for k_ in ("TWO_PI", "c1", "c2", "c3", "MAGIC"))
            n = TBS["n"]
            for t in [t]:
                cs = slice(t * TA, (t + 1) * TA)
                S_.dma("sync", Bp, pos_i, bass.AP(pos.tensor, t * TA, [[0, 128], [1, TA]]), writes=[Bp])
                S_.op("vector", lambda e: e.tensor_copy(out=TBS["posf"], in_=pos_i), reads=[Bp], writes=[TBS["Bpf"]])
                for k in range(4):
                    shift = (np.pi / 2) if k in (0, 2) else 0.0
                    S_.op("vector", lambda e, k=k, shift=shift: e.tensor_scalar(
                        out=ang, in0=TBS["posf"], scalar1=ccol(CF_INV + k), scalar2=shift, op0=ALU.mult, op1=ALU.add),
                        reads=[TBS["Bpf"], B_cf], writes=[Ba])
                    S_.op("vector", lambda e: e.tensor_scalar(out=kk, in0=ang, scalar1=1.0 / TWO_PI, scalar2=MAGIC,
                                                             op0=ALU.mult, op1=ALU.add), reads=[Ba], writes=[Bk])
                    S_.op("vector", lambda e: e.tensor_scalar(out=kk, in0=kk, scalar1=MAGIC, scalar2=None,
                                                             op0=ALU.subtract), reads=[Bk], writes=[Bk])
                    for cc in (c1, c2):
                        S_.op("vector", lambda e, cc=cc: e.scalar_tensor_tensor(
                            out=ang, in0=kk, scalar=-cc, in1=ang, op0=ALU.mult, op1=ALU.add),
                            reads=[Bk, Ba], writes=[Ba])
                    S_.op("vector", lambda e: e.tensor_scalar(out=ang, in0=ang, scalar1=3.1415925, scalar2=-3.1415925,
                                                             op0=ALU.min, op1=ALU.max), reads=[Ba], writes=[Ba])
                    i = n % 2
                    n += 1
                    S_.op("scalar", lambda e, i=i: e.activation(out=tb[i], in_=ang, func=AF.Sin),
                          reads=[Ba], writes=[Btb[i]])
                    S_.dma("sync", Btb[i], tabs_d[k, :, cs], tb[i], reads=[Btb[i]], writes=[B_tabs_t[t]])
            TBS["n"] = n

        WIN_OFF = 0
        WUP_OFF = 0
        WDN_OFF = 8 * NFF
        B_win = S_.buf("win", dma=True)
        B_wup = S_.buf("wup", dma=True)
        B_wdn = S_.buf("wdn", dma=True)
        B_wo = S_.buf("wo", dma=True)

        def Win(k, c0, n):
            return R[:, WIN_OFF + k * 3072 + c0:WIN_OFF + k * 3072 + c0 + n]

        def Wup(k, c0, n):
            return R[:, WUP_OFF + k * NFF + c0:WUP_OFF + k * NFF + c0 + n]

        def Wdn(j, c0, n):
            return R[:, WDN_OFF + j * D + c0:WDN_OFF + j * D + c0 + n]

        P_win = [S_.buf("win%d" % i, dma=True) for i in range(6)]
        P_wupg = [S_.buf("wupg%d" % i, dma=True) for i in range(6)]
        P_wupv = [S_.buf("wupv%d" % i, dma=True) for i in range(6)]
        P_wdn = [S_.buf("wdn%d" % i, dma=True) for i in range(6)]

        def _piece(Bw, dst, src):
            S_.dma("gpsimd", Bw, dst, src, writes=[Bw])

        def load_win(l):
            dst = R[:, WIN_OFF:WIN_OFF + 8 * 3072].rearrange("p (k n) -> p k n", k=8)
            sv = w_in[l].rearrange("(k p) n -> p k n", p=128)
            for i in range(6):
                _piece(P_win[i], dst[:, :, 512 * i:512 * i + 512], sv[:, :, 512 * i:512 * i + 512])

        def load_ffn(l):
            dup = R[:, WUP_OFF:WUP_OFF + 8 * NFF].rearrange("p (k n) -> p k n", k=8)
            sup = w_up[l].rearrange("(k p) n -> p k n", p=128)
            ddn = R[:, WDN_OFF:WDN_OFF + 22 * D].rearrange("p (k n) -> p k n", k=22)
            sdn = w_down[l].rearrange("(k p) n -> p k n", p=128)
            for i in range(6):
                w_ = 512 if i < 5 else 256
                _piece(P_wupg[i], dup[:, :, 512 * i:512 * i + w_], sup[:, :, 512 * i:512 * i + w_])
                _piece(P_wupv[i], dup[:, :, DFF + 512 * i:DFF + 512 * i + w_], sup[:, :, DFF + 512 * i:DFF + 512 * i + w_])
                j0, j1 = 4 * i, min(4 * i + 4, 22)
                _piece(P_wdn[i], ddn[:, j0:j1, :], sdn[:, j0:j1, :])

        def load_wo(l):
            sv = w_out[l].rearrange("(k p) n -> p k n", p=128)
            first = True
            for c0 in range(0, D, 512):
                S_.dma("gpsimd", B_wo, WO[:, :, c0:c0 + 512], sv[:, :, c0:c0 + 512], writes=[B_wo] if first else [])
                first = False
            B_wo.writers = [(B_wo.sem, S_.cnt[B_wo.sem])]

        CH = ([("kA", i, 128 * i) for i in range(2)] + [("kB", i, 256 + 128 * i) for i in range(2)] +
              [("kC", i, 512 + 128 * i) for i in range(4)] + [("qA", i, 1024 + 128 * i) for i in range(2)] +
              [("qB", i, 1280 + 128 * i) for i in range(2)] + [("qC", i, 1536 + 128 * i) for i in range(4)])
        VCOL = 2048

        def phase_A(l, xsrc, B_xsrc):
            A0 = 8 * 3072
            o2 = 67584 + 4096
            xt = [Rv(A0 + i * 8192, 8192, F32).rearrange("p (c t) -> p c t", c=8) for i in range(2)]
            o = A0 + 16384
            tab = [Rv(o + i * 2048, 2048).rearrange("p (k t) -> p k t", k=4) for i in range(2)]
            o += 4096
            hT = Rv(o, 4096).rearrange("p (c t) -> p c t", c=8)
            o += 4096
            sq = Rv(o, 4096).rearrange("p (c t) -> p c t", c=8)
            o += 4096
            rstd = Rv(o, 1024, F32)
            o += 1024
            lnt = Rv(o, 1024, F32)
            o += 1024
            t1 = [Rv(o + i * 1024, 1024, F32) for i in range(2)]
            o += 2048
            t2 = [Rv(o + i * 1024, 1024, F32) for i in range(2)]
            o += 2048
            r32 = [Rv(o + i * 1024, 1024, F32) for i in range(4)]
            o += 4096
            qb16 = [Rv(o + i * 512, 512) for i in range(2)]
            o += 1024
            stg = [Rv(o + i * 512, 512) for i in range(4)]
            o += 2048
            gm = Rv(o, 256, F32)
            o += 256
            t8 = Rv(o, 64, F32)
            o += 64
            thr = Rv(o, 8, F32)
            o += 8
            btoks = [Rv(o, 64), Rv(o2 + 2048 + 64, 64)]
            B_btok = [S_.buf("btok0"), S_.buf("btok1")]
            o += 64
            stgB = Rv(o, 512, parts=slice(0, 64))
            o += 512
            assert o <= 67584
            qh = [Rv(o2 + i * 512, 512) for i in range(2)]
            ql = [Rv(o2 + 1024 + i * 512, 512) for i in range(2)]
            kmh = Rv(o2 + 2048, 32).rearrange("p (c j) -> p c j", c=2)
            kml = Rv(o2 + 2048 + 32, 32).rearrange("p (c j) -> p c j", c=2)
            assert o2 + 2048 + 128 <= 75392
            B_qh = [S_.buf("qh%d" % i, dma=True) for i in range(2)]
            B_ql = [S_.buf("ql%d" % i) for i in range(2)]
            B_kmhl = S_.buf("kmhl")
            vstg = [Rv(67584 + i * 2048, 2048).rearrange("p (h c) -> p h c", h=16) for i in range(2)]
            for i in range(2):
                S_.op("gpsimd", lambda e, i=i: e.memset(vstg[i], 1.0), writes=[B_vstg[i]])
            B_xt = [S_.buf("xt%d" % i, dma=True) for i in range(2)]
            B_tab = [S_.buf("tab%d" % i, dma=True) for i in range(2)]
            B_hT = S_.buf("hT")
            B_sq = S_.buf("sq")
            B_sqc = [S_.buf("sq%d" % c) for c in range(8)]
            B_rstd = S_.buf("rstd")
            B_lnt = S_.buf("lnt")
            B_t1 = [S_.buf("t1%d" % i) for i in range(2)]
            B_t2 = [S_.buf("t2%d" % i) for i in range(2)]
            B_r32 = [S_.buf("r32%d" % i) for i in range(4)]
            B_qb16 = [S_.buf("qb16%d" % i) for i in range(2)]
            B_stg = [S_.buf("stg%d" % i, dma=True) for i in range(4)]
            B_gm = S_.buf("gm")
            B_stgB = S_.buf("stgB", dma=True)
            xv = xsrc.rearrange("(c p) t -> p c t", p=128)
            tv = tabs_d.rearrange("k p t -> p k t")

            def gate_sub(tg, s4):
                qblk = (tg * TA + s4 * 128) // 256
                if s4 == 0:
                    S_.op("vector", lambda e: e.tensor_copy(out=kmh, in_=kmean[:]), reads=[B_kmean], writes=[B_kmhl])
                    S_.op("vector", lambda e: e.tensor_tensor(out=kml, in0=kmean[:], in1=kmh, op=ALU.subtract),
                          reads=[B_kmean, B_kmhl], writes=[B_kmhl])
                for h in range(4):
                    cbk, hh = h // 2, h % 2
                    pr_ = slice(64 * hh, 64 * hh + 64)
                    tk_ = slice(128 * s4, 128 * s4 + 128)
                    for mi, (qa_, ka_) in enumerate(((qh, kmh), (qh, kml), (ql, kmh))):
                        gb_ = 7 - hh
                        S_.op("tensor", lambda e, h=h, mi=mi, qa_=qa_, ka_=ka_: e.matmul(
                            pss[gb_][:, 16 * h:16 * h + 16], qa_[cbk][pr_, tk_], ka_[pr_, cbk, :],
                            start=(mi == 0), stop=(mi == 2)),
                            reads=[B_qh[cbk], B_ql[cbk], B_kmhl], writes=[PS[gb_]])
                for h in range(4):
                    S_.op("vector", lambda e, h=h: e.tensor_tensor(
                        out=gm[:, 16 * h:16 * h + 16], in0=pss[7 - (h % 2)][:, 16 * h:16 * h + 16],
                        in1=cf[:, CF_MASKTAB + 16 * qblk:CF_MASKTAB + 16 * qblk + 16],
                        op=ALU.add), reads=[PS[7 - (h % 2)], B_cf], writes=[B_gm])
                for h in range(4):
                    S_.op("vector", lambda e, h=h: e.max(out=t8[:, 8 * h:8 * h + 8], in_=gm[:, 16 * h:16 * h + 16]),
                          reads=[B_gm], writes=[B_gm])
                S_.op("vector", lambda e: e.tensor_scalar(
                    out=thr[:, 0:4], in0=t8[:, 2:32:8], scalar1=-1e29, scalar2=None, op0=ALU.max),
                    reads=[B_gm], writes=[B_gm])
                for h in range(4):
                    S_.op("vector", lambda e, h=h: e.tensor_scalar(
                        out=gm[:, 64 + 16 * h:64 + 16 * h + 16], in0=gm[:, 16 * h:16 * h + 16],
                        scalar1=thr[:, h:h + 1], scalar2=None, op0=ALU.is_ge), reads=[B_gm], writes=[B_gm])
                for h in range(4):
                    S_.op("vector", lambda e, h=h: e.tensor_tensor(
                        out=gm[:, 64 + 16 * h:64 + 16 * h + 16], in0=gm[:, 64 + 16 * h:64 + 16 * h + 16],
                        in1=cf[:, CF_OWNTAB + 16 * qblk:CF_OWNTAB + 16 * qblk + 16], op=ALU.max),
                        reads=[B_gm, B_cf], writes=[B_gm])
                S_.op("vector", lambda e: e.tensor_scalar(out=btoks[s4 % 2], in0=gm[:, 64:128], scalar1=-NEGM, scalar2=NEGM,
                                                         op0=ALU.mult, op1=ALU.add), reads=[B_gm], writes=[B_btok[s4 % 2]])

            def gate_sub2(tg, s4):
                S_.op("tensor", lambda e: e.matmul(pss[5][0:64, 256:384], btoks[s4 % 2], ident, start=True, stop=True),
                      reads=[B_btok[s4 % 2], B_cb], writes=[PS[5]])
                S_.op("vector", lambda e: e.tensor_copy(out=stgB[:, 128 * s4:128 * s4 + 128],
                                                       in_=pss[5][0:64, 256:384]),
                      reads=[PS[5]], writes=[B_stgB])
                if s4 == TA // 128 - 1:
                    csg = slice(tg * TA, (tg + 1) * TA)
                    for h in range(4):
                        S_.dma("sync", B_stgB, qB_d[h, 64:80, csg], stgB[16 * h:16 * h + 16, :],
                               reads=[B_stgB], writes=[B_qB[h]])

            def load_tile(t):
                i = t % 2
                cs = slice(t * TA, (t + 1) * TA)
                S_.dma("sync", B_xt[i], xt[i], xv[:, :, cs], reads=[B_xsrc], writes=[B_xt[i]])
                S_.dma("sync", B_tab[i], tab[i], tv[:, :, cs], reads=[B_tabs_t[t]], writes=[B_tab[i]])

            load_tile(0)
            cnt = {"ps": 0, "n": 0, "stg": 0}
            for t in range(min(NT_A, int(os.environ.get('KDBGT', '99')))):
                i = t % 2
                cs = slice(t * TA, (t + 1) * TA)
                if t + 1 < NT_A:
                    load_tile(t + 1)
                def stats(ti):
                    ii = ti % 2
                    for c in range(8):
                        if c not in (2, 6):
                            S_.op("scalar", lambda e, c=c: e.activation(out=sq[:, c, :], in_=xt[ii][:, c, :], func=AF.Square),
                                  reads=[B_xt[ii]], writes=[B_sqc[c]])
                        else:
                            S_.op("gpsimd", lambda e, c=c: e.tensor_tensor(out=sq[:, c, :], in0=xt[ii][:, c, :],
                                                                          in1=xt[ii][:, c, :], op=ALU.mult),
                                  reads=[B_xt[ii]], writes=[B_sqc[c]])
                    for c in range(8):
                        S_.op("tensor", lambda e, c=c: e.matmul(pss[6][:, 0:TA], ones_bf, sq[:, c, :],
                                                               start=(c == 0), stop=(c == 7)),
                              reads=[B_sqc[c], B_cb], writes=[PS[6]])
                    S_.op("scalar", lambda e: e.activation(out=lnt, in_=pss[6][:, 0:TA], func=AF.Ln, scale=1.0 / D,
                                                          bias=ccol(CF_EPS)), reads=[PS[6], B_cf], writes=[B_lnt])
                    S_.op("scalar", lambda e: e.activation(out=rstd, in_=lnt, func=AF.Exp, scale=-0.5),
                          reads=[B_lnt], writes=[B_rstd])

                if t == 0:
                    stats(0)
                for c in range(8):
                    S_.op("vector", lambda e, c=c: e.scalar_tensor_tensor(
                        out=hT[:, c, :], in0=xt[i][:, c, :], scalar=col(PC_GMIX + 8 * l + c), in1=rstd,
                        op0=ALU.mult, op1=ALU.mult), reads=[B_xt[i], B_rstd, B_pc], writes=[B_hT])
                pend = {"f": None}

                def chunk_tail(kind, ci, pq, psw, n):
                    typA = kind in ("kA", "qA")
                    perm = cb[:, CB_PERMA:CB_PERMA + 128] if typA else cb[:, CB_PERMB:CB_PERMB + 128]
                    S_.op("tensor", lambda e, psw=psw, n=n, perm=perm: e.matmul(
                        pss[psw][:, 0:TA], perm, qb16[n], start=True, stop=True),
                        reads=[B_qb16[n], B_cb], writes=[PS[psw]])
                    tc_, ts_ = (0, 1) if typA else (2, 3)
                    S_.op("vector", lambda e, pq=pq, n=n, tc_=tc_: e.tensor_tensor(
                        out=t1[n], in0=pss[pq][:, 0:TA], in1=tab[i][:, tc_, :], op=ALU.mult),
                        reads=[PS[pq], B_tab[i]], writes=[B_t1[n]])
                    S_.op("vector", lambda e, psw=psw, n=n, ts_=ts_: e.tensor_tensor(
                        out=t2[n], in0=pss[psw][:, 0:TA], in1=tab[i][:, ts_, :], op=ALU.mult),
                        reads=[PS[psw], B_tab[i]], writes=[B_t2[n]])


                    def nstg():
                        s = cnt["stg"] % 4
                        cnt["stg"] += 1
                        return s

                    if kind in ("kA", "kC", "qC"):
                        s = nstg()
                        S_.op("vector", lambda e, n=n, s=s: e.tensor_tensor(out=stg[s], in0=t1[n], in1=t2[n], op=ALU.add),
                              reads=[B_t1[n], B_t2[n]], writes=[B_stg[s]])
                        dd, Bd = {"kA": (kA_d, B_kA), "kC": (kC_d, B_kC), "qC": (qC_d, B_qC)}[kind]
                        for hh in range(2):
                            h = 2 * ci + hh
                            S_.dma("sync", B_stg[s], dd[h, :, cs], stg[s][64 * hh:64 * hh + 64, :],
                                   reads=[B_stg[s]], writes=[Bd[h]])
                    elif kind == "qA":
                        ri = 2
                        S_.op("gpsimd", lambda e, n=n: e.tensor_tensor(out=r32[ri], in0=t1[n], in1=t2[n], op=ALU.add),
                              reads=[B_t1[n], B_t2[n]], writes=[B_r32[ri]])
                        for mp in range(2):
                            s = nstg()
                            S_.op("scalar", lambda e, s=s, mp=mp: e.activation(
                                out=stg[s], in_=r32[ri], func=AF.Copy, scale=ccol(CF_M0 + mp)),
                                reads=[B_r32[ri], B_cf], writes=[B_stg[s]])
                            for hh in range(2):
                                h = 2 * ci + hh
                                S_.dma("sync", B_stg[s], qA_d[2 * h + mp, :, cs], stg[s][64 * hh:64 * hh + 64, :],
                                       reads=[B_stg[s]], writes=[B_qA[2 * h + mp]])
                    else:
                        ri = ci if kind == "kB" else 2 + ci
                        S_.op("gpsimd", lambda e, n=n, ri=ri: e.tensor_tensor(out=r32[ri], in0=t1[n], in1=t2[n], op=ALU.add),
                              reads=[B_t1[n], B_t2[n]], writes=[B_r32[ri]])
                        if kind == "kB":
                            s = nstg()
                            S_.op("gpsimd", lambda e, s=s, ri=ri: e.tensor_copy(out=stg[s], in_=r32[ri]),
                                  reads=[B_r32[ri]], writes=[B_stg[s]])
                            for hh in range(2):
                                h = 2 * ci + hh
                                S_.dma("sync", B_stg[s], kB_d[h, :, cs], stg[s][64 * hh:64 * hh + 64, :],
                                       reads=[B_stg[s]], writes=[B_kB[h]])
                        else:
                            S_.op("gpsimd", lambda e, ri=ri: e.tensor_copy(out=qh[ci], in_=r32[ri]),
                                  reads=[B_r32[ri]], writes=[B_qh[ci]])
                            S_.op("gpsimd", lambda e, ri=ri: e.tensor_tensor(out=ql[ci], in0=r32[ri], in1=qh[ci],
                                                                            op=ALU.subtract),
                                  reads=[B_r32[ri], B_qh[ci]], writes=[B_ql[ci]])
                            for hh in range(2):
                                h = 2 * ci + hh
                                S_.dma("sync", B_qh[ci], qB_d[h, 0:64, cs], qh[ci][64 * hh:64 * hh + 64, :],
                                       reads=[B_qh[ci]], writes=[B_qB[h]])
                        if kind == "kB":
                            nb = TA // 256
                            S_.op("vector", lambda e, ri=ri, ci=ci: e.tensor_reduce(
                                out=kmean[:, ci, t * nb:(t + 1) * nb],
                                in_=r32[ri].rearrange("p (b k) -> p b k", b=nb),
                                axis=mybir.AxisListType.X, op=ALU.add), reads=[B_r32[ri]], writes=[B_kmean])

                ALVL = int(os.environ.get("KDBGA", "9"))
                for chi, (kind, ci, wc) in enumerate(CH):
                    pq = cnt["ps"] % 4
                    cnt["ps"] += 1
                    psw = 4 + (cnt["ps"] % 2)
                    for k in range(8):
                        S_.op("tensor", lambda e, k=k, pq=pq, wc=wc: e.matmul(
                            pss[pq][:, 0:TA], Win(k, wc, 128), hT[:, k, :], start=(k == 0), stop=(k == 7)),
                            reads=[B_hT, P_win[wc // 512]], writes=[PS[pq]])
                    SLVL = int(os.environ.get("KDBGS", "9"))
                    if SLVL < 2:
                        continue
                    n = cnt["n"] % 2
                    cnt["n"] += 1
                    S_.op("scalar", lambda e, pq=pq, n=n: e.activation(out=qb16[n], in_=pss[pq][:, 0:TA], func=AF.Copy),
                          reads=[PS[pq]], writes=[B_qb16[n]])
                    if pend["f"] is not None:
                        pend["f"]()
                    pend["f"] = (lambda kind=kind, ci=ci, pq=pq, psw=psw, n=n: chunk_tail(kind, ci, pq, psw, n))
                    if LAZY["on"] and chi == 6 and t + 2 < NT_A:
                        tables_tile(t + 2)
                    if t >= 1 and chi in (2, 5, 8, 10, 12, 14):
                        gi_ = (2, 5, 8, 10, 12, 14).index(chi)
                        if gi_ >= 2:
                            gate_sub2(t - 1, gi_ - 2)
                        if gi_ <= 3:
                            gate_sub(t - 1, gi_)
                    if chi == 13 and t + 1 < NT_A:
                        stats(t + 1)
                if pend["f"] is not None:
                    pend["f"]()
                    pend["f"] = None

                for s4 in range(TA // 128):
                    vi = (t * 4 + s4) % 2
                    for g in range(2):
                        pq = cnt["ps"] % 4
                        cnt["ps"] += 1
                        for k in range(8):
                            S_.op("tensor", lambda e, k=k, pq=pq, g=g, s4=s4: e.matmul(
                                pss[pq][:, 0:512], hT[:, k, 128 * s4:128 * s4 + 128], Win(k, VCOL + 512 * g, 512),
                                start=(k == 0), stop=(k == 7)), reads=[B_hT, P_win[4 + g]], writes=[PS[pq]])
                        S_.op("scalar", lambda e, pq=pq, g=g, vi=vi: e.activation(
                            out=vstg[vi][:, 8 * g:8 * g + 8, 0:64],
                            in_=pss[pq][:, 0:512].rearrange("p (h d) -> p h d", h=8), func=AF.Copy),
                            reads=[PS[pq]], writes=[B_vstg[vi]])
                    r0 = t * TA + s4 * 128
                    S_.dma("sync", B_vstg[vi], vaug_d[r0:r0 + 128, :, :], vstg[vi], reads=[B_vstg[vi]], writes=[B_vaug])
            for s4_ in range(TA // 128):
                gate_sub(NT_A - 1, s4_)
                gate_sub2(NT_A - 1, s4_)
            return (B_xt + B_tab + B_sqc + [B_hT, B_sq, B_rstd, B_lnt] + B_t1 + B_t2 + B_r32 + B_qb16 + B_stg + [B_gm, B_stgB])

        def phase_B(l):
            QS = [Rv(i * 4096, 4096) for i in range(2)]
            KS = [Rv(8192 + i * 4096, 4096) for i in range(2)]
            VS = [Rv(16384 + i * 4096, 4096).rearrange("p (b c) -> p b c", c=128) for i in range(2)]
            accs = [Rv(24576, 8192, F32), Rv(66560, 8192, F32)]
            pT = [Rv(32768 + i * 512, 512) for i in range(4)]
            pT2 = [Rv(32768 + 2048 + i * 1024, 1024) for i in range(3)]
            B_pT2 = [S_.buf("pT2_%d" % i) for i in range(3)]
            o = 32768 + 2048 + 3072
            rec = [Rv(o + i * 1024, 1024, F32, parts=slice(0, 64)) for i in range(2)]
            o += 2048
            o0 = [Rv(o + i * 1024, 1024, F32, parts=slice(0, 64)) for i in range(2)]
            o += 2048
            od = Rv(o, 1024, F32, parts=slice(0, 64))
            o += 1024
            sqd = Rv(o, 512, parts=slice(0, 64))
            o += 512
            lnd = Rv(o, 1024, F32, parts=slice(0, 64))
            o += 1024
            rsd = Rv(o, 1024, F32, parts=slice(0, 64))
            o += 1024
            mstg = [Rv(o + i * 512, 512, parts=slice(0, 64)) for i in range(3)]
            o += 1536
            lnd2 = Rv(o, 1024, F32, parts=slice(64, 128))
            B_lnd2 = S_.buf("lnd2")
            o += 1024
            assert o <= 50176
            KZ = [[Rv(50176 + (2 * i + m_) * 4096, 4096) for m_ in range(2)] for i in range(2)]
            B_KZ = [S_.buf("KZ%d" % i, dma=True) for i in range(2)]
            for i_ in range(2):
                S_.op("gpsimd", lambda e: e.memset(KZ[i_][0][64:128, :], 0.0), writes=[B_KZ[i_]])
                S_.op("gpsimd", lambda e: e.memset(KZ[i_][1][0:64, :], 0.0), writes=[B_KZ[i_]])
            B_Q = [S_.buf("Q%d" % i, dma=True) for i in range(2)]
            B_K = [S_.buf("K%d" % i, dma=True) for i in range(2)]
            B_V = [S_.buf("V%d" % i, dma=True) for i in range(2)]
            B_accs = [S_.buf("acc0"), S_.buf("acc1")]
            B_pT = [S_.buf("pT%d" % i) for i in range(4)]
            B_rec = [S_.buf("rec%d" % i) for i in range(2)]
            B_o0 = [S_.buf("o0%d" % i) for i in range(2)]
            B_od = S_.buf("od")
            B_sqd = S_.buf("sqd")
            B_lnd = S_.buf("lnd")
            B_rsd = S_.buf("rsd")
            B_mstg = [S_.buf("mstg%d" % i, dma=True) for i in range(3)]
            st = {"s": 0, "p": 0, "o": 0, "m": 0, "qk": 0, "v": 0}

            def next_s():
                st["s"] = (st["s"] + 1) % 4
                return st["s"]

            def next_p():
                st["p"] = (st["p"] + 1) % 4
                return st["p"]

            def next_m():
                st["m"] = (st["m"] + 1) % 3
                return st["m"]

            deferred = []

            def run_pipeline(steps, lag=2):
                nst = len(steps)
                for i in range(nst + lag):
                    if i < nst:
                        steps[i][0]()
                        steps[i][1]()
                    j = i - lag
                    if 0 <= j < nst:
                        steps[j][2]()
                    for d_ in list(deferred):
                        d_[0] -= 1
                        if d_[0] <= 0:
                            deferred.remove(d_)
                            d_[1]()
                for d_ in list(deferred):
                    deferred.remove(d_)
                    d_[1]()

            def store_mix(m, row0, cs):
                S_.dma("sync", B_mstg[m], mixT_d[row0:row0 + 64, cs], mstg[m], reads=[B_mstg[m]], writes=[B_mixT])

            def causal_head(kap, qap, kq_bufs, vslot, nmaps, scale, fin):
                steps = []
                for Qt in range(S // 512):
                    cs = slice(Qt * 512, (Qt + 1) * 512)
                    ob = 4 + 2 * (st["o"] % 2)
                    st["o"] += 1
                    last_kt = 4 * Qt + 3
                    for kp in range((last_kt + 1) // 2):
                        for mp in range(nmaps):
                            ctx = {}

                            def qk_fn(kp=kp, mp=mp, ctx=ctx, Qt=Qt):
                                st["r"] = (st.get("r", 0) + 1) % 2
                                r = st["r"]
                                ctx["r"] = r
                                pb = 64 * mp
                                for u in range(2):
                                    kt = 2 * kp + u
                                    j = kt - 4 * Qt
                                    c0 = 128 * max(j, 0)
                                    b0 = 1024 * r + 512 * u
                                    S_.op("tensor", lambda e: e.matmul(
                                        psbig[:, b0 + c0:b0 + 512], kap(mp)[:, 128 * kt:128 * kt + 128],
                                        qap[:, Qt * 512 + c0:(Qt + 1) * 512], start=True, stop=(j < 0)),
                                        reads=kq_bufs, writes=[PS[2 * r + u]])
                                    if j >= 0:
                                        S_.op("tensor", lambda e: e.matmul(
                                            psbig[:, b0 + c0:b0 + c0 + 128], ident, cb[:, CB_TRI:CB_TRI + 128],
                                            start=False, stop=True), reads=[B_cb], writes=[PS[2 * r + u]])

                            def exp_fn(kp=kp, ctx=ctx, Qt=Qt):
                                r = ctx["r"]
                                st["p2"] = (st.get("p2", 0) + 1) % 3
                                p = st["p2"]
                                ctx["p"] = p
                                if 2 * kp + 1 < 4 * Qt:
                                    S_.op("scalar", lambda e: e.activation(
                                        out=pT2[p], in_=psbig[:, 1024 * r:1024 * r + 1024], func=AF.Exp, scale=scale),
                                        reads=[PS[2 * r], PS[2 * r + 1]], writes=[B_pT2[p]])
                                else:
                                    for u in range(2):
                                        c0 = 128 * max(2 * kp + u - 4 * Qt, 0)
                                        S_.op("scalar", lambda e: e.activation(
                                            out=pT2[p][:, 512 * u + c0:512 * u + 512],
                                            in_=psbig[:, 1024 * r + 512 * u + c0:1024 * r + 512 * u + 512],
                                            func=AF.Exp, scale=scale), reads=[PS[2 * r + u]], writes=[B_pT2[p]])

                            def pv_fn(kp=kp, mp=mp, ctx=ctx, ob=ob, last_kt=last_kt, Qt=Qt, cs=cs):
                                p = ctx["p"]
                                for u in range(2):
                                    kt = 2 * kp + u
                                    c0 = 128 * max(kt - 4 * Qt, 0)
                                    S_.op("tensor", lambda e: e.matmul(
                                        pss[ob + mp][:, c0:512], VS[vslot][:, kt, :], pT2[p][:, 512 * u + c0:512 * u + 512],
                                        start=(kt == 0), stop=(kt == last_kt)),
                                        reads=[B_V[vslot], B_pT2[p]], writes=[PS[ob + mp]])
                                if 2 * kp + 1 == last_kt and mp == nmaps - 1:
                                    fin(ob, cs)

                            steps.append((qk_fn, exp_fn, pv_fn))
                run_pipeline(steps, lag=1)

            def fin_diff(h):
                def fin(ob, cs):
                    for mp in range(2):
                        S_.op("vector", lambda e, mp=mp: e.reciprocal(out=rec[mp], in_=pss[ob + mp][64:128, :]),
                              reads=[PS[ob + mp]], writes=[B_rec[mp]])
                        S_.op("vector", lambda e, mp=mp: e.tensor_tensor(out=o0[mp], in0=pss[ob + mp][0:64, :], in1=rec[mp],
                                                                        op=ALU.mult),
                              reads=[PS[ob + mp], B_rec[mp]], writes=[B_o0[mp]])
                    S_.op("vector", lambda e: e.scalar_tensor_tensor(
                        out=od, in0=o0[1], scalar=small[0:64, 4 * l + 1:4 * l + 2], in1=o0[0], op0=ALU.mult, op1=ALU.add),
                        reads=[B_o0[0], B_o0[1], B_small], writes=[B_od])
                    S_.op("gpsimd", lambda e: e.tensor_tensor(out=sqd, in0=od, in1=od, op=ALU.mult),
                          reads=[B_od], writes=[B_sqd])
                    deferred.append([6, lambda: fin2(cs)])

                def fin2(cs):
                    s = next_s()
                    S_.op("tensor", lambda e: e.matmul(pss[s][0:64, :], cb[0:64, CB_ONES:CB_ONES + 64], sqd,
                                                      start=True, stop=True), reads=[B_sqd, B_cb], writes=[PS[s]])
                    S_.op("scalar", lambda e: e.activation(out=lnd, in_=pss[s][0:64, :], func=AF.Ln, scale=1.0 / 64,
                                                          bias=ccol(CF_EPS, slice(0, 64))),
                          reads=[PS[s], B_cf], writes=[B_lnd])
                    S_.op("scalar", lambda e: e.activation(out=rsd, in_=lnd, func=AF.Exp, scale=-0.5),
                          reads=[B_lnd], writes=[B_rsd])
                    m = next_m()
                    S_.op("vector", lambda e: e.scalar_tensor_tensor(
                        out=mstg[m], in0=od, scalar=small[0:64, 4 * l + 2:4 * l + 3], in1=rsd, op0=ALU.mult, op1=ALU.mult),
                        reads=[B_od, B_rsd, B_small], writes=[B_mstg[m]])
                    store_mix(m, 64 * h, cs)
                return fin

            def fin_plain(row0):
                def fin(ob, cs):
                    S_.op("vector", lambda e: e.reciprocal(out=rec[0], in_=pss[ob][64:128, :]),
                          reads=[PS[ob]], writes=[B_rec[0]])
                    m = next_m()
                    S_.op("vector", lambda e: e.tensor_tensor(out=mstg[m], in0=pss[ob][0:64, :], in1=rec[0], op=ALU.mult),
                          reads=[PS[ob], B_rec[0]], writes=[B_mstg[m]])
                    store_mix(m, row0, cs)
                return fin

            def load_v(vslot, h, dil):
                src = vaug_d[:, h, :]
                if dil == 1:
                    sv = src.rearrange("(n i) c -> i n c", i=128)
                    S_.dma("sync", B_V[vslot], VS[vslot][:, 0:NKT, :], sv, reads=[B_vaug], writes=[B_V[vslot]])
                else:
                    nb = S // (128 * dil)
                    sv = src.rearrange("(n i r) c -> i r n c", i=128, r=dil)
                    dv = VS[vslot][:, 0:NKT, :].rearrange("p (r n) c -> p r n c", r=dil)
                    first = True
                    for r in range(dil):
                        S_.dma("sync", B_V[vslot], dv[:, r, :, :], sv[:, r, :, :], reads=[B_vaug],
                               writes=[B_V[vslot]] if first else [])
                        first = False
                    B_V[vslot].writers = [(B_V[vslot].sem, S_.cnt[B_V[vslot].sem])]

            def nqk():
                st["qk"] += 1
                return st["qk"] % 2

            def nv():
                st["v"] += 1
                return st["v"] % 2

            def load_diff(h):
                qk = nqk()
                vs = nv()
                first = True
                for mp in range(2):
                    S_.dma("sync", B_Q[qk], QS[qk][64 * mp:64 * mp + 64, 0:S], qA_d[2 * h + mp], reads=[B_qA[2 * h + mp]],
                           writes=[B_Q[qk]] if first else [])
                    S_.dma("sync", B_KZ[qk], KZ[qk][mp][64 * mp:64 * mp + 64, 0:S], kA_d[h], reads=[B_kA[h]],
                           writes=[B_KZ[qk]] if first else [])
                    first = False
                B_Q[qk].writers = [(B_Q[qk].sem, S_.cnt[B_Q[qk].sem])]
                B_KZ[qk].writers = [(B_KZ[qk].sem, S_.cnt[B_KZ[qk].sem])]
                load_v(vs, h, 1)
                return (qk, vs)

            def comp_diff(h, ctx_):
                qk, vs = ctx_
                causal_head(lambda mp, qk=qk: KZ[qk][mp][0:128, :], QS[qk][0:128, :], [B_KZ[qk], B_Q[qk]], vs, 2,
                            32 ** -0.5, fin_diff(h))

            def load_moba(h):
                qk = nqk()
                vs = nv()
                S_.dma("sync", B_Q[qk], QS[qk][0:80, 0:S], qB_d[h], reads=[B_qB[h]], writes=[B_Q[qk]])
                S_.dma("sync", B_K[qk], KS[qk][0:64, 0:S], kB_d[h], reads=[B_kB[h]], writes=[B_K[qk]])
                S_.dma("sync", B_K[qk], KS[qk][64:80, 0:S], kind_in, reads=[], writes=[])
                B_K[qk].writers = [(B_K[qk].sem, S_.cnt[B_K[qk].sem])]
                load_v(vs, 4 + h, 1)
                return (qk, vs)

            def comp_moba(h, ctx_):
                qk, vs = ctx_
                causal_head(lambda mp, qk=qk: KS[qk][0:80, :], QS[qk][0:80, :], [B_K[qk], B_Q[qk]], vs, 1,
                            0.125, fin_plain(256 + 64 * h))

            def load_pair(pr):
                qk = nqk()
                first = True
                for hh in range(2):
                    S_.dma("sync", B_Q[qk], QS[qk][64 * hh:64 * hh + 64, 0:S], qC_d[2 * pr + hh], reads=[B_qC[2 * pr + hh]],
                           writes=[B_Q[qk]] if first else [])
                    S_.dma("sync", B_KZ[qk], KZ[qk][hh][64 * hh:64 * hh + 64, 0:S], kC_d[2 * pr + hh], reads=[B_kC[2 * pr + hh]],
                           writes=[B_KZ[qk]] if first else [])
                    first = False
                B_Q[qk].writers = [(B_Q[qk].sem, S_.cnt[B_Q[qk].sem])]
                B_KZ[qk].writers = [(B_KZ[qk].sem, S_.cnt[B_KZ[qk].sem])]
                return qk

            def comp_pair(pr, qk):
                for hh in range(2):
                    head = 2 * pr + hh
                    pb = 64 * hh
                    acc = accs[head % 2]
                    B_acc = B_accs[head % 2]
                    for di, dil in enumerate((1, 4, 16)):
                        vs = nv()
                        load_v(vs, 8 + head, dil)
                        nb = S // (128 * dil)
                        groups = []
                        ngr = (S // 128) // 4
                        for g in range(ngr):
                            if dil == 1:
                                blocks = [(0, 4 * g + b) for b in range(4)]
                                astart, ost, ist = 512 * g, 128, 1
                            elif dil == 4:
                                r, n0 = g // (nb // 4), 4 * (g % (nb // 4))
                                blocks = [(r, n0 + b) for b in range(4)]
                                astart, ost, ist = 512 * n0 + r, 512, 4
                            else:
                                n, r0 = g // (dil // 4), 4 * (g % (dil // 4))
                                blocks = [(r0 + b, n) for b in range(4)]
                                astart, ost, ist = 2048 * n + r0, 1, 16
                            groups.append((blocks, astart, ost, ist))
                        steps = []
                        for (blocks, astart, ost, ist) in groups:
                            ctx = {}

                            def tok(r, n, dil=dil):
                                t0 = 128 * n * dil + r
                                return slice(t0, t0 + 127 * dil + 1, dil) if dil > 1 else slice(t0, t0 + 128)

                            def qk_fn(blocks=blocks, ctx=ctx, tok=tok):
                                so = next_s()
                                sp = next_s()
                                ctx["so"], ctx["sp"] = so, sp
                                anyprev = any(n > 0 for (_, n) in blocks)
                                S_.op("tensor", lambda e: e.matmul(pss[so][:, :], ident, cb[:, CB_OWN4:CB_OWN4 + 512],
                                                                  start=True, stop=False), reads=[B_cb], writes=[PS[so]])
                                for b, (r, n) in enumerate(blocks):
                                    S_.op("tensor", lambda e, b=b, r=r, n=n: e.matmul(
                                        pss[so][:, 128 * b:128 * b + 128], KZ[qk][hh][0:128, tok(r, n)],
                                        QS[qk][0:128, tok(r, n)], start=False, stop=True),
                                        reads=[B_KZ[qk], B_Q[qk]], writes=[PS[so]])
                                ctx["anyprev"] = anyprev
                                if anyprev:
                                    S_.op("tensor", lambda e: e.matmul(pss[sp][:, :], ident, cb[:, CB_PREV4:CB_PREV4 + 512],
                                                                      start=True, stop=False), reads=[B_cb], writes=[PS[sp]])
                                    for b, (r, n) in enumerate(blocks):
                                        if n == 0:
                                            continue
                                        S_.op("tensor", lambda e, b=b, r=r, n=n: e.matmul(
                                            pss[sp][:, 128 * b:128 * b + 128], KZ[qk][hh][0:128, tok(r, n - 1)],
                                            QS[qk][0:128, tok(r, n)], start=False, stop=True),
                                            reads=[B_KZ[qk], B_Q[qk]], writes=[PS[sp]])

                            def exp_fn(ctx=ctx):
                                po = next_p()
                                ctx["po"] = po
                                S_.op("scalar", lambda e: e.activation(out=pT[po], in_=pss[ctx["so"]][:, :], func=AF.Exp,
                                                                      scale=0.125), reads=[PS[ctx["so"]]], writes=[B_pT[po]])
                                if ctx["anyprev"]:
                                    pp = next_p()
                                    ctx["pp"] = pp
                                    S_.op("scalar", lambda e: e.activation(out=pT[pp], in_=pss[ctx["sp"]][:, :], func=AF.Exp,
                                                                          scale=0.125),
                                          reads=[PS[ctx["sp"]]], writes=[B_pT[pp]])

                            def pv_fn(blocks=blocks, ctx=ctx, astart=astart, ost=ost, ist=ist, di=di, vs=vs, dil=dil, nb=nb):
                                ob = 4 + (st["o"] % 4)
                                st["o"] += 1
                                firstmm = True
                                for b, (r, n) in enumerate(blocks):
                                    S_.op("tensor", lambda e, b=b, r=r, n=n, firstmm=firstmm: e.matmul(
                                        pss[ob][:, 128 * b:128 * b + 128], VS[vs][:, r * nb + n, :],
                                        pT[ctx["po"]][:, 128 * b:128 * b + 128], start=firstmm, stop=(n == 0)),
                                        reads=[B_V[vs], B_pT[ctx["po"]]], writes=[PS[ob]])
                                    firstmm = False
                                    if n > 0:
                                        S_.op("tensor", lambda e, b=b, r=r, n=n: e.matmul(
                                            pss[ob][:, 128 * b:128 * b + 128], VS[vs][:, r * nb + n - 1, :],
                                            pT[ctx["pp"]][:, 128 * b:128 * b + 128], start=False, stop=True),
                                            reads=[B_V[vs], B_pT[ctx["pp"]]], writes=[PS[ob]])
                                if dil == 1:
                                    av = acc[:, astart:astart + 512].rearrange("p (b i) -> p b i", b=4)
                                elif dil == 4:
                                    n0_ = blocks[0][1]
                                    av = acc[:, 512 * n0_:512 * n0_ + 2048].rearrange("p (b i r) -> p b i r", b=4, r=4)[:, :, :, blocks[0][0]]
                                else:
                                    n_ = blocks[0][1]
                                    r0_ = blocks[0][0]
                                    av = acc[:, 2048 * n_:2048 * n_ + 2048].rearrange("p (i r) -> p r i", r=16)[:, r0_:r0_ + 4, :]
                                pv = pss[ob][:, :].rearrange("p (b i) -> p b i", b=4)
                                if di == 0:
                                    S_.op("vector", lambda e: e.tensor_copy(out=av, in_=pv), reads=[PS[ob]], writes=[B_acc])
                                else:
                                    S_.op("vector", lambda e: e.tensor_tensor(out=av, in0=pv, in1=av, op=ALU.add),
                                          reads=[PS[ob], B_acc], writes=[B_acc])

                            steps.append((qk_fn, exp_fn, pv_fn))
                        run_pipeline(steps, lag=1)
                    def fin_tile(Qt, acc=acc, B_acc=B_acc, head=head):
                        cs = slice(Qt * 512, (Qt + 1) * 512)
                        S_.op("scalar", lambda e: e.activation(out=lnd2, in_=acc[64:128, cs], func=AF.Ln),
                              reads=[B_acc], writes=[B_lnd2])
                        S_.op("scalar", lambda e: e.activation(out=rec[0], in_=lnd2, func=AF.Exp, scale=-1.0),
                              reads=[B_lnd2], writes=[B_rec[0]])
                        m = next_m()
                        S_.op("vector", lambda e: e.tensor_tensor(out=mstg[m], in0=acc[0:64, cs], in1=rec[0],
                                                                 op=ALU.mult),
                              reads=[B_acc, B_rec[0]], writes=[B_mstg[m]])
                        store_mix(m, 512 + 64 * head, cs)

                    for Qt in range(S // 512):
                        deferred.append([Qt + 2, lambda Qt=Qt, f_=fin_tile: f_(Qt)])
            jobs = ([(load_diff, comp_diff, h) for h in range(4)] + [(load_moba, comp_moba, h) for h in range(4)] +
                    [(load_pair, comp_pair, p_) for p_ in range(4)])
            ctx_ = jobs[0][0](jobs[0][2])
            for ji, (lf, cf_, arg) in enumerate(jobs):
                nxt_ = jobs[ji + 1][0](jobs[ji + 1][2]) if ji + 1 < len(jobs) else None
                cf_(arg, ctx_)
                ctx_ = nxt_
            for d_ in list(deferred):
                deferred.remove(d_)
                d_[1]()
            return (B_Q + B_K + B_KZ + B_V + B_accs + B_pT + B_pT2 + B_rec + B_o0 + [B_od, B_sqd, B_lnd, B_rsd] + B_mstg)

        def phase_C(l, xsrc, B_xsrc, is_last):
            xt = [Tv(i * 4096, 4096, F32).rearrange("p (c t) -> p c t", c=8) for i in range(2)]
            o = 8192
            mt = Tv(o, 2048).rearrange("p (c t) -> p c t", c=8)
            o += 2048
            h2e = Tv(o, 8 * (TC + 2)).rearrange("p (c t) -> p c t", c=8)
            h2 = h2e[:, :, 2:TC + 2]
            o += 8 * (TC + 2)
            cv = [Tv(o + i * 512, 512, F32) for i in range(4)]
            o += 2048
            rstd = Tv(o, 512, F32)
            o += 512
            lnt = Tv(o, 512, F32)
            o += 512
            cvb = [Tv(o + i * 512, 512, F32) for i in range(2)]
            B_cvb = [S_.buf("cvb%d" % i) for i in range(2)]
            o += 1024
            assert o <= 16448
            G0 = WDN_OFF + 22 * D
            gact = Rv(G0, 22 * TC).rearrange("p (j t) -> p j t", j=22)
            sq = Rv(G0 + 22 * TC, 8 * TC).rearrange("p (c t) -> p c t", c=8)
            htmp = Rv(G0 + 30 * TC, 4, F32)
            assert G0 + 30 * TC + 8 <= 75392
            B_xt = [S_.buf("cxt%d" % i, dma=True) for i in range(2)]
            B_mt = S_.buf("mt", dma=True)
            B_h2 = S_.buf("h2")
            B_cv = [S_.buf("cv%d" % i) for i in range(4)]
            B_rstd = S_.buf("crstd")
            B_lnt = S_.buf("clnt")
            B_g = S_.buf("gact")
            B_sq = S_.buf("csq")
            B_sqc = [S_.buf("csq%d" % c) for c in range(8)]
            B_ht = S_.buf("htmp")
            xv = xsrc.rearrange("(c p) t -> p c t", p=128)
            mv = mixT_d.rearrange("(c p) t -> p c t", p=128)
            ov = (outT if is_last else xres_d).rearrange("(c p) t -> p c t", p=128)
            cw = lambda j, ci: col(PC_CONVW + l * 132 + j * 44 + ci)
            cbias = lambda ci: col(PC_CONVB + l * 44 + ci)
            pcnt = {"a": 0, "u": 0, "cv": 0}

            def norm_stats(xi, gbase, out_fn, bank=2):
                for c in range(8):
                    if True:
                        S_.op("scalar", lambda e, c=c: e.activation(out=sq[:, c, :], in_=xt[xi][:, c, :], func=AF.Square),
                              reads=[B_xt[xi]], writes=[B_sqc[c]])
                    else:
                        S_.op("gpsimd", lambda e, c=c: e.tensor_tensor(out=sq[:, c, :], in0=xt[xi][:, c, :],
                                                                      in1=xt[xi][:, c, :], op=ALU.mult),
                              reads=[B_xt[xi]], writes=[B_sqc[c]])
                for c in range(8):
                    S_.op("tensor", lambda e, c=c: e.matmul(pss[bank][:, 0:TC], ones_bf, sq[:, c, :], start=(c == 0), stop=(c == 7)),
                          reads=[B_sqc[c], B_cb], writes=[PS[bank]])
                S_.op("scalar", lambda e: e.activation(out=lnt, in_=pss[bank][:, 0:TC], func=AF.Ln, scale=1.0 / D,
                                                      bias=ccol(CF_EPS)), reads=[PS[bank], B_cf], writes=[B_lnt])
                S_.op("scalar", lambda e: e.activation(out=rstd, in_=lnt, func=AF.Exp, scale=-0.5),
                      reads=[B_lnt], writes=[B_rstd])
                for c in range(8):
                    out_fn(c, col(gbase + c))

            def load_tile(t):
                i = t % 2
                cs = slice(t * TC, (t + 1) * TC)
                S_.dma("sync", B_xt[i], xt[i], xv[:, :, cs], reads=[B_xsrc], writes=[B_xt[i]])

            DBANK = [0, 1, 2, 7]
            B_gj = [S_.buf("gj%d" % j) for j in range(22)]

            def dacc(m):
                return pss[DBANK[m // 2]][:, TC * (m % 2):TC * (m % 2) + TC]

            def down_mm(j):
                for m in range(8):
                    S_.op("tensor", lambda e, m=m: e.matmul(
                        dacc(m), Wdn(j, 128 * m, 128), gact[:, j, :], start=(j == 0 and m % 2 == 0), stop=(j == 21),
                        skip_group_check=True), reads=[P_wdn[j // 4], B_gj[j]], writes=[PS[DBANK[m // 2]]])

            def load_mt(t):
                cs_ = slice(t * TC, (t + 1) * TC)
                S_.dma("sync", B_mt, mt, mv[:, :, cs_], reads=[B_mixT], writes=[B_mt])

            def prologue(t, obanks, sbank):
                xi = t % 2
                if t == 0:
                    S_.op("gpsimd", lambda e: e.memset(h2e[:, :, 0:2], 0.0), writes=[B_h2])
                else:
                    S_.op("scalar", lambda e: e.activation(out=h2e[:, :, 0:2], in_=h2e[:, :, TC:TC + 2], func=AF.Copy),
                          reads=[B_h2], writes=[B_h2])
                for m in range(8):
                    pa = obanks[m % 2]
                    for k in range(8):
                        S_.op("tensor", lambda e, k=k, m=m, pa=pa: e.matmul(
                            pss[pa][:, 0:TC], WO[:, k, 128 * m:128 * m + 128], mt[:, k, :], start=(k == 0), stop=(k == 7)),
                            reads=[B_wo, B_mt], writes=[PS[pa]])
                    S_.op("vector", lambda e, m=m, pa=pa: e.tensor_tensor(out=xt[xi][:, m, :], in0=pss[pa][:, 0:TC],
                                                                         in1=xt[xi][:, m, :], op=ALU.add),
                          reads=[PS[pa], B_xt[xi]], writes=[B_xt[xi]])
                if t + 1 < NT_C:
                    load_mt(t + 1)
                norm_stats(xi, PC_GFFN + 8 * l, lambda c, g: S_.op(
                    "vector", lambda e, c=c, g=g: e.scalar_tensor_tensor(out=h2[:, c, :], in0=xt[xi][:, c, :], scalar=g,
                                                                         in1=rstd, op0=ALU.mult, op1=ALU.mult),
                    reads=[B_xt[xi], B_rstd, B_pc], writes=[B_h2]), bank=sbank)

            load_tile(0)
            load_mt(0)
            prologue(0, (0, 1), 2)
            for t in range(NT_C):
                xi = t % 2
                cs = slice(t * TC, (t + 1) * TC)
                if t + 1 < NT_C:
                    load_tile(t + 1)
                for j in range(22):
                    pu = 3 + 2 * (pcnt["u"] % 2)
                    pcnt["u"] += 1
                    cvi = []
                    for half, ci in enumerate((j, 22 + j)):
                        pb = pu + half
                        for k in range(8):
                            S_.op("tensor", lambda e, k=k, ci=ci, pb=pb: e.matmul(
                                pss[pb][:, 0:TC + 2], Wup(k, 128 * ci, 128), h2e[:, k, :], start=(k == 0), stop=(k == 7)),
                                reads=[(P_wupg if ci < 22 else P_wupv)[(ci % 22) // 4], B_h2], writes=[PS[pb]])
                        c_ = pcnt["cv"] % 4
                        pcnt["cv"] += 1
                        cvi.append(c_)
                        S_.op("scalar", lambda e, ci=ci, pb=pb, c_=c_: e.activation(
                            out=cv[c_], in_=pss[pb][:, 2:TC + 2], func=AF.Identity, scale=cw(2, ci), bias=cbias(ci)),
                            reads=[PS[pb], B_pc], writes=[B_cv[c_]])
                        S_.op("vector", lambda e, ci=ci, pb=pb, c_=c_: e.scalar_tensor_tensor(
                            out=cv[c_], in0=pss[pb][:, 1:TC + 1], scalar=cw(1, ci), in1=cv[c_],
                            op0=ALU.mult, op1=ALU.add), reads=[PS[pb], B_cv[c_], B_pc], writes=[B_cv[c_]])
                        S_.op("vector", lambda e, ci=ci, pb=pb, c_=c_: e.scalar_tensor_tensor(
                            out=cv[c_], in0=pss[pb][:, 0:TC], scalar=cw(0, ci), in1=cv[c_],
                            op0=ALU.mult, op1=ALU.add), reads=[PS[pb], B_cv[c_], B_pc], writes=[B_cv[c_]])
                    if j == 21 and t + 1 < NT_C:
                        prologue(t + 1, (8 - pu, 9 - pu), 8 - pu)
                    cg, cvv = cvi
                    S_.op("scalar", lambda e, cg=cg: e.activation(out=cv[cg], in_=cv[cg], func=AF.Silu),
                          reads=[B_cv[cg]], writes=[B_cv[cg]])
                    S_.op("gpsimd", lambda e, cg=cg, cvv=cvv, j=j: e.tensor_tensor(out=gact[:, j, :], in0=cv[cg], in1=cv[cvv],
                                                                                  op=ALU.mult),
                          reads=[B_cv[cg], B_cv[cvv]], writes=[B_gj[j]])
                    if j >= 1:
                        down_mm(j - 1)
                down_mm(21)
                for m in range(8):
                    S_.op("vector", lambda e, m=m: e.tensor_tensor(out=xt[xi][:, m, :], in0=dacc(m),
                                                                  in1=xt[xi][:, m, :], op=ALU.add),
                          reads=[PS[DBANK[m // 2]], B_xt[xi]], writes=[B_xt[xi]])
                if is_last and final:
                    norm_stats(xi, PC_GFIN, lambda c, g: S_.op(
                        "vector", lambda e, c=c, g=g: e.scalar_tensor_tensor(out=xt[xi][:, c, :], in0=xt[xi][:, c, :], scalar=g,
                                                                             in1=rstd, op0=ALU.mult, op1=ALU.mult),
                        reads=[B_xt[xi], B_rstd, B_pc], writes=[B_xt[xi]]))
                Bo = B_out if is_last else B_xres
                S_.dma("sync", B_xt[xi], ov[:, :, cs], xt[xi], reads=[B_xt[xi]], writes=[Bo])
            return B_xt + [B_mt, B_h2] + B_cv + B_cvb + [B_rstd, B_lnt, B_g, B_ht] + B_gj + B_sqc

        B_out = S_.buf("out")
        B_xin = S_.buf("xin")
        tb_bufs = tables_init()
        tables_tile(0)
        tables_tile(1)
        LAZY = {"on": True}
        prevC = []
        prevR = []
        nl = len(layers)
        for li, l in enumerate(layers):
            xsrc, B_xsrc = (xT, B_xin) if (li == 0 and first) else (xres_d, B_xres)
            S_.wait_all("gpsimd", prevR)
            load_win(l)
            S_.wait_all("sync", prevR + prevC)
            S_.wait_all("vector", prevR)
            S_.wait_all("scalar", prevR)
            S_.wait_all("tensor", prevR)
            if dbg == "T":
                break
            bufsA = phase_A(l, xsrc, B_xsrc)
            if LAZY["on"]:
                LAZY["on"] = False
                bufsA = bufsA + tb_bufs
            if dbg == "A":
                break
            for eng in ("sync", "gpsimd", "vector", "scalar", "tensor"):
                S_.wait_all(eng, bufsA + P_win)
            load_wo(l)
            bufsB = phase_B(l)
            if dbg == "B":
                break
            for eng in ("sync", "gpsimd", "vector", "scalar", "tensor"):
                S_.wait_all(eng, bufsB)
            load_ffn(l)
            is_last = (li == nl - 1)
            bufsC = phase_C(l, xsrc, B_xsrc, is_last)
            prevR = bufsC + P_wupg + P_wupv + P_wdn
            prevC = bufsC
        S_.wait_all("sync", [B_out, B_xres])
        S_.final_wait("sync")
        S_.emit()
    return nc


def _prep_shared(inputs):
    w_in = np.asarray(inputs["w_in"], np.float32)
    qa, ka, va = w_in[..., 0:256], w_in[..., 256:512], w_in[..., 512:768]
    qb, kb, vb = w_in[..., 768:1024], w_in[..., 1024:1280], w_in[..., 1280:1536]
    qc, kc, vc = w_in[..., 1536:2048], w_in[..., 2048:2560], w_in[..., 2560:3072]
    w_in_r = np.ascontiguousarray(np.concatenate([ka, kb, kc, qa, qb, qc, va, vb, vc], axis=-1))
    pc = np.zeros((128, PC_N), np.float32)
    g_mix = np.asarray(inputs["g_mix"], np.float32)
    g_ffn = np.asarray(inputs["g_ffn"], np.float32)
    g_fin = np.asarray(inputs["g_final"], np.float32)
    conv_w = np.asarray(inputs["conv_w"], np.float32)
    conv_b = np.asarray(inputs["conv_b"], np.float32)
    g_diff = np.asarray(inputs["g_diff"], np.float32)
    for l in range(DEPTH):
        pc[:, PC_GMIX + 8 * l:PC_GMIX + 8 * l + 8] = g_mix[l].reshape(8, 128).T
        pc[:, PC_GFFN + 8 * l:PC_GFFN + 8 * l + 8] = g_ffn[l].reshape(8, 128).T
        for j in range(3):
            pc[:, PC_CONVW + l * 132 + j * 44:PC_CONVW + l * 132 + j * 44 + 44] = conv_w[l, j].reshape(44, 128).T
        pc[:, PC_CONVB + l * 44:PC_CONVB + l * 44 + 44] = conv_b[l].reshape(44, 128).T
        pc[:, PC_GDIFF + l] = np.concatenate([g_diff[l], g_diff[l]])
    pc[:, PC_GFIN:PC_GFIN + 8] = g_fin.reshape(8, 128).T
    lam = np.stack([np.asarray(inputs[k], np.float32) for k in ("lambda_q1", "lambda_k1", "lambda_q2", "lambda_k2")],
                   axis=1)
    cbm, cfm = _host_consts()
    return {
        "w_in": w_in_r,
        "w_out": np.ascontiguousarray(np.asarray(inputs["w_out"], np.float32)),
        "w_up": np.ascontiguousarray(np.asarray(inputs["w_up"], np.float32)),
        "w_down": np.ascontiguousarray(np.asarray(inputs["w_down"], np.float32)),
        "pcols": pc,
        "lam_in": np.ascontiguousarray(lam.reshape(1, -1)),
        "cb": cbm,
        "cf": cfm,
    }


_NC_CACHE = {}
import os
_DBG = os.environ.get("KDBG")


def kernel(**inputs):
    x = np.asarray(inputs["x"], np.float32)
    positions = np.asarray(inputs["positions"], np.int32)
    Bn, S, _ = x.shape
    shared = _prep_shared(inputs)
    shared["kind"] = _kind(S)
    key = (S,)
    if key not in _NC_CACHE:
        _NC_CACHE[key] = build(S=S, layers=(0, 1), final=True, first=True, dbg=_DBG)
    nc = _NC_CACHE[key]
    in_maps = []
    for b in range(Bn):
        m = dict(shared)
        m["xT"] = np.ascontiguousarray(x[b].T)
        m["pos"] = np.ascontiguousarray(positions[b].reshape(1, S))
        in_maps.append(m)
    res = run_bass_kernel_spmd(nc, in_maps, core_ids=list(range(Bn)))
    out = np.stack([np.ascontiguousarray(r["outT"].T) for r in res.results], axis=0)
    return out.astype(np.float32)
```

```python
import math
import os
from contextlib import ExitStack

import numpy as np
import ml_dtypes
import concourse.bass as bass
import concourse.mybir as mybir
from concourse.bass_utils import run_bass_kernel_spmd

F32 = mybir.dt.float32
BF16 = mybir.dt.bfloat16
I32 = mybir.dt.int32
ALU = mybir.AluOpType
AF = mybir.ActivationFunctionType

D = 1024
DFF = 2816
NFF = 2 * DFF
NCH_FF = NFF // 128
DEPTH = 2
EPS = 1e-6
THETA = 500000.0
NEGM = -30000.0
TA = 512
TC = 256


class Buf:
    def __init__(self, name, sem=None):
        self.name = name
        self.sem = sem
        self.writers = []
        self.readers = []
        self.excl = False


class _Rec:
    def __init__(self):
        self.call = None

    def __getattr__(self, name):
        def f(*a, **k):
            self.call = (name, a, k)
            return self
        return f


class Sched:
    def __init__(self, nc, es):
        self.nc = nc
        self.es = es
        self.names = ["tensor", "vector", "scalar", "gpsimd", "sync"]
        self.sem = {}
        self.cnt = {}
        self.isdma = {}
        for n in self.names:
            self.sem[n] = es.enter_context(nc.semaphore("e_" + n))
            self.cnt[n] = 0
            self.isdma[n] = False
        self.prog = {n: [] for n in self.names}
        self.waited = {n: {} for n in self.names}
        self.nslot = 0

    def buf(self, name, dma=False):
        b = Buf(name)
        if dma:
            self.nslot += 1
            k = "d%d" % self.nslot
            self.sem[k] = self.es.enter_context(self.nc.semaphore(k))
            self.cnt[k] = 0
            self.isdma[k] = True
            b.sem = k
        return b

    def _waits(self, eng, deps):
        for (k, v) in deps:
            if k == "tensor" and eng == "tensor":
                continue
            if self.isdma[k]:
                v = max(v, self.cnt[k])
            if self.waited[eng].get(k, 0) >= v:
                continue
            self.waited[eng][k] = v
            sem = self.sem[k]
            self.prog[eng].append(lambda e, sem=sem, v=v: e.wait_ge(sem, v))

    def _deps(self, reads, writes):
        deps = []
        for b in reads:
            deps += b.writers
            if b.excl:
                deps += b.readers
        for b in writes:
            deps += b.writers + b.readers
        return deps

    def _commit(self, tok, reads, writes):
        for b in reads:
            b.readers.append(tok)
        for b in writes:
            b.writers = [tok]
            b.readers = []

    def op(self, eng, fn, reads=(), writes=()):
        self._waits(eng, self._deps(reads, writes))
        self.cnt[eng] += 1
        sem = self.sem[eng]
        rec = _Rec()
        fn(rec)
        name, a, k = rec.call
        self.prog[eng].append(lambda e, name=name, a=a, k=k, sem=sem: getattr(e, name)(*a, **k).then_inc(sem, 1))
        tok = (eng, self.cnt[eng])
        self._commit(tok, reads, writes)
        return tok

    def dma(self, queue, sbuf, out, in_, reads=(), writes=(), **kw):
        self._waits(queue, self._deps(reads, writes))
        k = sbuf.sem
        self.cnt[k] += 16
        sem = self.sem[k]
        self.prog[queue].append(
            lambda e, sem=sem, out=out, in_=in_, kw=kw: e.dma_start(out=out, in_=in_, **kw).then_inc(sem, 16))
        tok = (k, self.cnt[k])
        self._commit(tok, reads, writes)
        return tok

    def wait_all(self, eng, bufs):
        deps = []
        for b in bufs:
            deps += b.writers + b.readers
        self._waits(eng, deps)

    def final_wait(self, eng="sync"):
        for k in list(self.sem.keys()):
            if self.cnt[k] > 0 and k != eng:
                v = self.cnt[k]
                if self.waited[eng].get(k, 0) >= v:
                    continue
                self.waited[eng][k] = v
                sem = self.sem[k]
                self.prog[eng].append(lambda e, sem=sem, v=v: e.wait_ge(sem, v))

    def emit(self):
        nc = self.nc
        with nc.Block() as block:
            @block.tensor
            def _(e):
                for f in self.prog["tensor"]:
                    f(e)

            @block.vector
            def _(e):
                for f in self.prog["vector"]:
                    f(e)

            @block.scalar
            def _(e):
                for f in self.prog["scalar"]:
                    f(e)

            @block.gpsimd
            def _(e):
                for f in self.prog["gpsimd"]:
                    f(e)

            @block.sync
            def _(e):
                for f in self.prog["sync"]:
                    f(e)


CB_IDENT = 0
CB_TRI = 128
CB_OWN4 = 256
CB_PREV4 = 768
CB_ONES = 1280
CB_PERMA = 1408
CB_PERMB = 1536
CB_N = 1664
CF_INV = 0
CF_EPS = 4
CF_M0 = 5
CF_M1 = 6
CF_MASKTAB = 8
CF_OWNTAB = 8 + 256
CF_N = 8 + 512


def _host_consts():
    cb = np.zeros((128, CB_N), np.float32)
    cb[:, CB_IDENT:CB_IDENT + 128] = np.eye(128)
    p = np.arange(128)[:, None]
    f = np.arange(128)[None, :]
    tri = np.where(p <= f, 0.0, NEGM)
    prev = np.where(p >= f, 0.0, NEGM)
    cb[:, CB_TRI:CB_TRI + 128] = tri
    cb[:, CB_OWN4:CB_OWN4 + 512] = np.tile(tri, (1, 4))
    cb[:, CB_PREV4:CB_PREV4 + 512] = np.tile(prev, (1, 4))
    cb[:, CB_ONES:CB_ONES + 128] = 1.0
    cf = np.zeros((128, CF_N), np.float32)
    permA = np.zeros((128, 128), np.float32)
    permB = np.zeros((128, 128), np.float32)
    for m in range(128):
        r = m % 32
        if r < 4:
            permA[m + 4, m] = 1.0
            cf[m, CF_INV + 0] = THETA ** (-(r) / 4.0)
            cf[m, CF_INV + 1] = -(THETA ** (-(r) / 4.0))
        elif r < 8:
            permA[m - 4, m] = 1.0
            cf[m, CF_INV + 0] = THETA ** (-(r - 4) / 4.0)
            cf[m, CF_INV + 1] = THETA ** (-(r - 4) / 4.0)
        r = m % 64
        if r < 8:
            permB[m + 8, m] = 1.0
            cf[m, CF_INV + 2] = THETA ** (-(r) / 8.0)
            cf[m, CF_INV + 3] = -(THETA ** (-(r) / 8.0))
        elif r < 16:
            permB[m - 8, m] = 1.0
            cf[m, CF_INV + 2] = THETA ** (-(r - 8) / 8.0)
            cf[m, CF_INV + 3] = THETA ** (-(r - 8) / 8.0)
    cb[:, CB_PERMA:CB_PERMA + 128] = permA
    cb[:, CB_PERMB:CB_PERMB + 128] = permB
    cf[:, CF_EPS] = EPS
    cf[:, CF_M0] = ((np.arange(128) % 64) < 32).astype(np.float32)
    cf[:, CF_M1] = ((np.arange(128) % 64) >= 32).astype(np.float32)
    for qb in range(16):
        for j in range(16):
            cf[:, CF_MASKTAB + qb * 16 + j] = 0.0 if j < qb else -1e30
            cf[:, CF_OWNTAB + qb * 16 + j] = 1.0 if j == qb else 0.0
    return cb.astype(ml_dtypes.bfloat16), cf


def _kind(S):
    k = np.zeros((16, S), np.float32)
    for j in range(S // 256):
        k[j, 256 * j:256 * (j + 1)] = 1.0
    return k.astype(ml_dtypes.bfloat16)


PC_GMIX = 0
PC_GFFN = 16
PC_GFIN = 32
PC_CONVW = 40
PC_CONVB = 40 + 264
PC_GDIFF = 40 + 264 + 88
PC_N = PC_GDIFF + 2


def build(S=4096, layers=(0, 1), final=True, first=True, dbg=None):
    NT_A = S // TA
    NT_C = S // TC
    NKT = S // 128
    nc = bass.Bass("TRN2", target_bir_lowering=False)
    dt_in = lambda n, sh, dt: nc.dram_tensor(n, sh, dt, kind="ExternalInput").ap()
    xT = dt_in("xT", [D, S], F32)
    pos = dt_in("pos", [1, S], I32)
    w_in = dt_in("w_in", [DEPTH, D, 3 * D], F32)
    w_out = dt_in("w_out", [DEPTH, D, D], F32)
    w_up = dt_in("w_up", [DEPTH, D, NFF], F32)
    w_down = dt_in("w_down", [DEPTH, DFF, D], F32)
    pcols = dt_in("pcols", [128, PC_N], F32)
    lam_in = dt_in("lam_in", [1, DEPTH * 4 * 32], F32)
    cb_in = dt_in("cb", [128, CB_N], BF16)
    cf_in = dt_in("cf", [128, CF_N], F32)
    kind_in = dt_in("kind", [16, S], BF16)
    outT = nc.dram_tensor("outT", [D, S], F32, kind="ExternalOutput").ap()
    scr = lambda n, sh, dt: nc.dram_tensor(n, sh, dt).ap()
    qA_d = scr("qA_d", [8, 64, S], BF16)
    kA_d = scr("kA_d", [4, 64, S], BF16)
    qB_d = scr("qB_d", [4, 80, S], BF16)
    kB_d = scr("kB_d", [4, 64, S], BF16)
    qC_d = scr("qC_d", [8, 64, S], BF16)
    kC_d = scr("kC_d", [8, 64, S], BF16)
    vaug_d = scr("vaug_d", [S, 16, 128], BF16)
    mixT_d = scr("mixT_d", [D, S], BF16)
    xres_d = scr("xres_d", [D, S], F32)
    tabs_d = scr("tabs_d", [4, 128, S], BF16)

    with ExitStack() as es:
        S_ = Sched(nc, es)
        sbt = lambda n, sh, dt: es.enter_context(nc.sbuf_tensor(n, sh, dt))
        R = sbt("R", [128, 75392], BF16)
        WO = sbt("WO", [128, 8, D], BF16)
        T = sbt("T", [128, 16448], BF16)
        cb = sbt("cbs", [128, CB_N], BF16)
        cf = sbt("cfs", [128, CF_N], F32)
        pc = sbt("pcs", [128, PC_N], F32)
        lamv = sbt("lamv", [128, DEPTH * 4 * 32], F32)
        small = sbt("small", [128, 64], F32)
        kmean = sbt("kmean", [128, 2, 16], F32)
        psbig = es.enter_context(nc.psum_tensor("psbig", [128, 4096], F32))
        pss = [psbig[:, 512 * i:512 * (i + 1)] for i in range(8)]
        PS = [S_.buf("ps%d" % i) for i in range(8)]
        for b_ in PS:
            b_.excl = True

        B_cb = S_.buf("cb", dma=True)
        B_cf = S_.buf("cf", dma=True)
        B_pc = S_.buf("pc", dma=True)
        B_lamv = S_.buf("lamv", dma=True)
        B_small = S_.buf("small")
        B_kmean = S_.buf("kmean")
        B_halo = [S_.buf("halo%d" % i) for i in range(NCH_FF)]
        B_vstg = [S_.buf("vstg%d" % i, dma=True) for i in range(2)]
        B_qA = [S_.buf("qA%d" % i) for i in range(8)]
        B_kA = [S_.buf("kA%d" % i) for i in range(4)]
        B_qB = [S_.buf("qB%d" % i) for i in range(4)]
        B_kB = [S_.buf("kB%d" % i) for i in range(4)]
        B_qC = [S_.buf("qC%d" % i) for i in range(8)]
        B_kC = [S_.buf("kC%d" % i) for i in range(8)]
        B_vaug = S_.buf("vaug")
        B_mixT = S_.buf("mixT")
        B_xres = S_.buf("xres")
        B_tabs = S_.buf("tabs")
        B_tabs_t = [S_.buf("tabs%d" % i) for i in range(NT_A)]
        B_R = S_.buf("Rarena")
        B_T = S_.buf("Tarena")

        ident = cb[:, CB_IDENT:CB_IDENT + 128]
        ones_bf = cb[:, CB_ONES:CB_ONES + 128]

        def Rv(off, n, dt=BF16, parts=slice(0, 128)):
            a = R[parts, off:off + n]
            return a if dt == BF16 else a.bitcast(dt)

        def Tv(off, n, dt=BF16, parts=slice(0, 128)):
            a = T[parts, off:off + n]
            return a if dt == BF16 else a.bitcast(dt)

        S_.dma("sync", B_cb, cb[:], cb_in, writes=[B_cb])
        S_.dma("sync", B_cf, cf[:], cf_in, writes=[B_cf])
        S_.dma("sync", B_pc, pc[:], pcols, writes=[B_pc])
        S_.dma("sync", B_lamv, lamv[:], bass.AP(lam_in.tensor, 0, [[0, 128], [1, DEPTH * 4 * 32]]), writes=[B_lamv])
        S_.op("gpsimd", lambda e: e.memset(kmean[:], 0.0), writes=[B_kmean])
        S_.op("gpsimd", lambda e: e.memset(small[:], 0.0), writes=[B_small])

        def col(off, parts=slice(0, 128)):
            return pc[parts, off:off + 1]

        def ccol(off, parts=slice(0, 128)):
            return cf[parts, off:off + 1]

        def lam_setup(l):
            lam_init = 0.8 - 0.6 * math.exp(-0.3 * l)
            base = l * 128
            prod = small[:, 32:64]
            for i, (a, b) in enumerate(((0, 1), (2, 3))):
                S_.op("vector", lambda e, a=a, b=b: e.tensor_tensor(
                    out=prod, in0=lamv[:, base + 32 * a:base + 32 * a + 32],
                    in1=lamv[:, base + 32 * b:base + 32 * b + 32], op=ALU.mult),
                    reads=[B_lamv], writes=[B_small])
                S_.op("vector", lambda e, i=i: e.tensor_reduce(
                    out=small[:, 16 + i:17 + i], in_=prod, axis=mybir.AxisListType.X, op=ALU.add),
                    reads=[B_small], writes=[B_small])
            S_.op("scalar", lambda e: e.activation(out=small[:, 18:20], in_=small[:, 16:18], func=AF.Exp),
                  reads=[B_small], writes=[B_small])
            S_.op("vector", lambda e: e.tensor_tensor(out=small[:, 4 * l:4 * l + 1], in0=small[:, 18:19],
                                                     in1=small[:, 19:20], op=ALU.subtract),
                  reads=[B_small], writes=[B_small])
            S_.op("vector", lambda e: e.tensor_scalar(out=small[:, 4 * l + 1:4 * l + 2], in0=small[:, 4 * l:4 * l + 1],
                                                     scalar1=lam_init, scalar2=-1.0, op0=ALU.add, op1=ALU.mult),
                  reads=[B_small], writes=[B_small])
            S_.op("vector", lambda e: e.tensor_scalar(out=small[:, 4 * l + 2:4 * l + 3],
                                                     in0=pc[:, PC_GDIFF + l:PC_GDIFF + l + 1],
                                                     scalar1=1.0 - lam_init, scalar2=None, op0=ALU.mult),
                  reads=[B_pc, B_small], writes=[B_small])

        for l in layers:
            lam_setup(l)

        TBS = {}

        def tables_init():
            pos_i = Tv(0, 2 * TA, I32)
            ang = Tv(2 * TA, 2 * TA, F32)
            kk = Tv(4 * TA, 2 * TA, F32)
            tb = [Tv(6 * TA + i * TA, TA, BF16) for i in range(2)]
            Bp = S_.buf("pos_i", dma=True)
            Ba = S_.buf("ang")
            Bk = S_.buf("kk")
            Btb = [S_.buf("tb%d" % i, dma=True) for i in range(2)]
            TWO_PI = 2 * np.pi
            c1 = float(np.float32(6.28125))
            c2 = float(np.float32(TWO_PI - 6.28125))
            c3 = float(TWO_PI - 6.28125 - float(np.float32(TWO_PI - 6.28125)))
            MAGIC = 12582912.0
            TBS["posf"] = Tv(8 * TA, 2 * TA, F32)
            TBS["Bpf"] = S_.buf("posf")
            TBS.update(dict(pos_i=pos_i, ang=ang, kk=kk, tb=tb, Bp=Bp, Ba=Ba, Bk=Bk, Btb=Btb, TWO_PI=TWO_PI,
                            c1=c1, c2=c2, c3=c3, MAGIC=MAGIC, n=0))
            return [Bp, Ba, Bk, TBS["Bpf"]] + Btb

        def tables_tile(t):
            pos_i, ang, kk, tb, Bp, Ba, Bk, Btb = (TBS[k_] for k_ in ("pos_i", "ang", "kk", "tb", "Bp", "Ba", "Bk", "Btb"))
            TWO_PI, c1, c2, c3, MAGIC = (TBS[k_] for k_ in ("TWO_PI", "c1", "c2", "c3", "MAGIC"))
            n = TBS["n"]
            for t in [t]:
                cs = slice(t * TA, (t + 1) * TA)
                S_.dma("sync", Bp, pos_i, bass.AP(pos.tensor, t * TA, [[0, 128], [1, TA]]), writes=[Bp])
                S_.op("vector", lambda e: e.tensor_copy(out=TBS["posf"], in_=pos_i), reads=[Bp], writes=[TBS["Bpf"]])
                for k in range(4):
                    shift = (np.pi / 2) if k in (0, 2) else 0.0
                    S_.op("vector", lambda e, k=k, shift=shift: e.tensor_scalar(
                        out=ang, in0=TBS["posf"], scalar1=ccol(CF_INV + k), scalar2=shift, op0=ALU.mult, op1=ALU.add),
                        reads=[TBS["Bpf"], B_cf], writes=[Ba])
                    S_.op("vector", lambda e: e.tensor_scalar(out=kk, in0=ang, scalar1=1.0 / TWO_PI, scalar2=MAGIC,
                                                             op0=ALU.mult, op1=ALU.add), reads=[Ba], writes=[Bk])
                    S_.op("vector", lambda e: e.tensor_scalar(out=kk, in0=kk, scalar1=MAGIC, scalar2=None,
                                                             op0=ALU.subtract), reads=[Bk], writes=[Bk])
                    for cc in (c1, c2):
                        S_.op("vector", lambda e, cc=cc: e.scalar_tensor_tensor(
                            out=ang, in0=kk, scalar=-cc, in1=ang, op0=ALU.mult, op1=ALU.add),
                            reads=[Bk, Ba], writes=[Ba])
                    S_.op("vector", lambda e: e.tensor_scalar(out=ang, in0=ang, scalar1=3.1415925, scalar2=-3.1415925,
                                                             op0=ALU.min, op1=ALU.max), reads=[Ba], writes=[Ba])
                    i = n % 2
                    n += 1
                    S_.op("scalar", lambda e, i=i: e.activation(out=tb[i], in_=ang, func=AF.Sin),
                          reads=[Ba], writes=[Btb[i]])
                    S_.dma("sync", Btb[i], tabs_d[k, :, cs], tb[i], reads=[Btb[i]], writes=[B_tabs_t[t]])
            TBS["n"] = n

        WIN_OFF = 0
        WUP_OFF = 0
        WDN_OFF = 8 * NFF
        B_win = S_.buf("win", dma=True)
        B_wup = S_.buf("wup", dma=True)
        B_wdn = S_.buf("wdn", dma=True)
        B_wo = S_.buf("wo", dma=True)

        def Win(k, c0, n):
            return R[:, WIN_OFF + k * 3072 + c0:WIN_OFF + k * 3072 + c0 + n]

        def Wup(k, c0, n):
            return R[:, WUP_OFF + k * NFF + c0:WUP_OFF + k * NFF + c0 + n]

        def Wdn(j, c0, n):
            return R[:, WDN_OFF + j * D + c0:WDN_OFF + j * D + c0 + n]

        P_win = [S_.buf("win%d" % i, dma=True) for i in range(6)]
        P_wupg = [S_.buf("wupg%d" % i, dma=True) for i in range(6)]
        P_wupv = [S_.buf("wupv%d" % i, dma=True) for i in range(6)]
        P_wdn = [S_.buf("wdn%d" % i, dma=True) for i in range(6)]

        def _piece(Bw, dst, src):
            S_.dma("gpsimd", Bw, dst, src, writes=[Bw])

        def load_win(l):
            dst = R[:, WIN_OFF:WIN_OFF + 8 * 3072].rearrange("p (k n) -> p k n", k=8)
            sv = w_in[l].rearrange("(k p) n -> p k n", p=128)
            for i in range(6):
                _piece(P_win[i], dst[:, :, 512 * i:512 * i + 512], sv[:, :, 512 * i:512 * i + 512])

        def load_ffn(l):
            dup = R[:, WUP_OFF:WUP_OFF + 8 * NFF].rearrange("p (k n) -> p k n", k=8)
            sup = w_up[l].rearrange("(k p) n -> p k n", p=128)
            ddn = R[:, WDN_OFF:WDN_OFF + 22 * D].rearrange("p (k n) -> p k n", k=22)
            sdn = w_down[l].rearrange("(k p) n -> p k n", p=128)
            for i in range(6):
                w_ = 512 if i < 5 else 256
                _piece(P_wupg[i], dup[:, :, 512 * i:512 * i + w_], sup[:, :, 512 * i:512 * i + w_])
                _piece(P_wupv[i], dup[:, :, DFF + 512 * i:DFF + 512 * i + w_], sup[:, :, DFF + 512 * i:DFF + 512 * i + w_])
                j0, j1 = 4 * i, min(4 * i + 4, 22)
                _piece(P_wdn[i], ddn[:, j0:j1, :], sdn[:, j0:j1, :])

        def load_wo(l):
            sv = w_out[l].rearrange("(k p) n -> p k n", p=128)
            first = True
            for c0 in range(0, D, 512):
                S_.dma("gpsimd", B_wo, WO[:, :, c0:c0 + 512], sv[:, :, c0:c0 + 512], writes=[B_wo] if first else [])
                first = False
            B_wo.writers = [(B_wo.sem, S_.cnt[B_wo.sem])]

        CH = ([("kA", i, 128 * i) for i in range(2)] + [("kB", i, 256 + 128 * i) for i in range(2)] +
              [("kC", i, 512 + 128 * i) for i in range(4)] + [("qA", i, 1024 + 128 * i) for i in range(2)] +
              [("qB", i, 1280 + 128 * i) for i in range(2)] + [("qC", i, 1536 + 128 * i) for i in range(4)])
        VCOL = 2048

        def phase_A(l, xsrc, B_xsrc):
            A0 = 8 * 3072
            o2 = 67584 + 4096
            xt = [Rv(A0 + i * 8192, 8192, F32).rearrange("p (c t) -> p c t", c=8) for i in range(2)]
            o = A0 + 16384
            tab = [Rv(o + i * 2048, 2048).rearrange("p (k t) -> p k t", k=4) for i in range(2)]
            o += 4096
            hT = Rv(o, 4096).rearrange("p (c t) -> p c t", c=8)
            o += 4096
            sq = Rv(o, 4096).rearrange("p (c t) -> p c t", c=8)
            o += 4096
            rstd = Rv(o, 1024, F32)
            o += 1024
            lnt = Rv(o, 1024, F32)
            o += 1024
            t1 = [Rv(o + i * 1024, 1024, F32) for i in range(2)]
            o += 2048
            t2 = [Rv(o + i * 1024, 1024, F32) for i in range(2)]
            o += 2048
            r32 = [Rv(o + i * 1024, 1024, F32) for i in range(4)]
            o += 4096
            qb16 = [Rv(o + i * 512, 512) for i in range(2)]
            o += 1024
            stg = [Rv(o + i * 512, 512) for i in range(4)]
            o += 2048
            gm = Rv(o, 256, F32)
            o += 256
            t8 = Rv(o, 64, F32)
            o += 64
            thr = Rv(o, 8, F32)
            o += 8
            btoks = [Rv(o, 64), Rv(o2 + 2048 + 64, 64)]
            B_btok = [S_.buf("btok0"), S_.buf("btok1")]
            o += 64
            stgB = Rv(o, 512, parts=slice(0, 64))
            o += 512
            assert o <= 67584
            qh = [Rv(o2 + i * 512, 512) for i in range(2)]
            ql = [Rv(o2 + 1024 + i * 512, 512) for i in range(2)]
            kmh = Rv(o2 + 2048, 32).rearrange("p (c j) -> p c j", c=2)
            kml = Rv(o2 + 2048 + 32, 32).rearrange("p (c j) -> p c j", c=2)
            assert o2 + 2048 + 128 <= 75392
            B_qh = [S_.buf("qh%d" % i, dma=True) for i in range(2)]
            B_ql = [S_.buf("ql%d" % i) for i in range(2)]
            B_kmhl = S_.buf("kmhl")
            vstg = [Rv(67584 + i * 2048, 2048).rearrange("p (h c) -> p h c", h=16) for i in range(2)]
            for i in range(2):
                S_.op("gpsimd", lambda e, i=i: e.memset(vstg[i], 1.0), writes=[B_vstg[i]])
            B_xt = [S_.buf("xt%d" % i, dma=True) for i in range(2)]
            B_tab = [S_.buf("tab%d" % i, dma=True) for i in range(2)]
            B_hT = S_.buf("hT")
            B_sq = S_.buf("sq")
            B_sqc = [S_.buf("sq%d" % c) for c in range(8)]
            B_rstd = S_.buf("rstd")
            B_lnt = S_.buf("lnt")
            B_t1 = [S_.buf("t1%d" % i) for i in range(2)]
            B_t2 = [S_.buf("t2%d" % i) for i in range(2)]
            B_r32 = [S_.buf("r32%d" % i) for i in range(4)]
            B_qb16 = [S_.buf("qb16%d" % i) for i in range(2)]
            B_stg = [S_.buf("stg%d" % i, dma=True) for i in range(4)]
            B_gm = S_.buf("gm")
            B_stgB = S_.buf("stgB", dma=True)
            xv = xsrc.rearrange("(c p) t -> p c t", p=128)
            tv = tabs_d.rearrange("k p t -> p k t")

            def gate_sub(tg, s4):
                qblk = (tg * TA + s4 * 128) // 256
                if s4 == 0:
                    S_.op("vector", lambda e: e.tensor_copy(out=kmh, in_=kmean[:]), reads=[B_kmean], writes=[B_kmhl])
                    S_.op("vector", lambda e: e.tensor_tensor(out=kml, in0=kmean[:], in1=kmh, op=ALU.subtract),
                          reads=[B_kmean, B_kmhl], writes=[B_kmhl])
                for h in range(4):
                    cbk, hh = h // 2, h % 2
                    pr_ = slice(64 * hh, 64 * hh + 64)
                    tk_ = slice(128 * s4, 128 * s4 + 128)
                    for mi, (qa_, ka_) in enumerate(((qh, kmh), (qh, kml), (ql, kmh))):
                        gb_ = 7 - hh
                        S_.op("tensor", lambda e, h=h, mi=mi, qa_=qa_, ka_=ka_: e.matmul(
                            pss[gb_][:, 16 * h:16 * h + 16], qa_[cbk][pr_, tk_], ka_[pr_, cbk, :],
                            start=(mi == 0), stop=(mi == 2)),
                            reads=[B_qh[cbk], B_ql[cbk], B_kmhl], writes=[PS[gb_]])
                for h in range(4):
                    S_.op("vector", lambda e, h=h: e.tensor_tensor(
                        out=gm[:, 16 * h:16 * h + 16], in0=pss[7 - (h % 2)][:, 16 * h:16 * h + 16],
                        in1=cf[:, CF_MASKTAB + 16 * qblk:CF_MASKTAB + 16 * qblk + 16],
                        op=ALU.add), reads=[PS[7 - (h % 2)], B_cf], writes=[B_gm])
                for h in range(4):
                    S_.op("vector", lambda e, h=h: e.max(out=t8[:, 8 * h:8 * h + 8], in_=gm[:, 16 * h:16 * h + 16]),
                          reads=[B_gm], writes=[B_gm])
                S_.op("vector", lambda e: e.tensor_scalar(
                    out=thr[:, 0:4], in0=t8[:, 2:32:8], scalar1=-1e29, scalar2=None, op0=ALU.max),
                    reads=[B_gm], writes=[B_gm])
                for h in range(4):
                    S_.op("vector", lambda e, h=h: e.tensor_scalar(
                        out=gm[:, 64 + 16 * h:64 + 16 * h + 16], in0=gm[:, 16 * h:16 * h + 16],
                        scalar1=thr[:, h:h + 1], scalar2=None, op0=ALU.is_ge), reads=[B_gm], writes=[B_gm])
                for h in range(4):
                    S_.op("vector", lambda e, h=h: e.tensor_tensor(
                        out=gm[:, 64 + 16 * h:64 + 16 * h + 16], in0=gm[:, 64 + 16 * h:64 + 16 * h + 16],
                        in1=cf[:, CF_OWNTAB + 16 * qblk:CF_OWNTAB + 16 * qblk + 16], op=ALU.max),
                        reads=[B_gm, B_cf], writes=[B_gm])
                S_.op("vector", lambda e: e.tensor_scalar(out=btoks[s4 % 2], in0=gm[:, 64:128], scalar1=-NEGM, scalar2=NEGM,
                                                         op0=ALU.mult, op1=ALU.add), reads=[B_gm], writes=[B_btok[s4 % 2]])

            def gate_sub2(tg, s4):
                S_.op("tensor", lambda e: e.matmul(pss[5][0:64, 256:384], btoks[s4 % 2], ident, start=True, stop=True),
                      reads=[B_btok[s4 % 2], B_cb], writes=[PS[5]])
                S_.op("vector", lambda e: e.tensor_copy(out=stgB[:, 128 * s4:128 * s4 + 128],
                                                       in_=pss[5][0:64, 256:384]),
                      reads=[PS[5]], writes=[B_stgB])
                if s4 == TA // 128 - 1:
                    csg = slice(tg * TA, (tg + 1) * TA)
                    for h in range(4):
                        S_.dma("sync", B_stgB, qB_d[h, 64:80, csg], stgB[16 * h:16 * h + 16, :],
                               reads=[B_stgB], writes=[B_qB[h]])

            def load_tile(t):
                i = t % 2
                cs = slice(t * TA, (t + 1) * TA)
                S_.dma("sync", B_xt[i], xt[i], xv[:, :, cs], reads=[B_xsrc], writes=[B_xt[i]])
                S_.dma("sync", B_tab[i], tab[i], tv[:, :, cs], reads=[B_tabs_t[t]], writes=[B_tab[i]])

            load_tile(0)
            cnt = {"ps": 0, "n": 0, "stg": 0}
            for t in range(min(NT_A, int(os.environ.get('KDBGT', '99')))):
                i = t % 2
                cs = slice(t * TA, (t + 1) * TA)
                if t + 1 < NT_A:
                    load_tile(t + 1)
                def stats(ti):
                    ii = ti % 2
                    for c in range(8):
                        if True:
                            S_.op("scalar", lambda e, c=c: e.activation(out=sq[:, c, :], in_=xt[ii][:, c, :], func=AF.Square),
                                  reads=[B_xt[ii]], writes=[B_sqc[c]])
                        else:
                            S_.op("gpsimd", lambda e, c=c: e.tensor_tensor(out=sq[:, c, :], in0=xt[ii][:, c, :],
                                                                          in1=xt[ii][:, c, :], op=ALU.mult),
                                  reads=[B_xt[ii]], writes=[B_sqc[c]])
                    for c in range(8):
                        S_.op("tensor", lambda e, c=c: e.matmul(pss[6][:, 0:TA], ones_bf, sq[:, c, :],
                                                               start=(c == 0), stop=(c == 7)),
                              reads=[B_sqc[c], B_cb], writes=[PS[6]])
                    S_.op("scalar", lambda e: e.activation(out=lnt, in_=pss[6][:, 0:TA], func=AF.Ln, scale=1.0 / D,
                                                          bias=ccol(CF_EPS)), reads=[PS[6], B_cf], writes=[B_lnt])
                    S_.op("scalar", lambda e: e.activation(out=rstd, in_=lnt, func=AF.Exp, scale=-0.5),
                          reads=[B_lnt], writes=[B_rstd])

                if t == 0:
                    stats(0)
                for c in range(8):
                    S_.op("vector", lambda e, c=c: e.scalar_tensor_tensor(
                        out=hT[:, c, :], in0=xt[i][:, c, :], scalar=col(PC_GMIX + 8 * l + c), in1=rstd,
                        op0=ALU.mult, op1=ALU.mult), reads=[B_xt[i], B_rstd, B_pc], writes=[B_hT])
                pend = {"f": None}

                def chunk_tail(kind, ci, pq, psw, n):
                    typA = kind in ("kA", "qA")
                    perm = cb[:, CB_PERMA:CB_PERMA + 128] if typA else cb[:, CB_PERMB:CB_PERMB + 128]
                    S_.op("tensor", lambda e, psw=psw, n=n, perm=perm: e.matmul(
                        pss[psw][:, 0:TA], perm, qb16[n], start=True, stop=True),
                        reads=[B_qb16[n], B_cb], writes=[PS[psw]])
                    tc_, ts_ = (0, 1) if typA else (2, 3)
                    S_.op("vector", lambda e, pq=pq, n=n, tc_=tc_: e.tensor_tensor(
                        out=t1[n], in0=pss[pq][:, 0:TA], in1=tab[i][:, tc_, :], op=ALU.mult),
                        reads=[PS[pq], B_tab[i]], writes=[B_t1[n]])
                    S_.op("vector", lambda e, psw=psw, n=n, ts_=ts_: e.tensor_tensor(
                        out=t2[n], in0=pss[psw][:, 0:TA], in1=tab[i][:, ts_, :], op=ALU.mult),
                        reads=[PS[psw], B_tab[i]], writes=[B_t2[n]])


                    def nstg():
                        s = cnt["stg"] % 4
                        cnt["stg"] += 1
                        return s

                    if kind in ("kA", "kC", "qC"):
                        s = nstg()
                        S_.op("gpsimd", lambda e, n=n, s=s: e.tensor_tensor(out=stg[s], in0=t1[n], in1=t2[n], op=ALU.add),
                              reads=[B_t1[n], B_t2[n]], writes=[B_stg[s]])
                        dd, Bd = {"kA": (kA_d, B_kA), "kC": (kC_d, B_kC), "qC": (qC_d, B_qC)}[kind]
                        for hh in range(2):
                            h = 2 * ci + hh
                            S_.dma("sync", B_stg[s], dd[h, :, cs], stg[s][64 * hh:64 * hh + 64, :],
                                   reads=[B_stg[s]], writes=[Bd[h]])
                    elif kind == "qA":
                        ri = 2
                        S_.op("gpsimd", lambda e, n=n: e.tensor_tensor(out=r32[ri], in0=t1[n], in1=t2[n], op=ALU.add),
                              reads=[B_t1[n], B_t2[n]], writes=[B_r32[ri]])
                        for mp in range(2):
                            s = nstg()
                            S_.op("scalar", lambda e, s=s, mp=mp: e.activation(
                                out=stg[s], in_=r32[ri], func=AF.Copy, scale=ccol(CF_M0 + mp)),
                                reads=[B_r32[ri], B_cf], writes=[B_stg[s]])
                            for hh in range(2):
                                h = 2 * ci + hh
                                S_.dma("sync", B_stg[s], qA_d[2 * h + mp, :, cs], stg[s][64 * hh:64 * hh + 64, :],
                                       reads=[B_stg[s]], writes=[B_qA[2 * h + mp]])
                    else:
                        ri = ci if kind == "kB" else 2 + ci
                        S_.op("gpsimd", lambda e, n=n, ri=ri: e.tensor_tensor(out=r32[ri], in0=t1[n], in1=t2[n], op=ALU.add),
                              reads=[B_t1[n], B_t2[n]], writes=[B_r32[ri]])
                        if kind == "kB":
                            s = nstg()
                            S_.op("gpsimd", lambda e, s=s, ri=ri: e.tensor_copy(out=stg[s], in_=r32[ri]),
                                  reads=[B_r32[ri]], writes=[B_stg[s]])
                            for hh in range(2):
                                h = 2 * ci + hh
                                S_.dma("sync", B_stg[s], kB_d[h, :, cs], stg[s][64 * hh:64 * hh + 64, :],
                                       reads=[B_stg[s]], writes=[B_kB[h]])
                        else:
                            S_.op("gpsimd", lambda e, ri=ri: e.tensor_copy(out=qh[ci], in_=r32[ri]),
                                  reads=[B_r32[ri]], writes=[B_qh[ci]])
                            S_.op("gpsimd", lambda e, ri=ri: e.tensor_tensor(out=ql[ci], in0=r32[ri], in1=qh[ci],
                                                                            op=ALU.subtract),
                                  reads=[B_r32[ri], B_qh[ci]], writes=[B_ql[ci]])
                            for hh in range(2):
                                h = 2 * ci + hh
                                S_.dma("sync", B_qh[ci], qB_d[h, 0:64, cs], qh[ci][64 * hh:64 * hh + 64, :],
                                       reads=[B_qh[ci]], writes=[B_qB[h]])
                        if kind == "kB":
                            nb = TA // 256
                            S_.op("vector", lambda e, ri=ri, ci=ci: e.tensor_reduce(
                                out=kmean[:, ci, t * nb:(t + 1) * nb],
                                in_=r32[ri].rearrange("p (b k) -> p b k", b=nb),
                                axis=mybir.AxisListType.X, op=ALU.add), reads=[B_r32[ri]], writes=[B_kmean])

                ALVL = int(os.environ.get("KDBGA", "9"))
                for chi, (kind, ci, wc) in enumerate(CH):
                    pq = cnt["ps"] % 4
                    cnt["ps"] += 1
                    psw = 4 + (cnt["ps"] % 2)
                    for k in range(8):
                        S_.op("tensor", lambda e, k=k, pq=pq, wc=wc: e.matmul(
                            pss[pq][:, 0:TA], Win(k, wc, 128), hT[:, k, :], start=(k == 0), stop=(k == 7)),
                            reads=[B_hT, P_win[wc // 512]], writes=[PS[pq]])
                    SLVL = int(os.environ.get("KDBGS", "9"))
                    if SLVL < 2:
                        continue
                    n = cnt["n"] % 2
                    cnt["n"] += 1
                    S_.op("scalar", lambda e, pq=pq, n=n: e.activation(out=qb16[n], in_=pss[pq][:, 0:TA], func=AF.Copy),
                          reads=[PS[pq]], writes=[B_qb16[n]])
                    if pend["f"] is not None:
                        pend["f"]()
                    pend["f"] = (lambda kind=kind, ci=ci, pq=pq, psw=psw, n=n: chunk_tail(kind, ci, pq, psw, n))
                    if LAZY["on"] and chi == 6 and t + 2 < NT_A:
                        tables_tile(t + 2)
                    if t >= 1 and chi in (2, 5, 8, 10, 12, 14):
                        gi_ = (2, 5, 8, 10, 12, 14).index(chi)
                        if gi_ >= 2:
                            gate_sub2(t - 1, gi_ - 2)
                        if gi_ <= 3:
                            gate_sub(t - 1, gi_)
                    if chi == 13 and t + 1 < NT_A:
                        stats(t + 1)
                if pend["f"] is not None:
                    pend["f"]()
                    pend["f"] = None

                for s4 in range(TA // 128):
                    vi = (t * 4 + s4) % 2
                    for g in range(2):
                        pq = cnt["ps"] % 4
                        cnt["ps"] += 1
                        for k in range(8):
                            S_.op("tensor", lambda e, k=k, pq=pq, g=g, s4=s4: e.matmul(
                                pss[pq][:, 0:512], hT[:, k, 128 * s4:128 * s4 + 128], Win(k, VCOL + 512 * g, 512),
                                start=(k == 0), stop=(k == 7)), reads=[B_hT, P_win[4 + g]], writes=[PS[pq]])
                        S_.op("scalar", lambda e, pq=pq, g=g, vi=vi: e.activation(
                            out=vstg[vi][:, 8 * g:8 * g + 8, 0:64],
                            in_=pss[pq][:, 0:512].rearrange("p (h d) -> p h d", h=8), func=AF.Copy),
                            reads=[PS[pq]], writes=[B_vstg[vi]])
                    r0 = t * TA + s4 * 128
                    S_.dma("sync", B_vstg[vi], vaug_d[r0:r0 + 128, :, :], vstg[vi], reads=[B_vstg[vi]], writes=[B_vaug])
            for s4_ in range(TA // 128):
                gate_sub(NT_A - 1, s4_)
                gate_sub2(NT_A - 1, s4_)
            return (B_xt + B_tab + B_sqc + [B_hT, B_sq, B_rstd, B_lnt] + B_t1 + B_t2 + B_r32 + B_qb16 + B_stg + [B_gm, B_stgB])

        def phase_B(l):
            QS = [Rv(i * 4096, 4096) for i in range(2)]
            KS = [Rv(8192 + i * 4096, 4096) for i in range(2)]
            VS = [Rv(16384 + i * 4096, 4096).rearrange("p (b c) -> p b c", c=128) for i in range(2)]
            accs = [Rv(24576, 8192, F32), Rv(66560, 8192, F32)]
            pT = [Rv(32768 + i * 512, 512) for i in range(4)]
            pT2 = [Rv(32768 + 2048 + i * 1024, 1024) for i in range(3)]
            B_pT2 = [S_.buf("pT2_%d" % i) for i in range(3)]
            o = 32768 + 2048 + 3072
            rec = [Rv(o + i * 1024, 1024, F32, parts=slice(0, 64)) for i in range(2)]
            o += 2048
            o0 = [Rv(o + i * 1024, 1024, F32, parts=slice(0, 64)) for i in range(2)]
            o += 2048
            od = Rv(o, 1024, F32, parts=slice(0, 64))
            o += 1024
            sqd = Rv(o, 512, parts=slice(0, 64))
            o += 512
            lnd = Rv(o, 1024, F32, parts=slice(0, 64))
            o += 1024
            rsd = Rv(o, 1024, F32, parts=slice(0, 64))
            o += 1024
            mstg = [Rv(o + i * 512, 512, parts=slice(0, 64)) for i in range(3)]
            o += 1536
            lnd2 = Rv(o, 1024, F32, parts=slice(64, 128))
            B_lnd2 = S_.buf("lnd2")
            o += 1024
            assert o <= 50176
            KZ = [[Rv(50176 + (2 * i + m_) * 4096, 4096) for m_ in range(2)] for i in range(2)]
            B_KZ = [S_.buf("KZ%d" % i, dma=True) for i in range(2)]
            for i_ in range(2):
                S_.op("gpsimd", lambda e: e.memset(KZ[i_][0][64:128, :], 0.0), writes=[B_KZ[i_]])
                S_.op("gpsimd", lambda e: e.memset(KZ[i_][1][0:64, :], 0.0), writes=[B_KZ[i_]])
            B_Q = [S_.buf("Q%d" % i, dma=True) for i in range(2)]
            B_K = [S_.buf("K%d" % i, dma=True) for i in range(2)]
            B_V = [S_.buf("V%d" % i, dma=True) for i in range(2)]
            B_accs = [S_.buf("acc0"), S_.buf("acc1")]
            B_pT = [S_.buf("pT%d" % i) for i in range(4)]
            B_rec = [S_.buf("rec%d" % i) for i in range(2)]
            B_o0 = [S_.buf("o0%d" % i) for i in range(2)]
            B_od = S_.buf("od")
            B_sqd = S_.buf("sqd")
            B_lnd = S_.buf("lnd")
            B_rsd = S_.buf("rsd")
            B_mstg = [S_.buf("mstg%d" % i, dma=True) for i in range(3)]
            st = {"s": 0, "p": 0, "o": 0, "m": 0, "qk": 0, "v": 0}

            def next_s():
                st["s"] = (st["s"] + 1) % 4
                return st["s"]

            def next_p():
                st["p"] = (st["p"] + 1) % 4
                return st["p"]

            def next_m():
                st["m"] = (st["m"] + 1) % 3
                return st["m"]

            deferred = []

            def run_pipeline(steps, lag=2):
                nst = len(steps)
                for i in range(nst + lag):
                    if i < nst:
                        steps[i][0]()
                        steps[i][1]()
                    j = i - lag
                    if 0 <= j < nst:
                        steps[j][2]()
                    for d_ in list(deferred):
                        d_[0] -= 1
                        if d_[0] <= 0:
                            deferred.remove(d_)
                            d_[1]()
                for d_ in list(deferred):
                    deferred.remove(d_)
                    d_[1]()

            def store_mix(m, row0, cs):
                S_.dma("sync", B_mstg[m], mixT_d[row0:row0 + 64, cs], mstg[m], reads=[B_mstg[m]], writes=[B_mixT])

            def causal_head(kap, qap, kq_bufs, vslot, nmaps, scale, fin):
                steps = []
                for Qt in range(S // 512):
                    cs = slice(Qt * 512, (Qt + 1) * 512)
                    ob = 4 + 2 * (st["o"] % 2)
                    st["o"] += 1
                    last_kt = 4 * Qt + 3
                    for kp in range((last_kt + 1) // 2):
                        for mp in range(nmaps):
                            ctx = {}

                            def qk_fn(kp=kp, mp=mp, ctx=ctx, Qt=Qt):
                                st["r"] = (st.get("r", 0) + 1) % 2
                                r = st["r"]
                                ctx["r"] = r
                                pb = 64 * mp
                                for u in range(2):
                                    kt = 2 * kp + u
                                    j = kt - 4 * Qt
                                    c0 = 128 * max(j, 0)
                                    b0 = 1024 * r + 512 * u
                                    S_.op("tensor", lambda e: e.matmul(
                                        psbig[:, b0 + c0:b0 + 512], kap(mp)[:, 128 * kt:128 * kt + 128],
                                        qap[:, Qt * 512 + c0:(Qt + 1) * 512], start=True, stop=(j < 0)),
                                        reads=kq_bufs, writes=[PS[2 * r + u]])
                                    if j >= 0:
                                        S_.op("tensor", lambda e: e.matmul(
                                            psbig[:, b0 + c0:b0 + c0 + 128], ident, cb[:, CB_TRI:CB_TRI + 128],
                                            start=False, stop=True), reads=[B_cb], writes=[PS[2 * r + u]])

                            def exp_fn(kp=kp, ctx=ctx, Qt=Qt):
                                r = ctx["r"]
                                st["p2"] = (st.get("p2", 0) + 1) % 3
                                p = st["p2"]
                                ctx["p"] = p
                                if 2 * kp + 1 < 4 * Qt:
                                    S_.op("scalar", lambda e: e.activation(
                                        out=pT2[p], in_=psbig[:, 1024 * r:1024 * r + 1024], func=AF.Exp, scale=scale),
                                        reads=[PS[2 * r], PS[2 * r + 1]], writes=[B_pT2[p]])
                                else:
                                    for u in range(2):
                                        c0 = 128 * max(2 * kp + u - 4 * Qt, 0)
                                        S_.op("scalar", lambda e: e.activation(
                                            out=pT2[p][:, 512 * u + c0:512 * u + 512],
                                            in_=psbig[:, 1024 * r + 512 * u + c0:1024 * r + 512 * u + 512],
                                            func=AF.Exp, scale=scale), reads=[PS[2 * r + u]], writes=[B_pT2[p]])

                            def pv_fn(kp=kp, mp=mp, ctx=ctx, ob=ob, last_kt=last_kt, Qt=Qt, cs=cs):
                                p = ctx["p"]
                                for u in range(2):
                                    kt = 2 * kp + u
                                    c0 = 128 * max(kt - 4 * Qt, 0)
                                    S_.op("tensor", lambda e: e.matmul(
                                        pss[ob + mp][:, c0:512], VS[vslot][:, kt, :], pT2[p][:, 512 * u + c0:512 * u + 512],
                                        start=(kt == 0), stop=(kt == last_kt)),
                                        reads=[B_V[vslot], B_pT2[p]], writes=[PS[ob + mp]])
                                if 2 * kp + 1 == last_kt and mp == nmaps - 1:
                                    fin(ob, cs)

                            steps.append((qk_fn, exp_fn, pv_fn))
                run_pipeline(steps, lag=1)

            def fin_diff(h):
                def fin(ob, cs):
                    for mp in range(2):
                        S_.op("vector", lambda e, mp=mp: e.reciprocal(out=rec[mp], in_=pss[ob + mp][64:128, :]),
                              reads=[PS[ob + mp]], writes=[B_rec[mp]])
                        S_.op("vector", lambda e, mp=mp: e.tensor_tensor(out=o0[mp], in0=pss[ob + mp][0:64, :], in1=rec[mp],
                                                                        op=ALU.mult),
                              reads=[PS[ob + mp], B_rec[mp]], writes=[B_o0[mp]])
                    S_.op("vector", lambda e: e.scalar_tensor_tensor(
                        out=od, in0=o0[1], scalar=small[0:64, 4 * l + 1:4 * l + 2], in1=o0[0], op0=ALU.mult, op1=ALU.add),
                        reads=[B_o0[0], B_o0[1], B_small], writes=[B_od])
                    S_.op("vector", lambda e: e.tensor_tensor(out=sqd, in0=od, in1=od, op=ALU.mult),
                          reads=[B_od], writes=[B_sqd])
                    deferred.append([6, lambda: fin2(cs)])

                def fin2(cs):
                    s = next_s()
                    S_.op("tensor", lambda e: e.matmul(pss[s][0:64, :], cb[0:64, CB_ONES:CB_ONES + 64], sqd,
                                                      start=True, stop=True), reads=[B_sqd, B_cb], writes=[PS[s]])
                    S_.op("scalar", lambda e: e.activation(out=lnd, in_=pss[s][0:64, :], func=AF.Ln, scale=1.0 / 64,
                                                          bias=ccol(CF_EPS, slice(0, 64))),
                          reads=[PS[s], B_cf], writes=[B_lnd])
                    S_.op("scalar", lambda e: e.activation(out=rsd, in_=lnd, func=AF.Exp, scale=-0.5),
                          reads=[B_lnd], writes=[B_rsd])
                    m = next_m()
                    S_.op("vector", lambda e: e.scalar_tensor_tensor(
                        out=mstg[m], in0=od, scalar=small[0:64, 4 * l + 2:4 * l + 3], in1=rsd, op0=ALU.mult, op1=ALU.mult),
                        reads=[B_od, B_rsd, B_small], writes=[B_mstg[m]])
                    store_mix(m, 64 * h, cs)
                return fin

            def fin_plain(row0):
                def fin(ob, cs):
                    S_.op("vector", lambda e: e.reciprocal(out=rec[0], in_=pss[ob][64:128, :]),
                          reads=[PS[ob]], writes=[B_rec[0]])
                    m = next_m()
                    S_.op("vector", lambda e: e.tensor_tensor(out=mstg[m], in0=pss[ob][0:64, :], in1=rec[0], op=ALU.mult),
                          reads=[PS[ob], B_rec[0]], writes=[B_mstg[m]])
                    store_mix(m, row0, cs)
                return fin

            def load_v(vslot, h, dil):
                src = vaug_d[:, h, :]
                if dil == 1:
                    sv = src.rearrange("(n i) c -> i n c", i=128)
                    S_.dma("sync", B_V[vslot], VS[vslot][:, 0:NKT, :], sv, reads=[B_vaug], writes=[B_V[vslot]])
                else:
                    nb = S // (128 * dil)
                    sv = src.rearrange("(n i r) c -> i r n c", i=128, r=dil)
                    dv = VS[vslot][:, 0:NKT, :].rearrange("p (r n) c -> p r n c", r=dil)
                    first = True
                    for r in range(dil):
                        S_.dma("sync", B_V[vslot], dv[:, r, :, :], sv[:, r, :, :], reads=[B_vaug],
                               writes=[B_V[vslot]] if first else [])
                        first = False
                    B_V[vslot].writers = [(B_V[vslot].sem, S_.cnt[B_V[vslot].sem])]

            def nqk():
                st["qk"] += 1
                return st["qk"] % 2

            def nv():
                st["v"] += 1
                return st["v"] % 2

            def load_diff(h):
                qk = nqk()
                vs = nv()
                first = True
                for mp in range(2):
                    S_.dma("sync", B_Q[qk], QS[qk][64 * mp:64 * mp + 64, 0:S], qA_d[2 * h + mp], reads=[B_qA[2 * h + mp]],
                           writes=[B_Q[qk]] if first else [])
                    S_.dma("sync", B_KZ[qk], KZ[qk][mp][64 * mp:64 * mp + 64, 0:S], kA_d[h], reads=[B_kA[h]],
                           writes=[B_KZ[qk]] if first else [])
                    first = False
                B_Q[qk].writers = [(B_Q[qk].sem, S_.cnt[B_Q[qk].sem])]
                B_KZ[qk].writers = [(B_KZ[qk].sem, S_.cnt[B_KZ[qk].sem])]
                load_v(vs, h, 1)
                return (qk, vs)

            def comp_diff(h, ctx_):
                qk, vs = ctx_
                causal_head(lambda mp, qk=qk: KZ[qk][mp][0:128, :], QS[qk][0:128, :], [B_KZ[qk], B_Q[qk]], vs, 2,
                            32 ** -0.5, fin_diff(h))

            def load_moba(h):
                qk = nqk()
                vs = nv()
                S_.dma("sync", B_Q[qk], QS[qk][0:80, 0:S], qB_d[h], reads=[B_qB[h]], writes=[B_Q[qk]])
                S_.dma("sync", B_K[qk], KS[qk][0:64, 0:S], kB_d[h], reads=[B_kB[h]], writes=[B_K[qk]])
                S_.dma("sync", B_K[qk], KS[qk][64:80, 0:S], kind_in, reads=[], writes=[])
                B_K[qk].writers = [(B_K[qk].sem, S_.cnt[B_K[qk].sem])]
                load_v(vs, 4 + h, 1)
                return (qk, vs)

            def comp_moba(h, ctx_):
                qk, vs = ctx_
                causal_head(lambda mp, qk=qk: KS[qk][0:80, :], QS[qk][0:80, :], [B_K[qk], B_Q[qk]], vs, 1,
                            0.125, fin_plain(256 + 64 * h))

            def load_pair(pr):
                qk = nqk()
                first = True
                for hh in range(2):
                    S_.dma("sync", B_Q[qk], QS[qk][64 * hh:64 * hh + 64, 0:S], qC_d[2 * pr + hh], reads=[B_qC[2 * pr + hh]],
                           writes=[B_Q[qk]] if first else [])
                    S_.dma("sync", B_KZ[qk], KZ[qk][hh][64 * hh:64 * hh + 64, 0:S], kC_d[2 * pr + hh], reads=[B_kC[2 * pr + hh]],
                           writes=[B_KZ[qk]] if first else [])
                    first = False
                B_Q[qk].writers = [(B_Q[qk].sem, S_.cnt[B_Q[qk].sem])]
                B_KZ[qk].writers = [(B_KZ[qk].sem, S_.cnt[B_KZ[qk].sem])]
                return qk

            def comp_pair(pr, qk):
                for hh in range(2):
                    head = 2 * pr + hh
                    pb = 64 * hh
                    acc = accs[head % 2]
                    B_acc = B_accs[head % 2]
                    for di, dil in enumerate((1, 4, 16)):
                        vs = nv()
                        load_v(vs, 8 + head, dil)
                        nb = S // (128 * dil)
                        groups = []
                        ngr = (S // 128) // 4
                        for g in range(ngr):
                            if dil == 1:
                                blocks = [(0, 4 * g + b) for b in range(4)]
                                astart, ost, ist = 512 * g, 128, 1
                            elif dil == 4:
                                r, n0 = g // (nb // 4), 4 * (g % (nb // 4))
                                blocks = [(r, n0 + b) for b in range(4)]
                                astart, ost, ist = 512 * n0 + r, 512, 4
                            else:
                                n, r0 = g // (dil // 4), 4 * (g % (dil // 4))
                                blocks = [(r0 + b, n) for b in range(4)]
                                astart, ost, ist = 2048 * n + r0, 1, 16
                            groups.append((blocks, astart, ost, ist))
                        steps = []
                        for (blocks, astart, ost, ist) in groups:
                            ctx = {}

                            def tok(r, n, dil=dil):
                                t0 = 128 * n * dil + r
                                return slice(t0, t0 + 127 * dil + 1, dil) if dil > 1 else slice(t0, t0 + 128)

                            def qk_fn(blocks=blocks, ctx=ctx, tok=tok):
                                so = next_s()
                                sp = next_s()
                                ctx["so"], ctx["sp"] = so, sp
                                anyprev = any(n > 0 for (_, n) in blocks)
                                S_.op("tensor", lambda e: e.matmul(pss[so][:, :], ident, cb[:, CB_OWN4:CB_OWN4 + 512],
                                                                  start=True, stop=False), reads=[B_cb], writes=[PS[so]])
                                for b, (r, n) in enumerate(blocks):
                                    S_.op("tensor", lambda e, b=b, r=r, n=n: e.matmul(
                                        pss[so][:, 128 * b:128 * b + 128], KZ[qk][hh][0:128, tok(r, n)],
                                        QS[qk][0:128, tok(r, n)], start=False, stop=True),
                                        reads=[B_KZ[qk], B_Q[qk]], writes=[PS[so]])
                                ctx["anyprev"] = anyprev
                                if anyprev:
                                    S_.op("tensor", lambda e: e.matmul(pss[sp][:, :], ident, cb[:, CB_PREV4:CB_PREV4 + 512],
                                                                      start=True, stop=False), reads=[B_cb], writes=[PS[sp]])
                                    for b, (r, n) in enumerate(blocks):
                                        if n == 0:
                                            continue
                                        S_.op("tensor", lambda e, b=b, r=r, n=n: e.matmul(
                                            pss[sp][:, 128 * b:128 * b + 128], KZ[qk][hh][0:128, tok(r, n - 1)],
                                            QS[qk][0:128, tok(r, n)], start=False, stop=True),
                                            reads=[B_KZ[qk], B_Q[qk]], writes=[PS[sp]])

                            def exp_fn(ctx=ctx):
                                po = next_p()
                                ctx["po"] = po
                                S_.op("scalar", lambda e: e.activation(out=pT[po], in_=pss[ctx["so"]][:, :], func=AF.Exp,
                                                                      scale=0.125), reads=[PS[ctx["so"]]], writes=[B_pT[po]])
                                if ctx["anyprev"]:
                                    pp = next_p()
                                    ctx["pp"] = pp
                                    S_.op("scalar", lambda e: e.activation(out=pT[pp], in_=pss[ctx["sp"]][:, :], func=AF.Exp,
                                                                          scale=0.125),
                                          reads=[PS[ctx["sp"]]], writes=[B_pT[pp]])

                            def pv_fn(blocks=blocks, ctx=ctx, astart=astart, ost=ost, ist=ist, di=di, vs=vs, dil=dil, nb=nb):
                                ob = 4 + (st["o"] % 4)
                                st["o"] += 1
                                firstmm = True
                                for b, (r, n) in enumerate(blocks):
                                    S_.op("tensor", lambda e, b=b, r=r, n=n, firstmm=firstmm: e.matmul(
                                        pss[ob][:, 128 * b:128 * b + 128], VS[vs][:, r * nb + n, :],
                                        pT[ctx["po"]][:, 128 * b:128 * b + 128], start=firstmm, stop=(n == 0)),
                                        reads=[B_V[vs], B_pT[ctx["po"]]], writes=[PS[ob]])
                                    firstmm = False
                                    if n > 0:
                                        S_.op("tensor", lambda e, b=b, r=r, n=n: e.matmul(
                                            pss[ob][:, 128 * b:128 * b + 128], VS[vs][:, r * nb + n - 1, :],
                                            pT[ctx["pp"]][:, 128 * b:128 * b + 128], start=False, stop=True),
                                            reads=[B_V[vs], B_pT[ctx["pp"]]], writes=[PS[ob]])
                                if dil == 1:
                                    av = acc[:, astart:astart + 512].rearrange("p (b i) -> p b i", b=4)
                                elif dil == 4:
                                    n0_ = blocks[0][1]
                                    av = acc[:, 512 * n0_:512 * n0_ + 2048].rearrange("p (b i r) -> p b i r", b=4, r=4)[:, :, :, blocks[0][0]]
                                else:
                                    n_ = blocks[0][1]
                                    r0_ = blocks[0][0]
                                    av = acc[:, 2048 * n_:2048 * n_ + 2048].rearrange("p (i r) -> p r i", r=16)[:, r0_:r0_ + 4, :]
                                pv = pss[ob][:, :].rearrange("p (b i) -> p b i", b=4)
                                if di == 0:
                                    S_.op("vector", lambda e: e.tensor_copy(out=av, in_=pv), reads=[PS[ob]], writes=[B_acc])
                                else:
                                    S_.op("vector", lambda e: e.tensor_tensor(out=av, in0=pv, in1=av, op=ALU.add),
                                          reads=[PS[ob], B_acc], writes=[B_acc])

                            steps.append((qk_fn, exp_fn, pv_fn))
                        run_pipeline(steps, lag=1)
                    def fin_tile(Qt, acc=acc, B_acc=B_acc, head=head):
                        cs = slice(Qt * 512, (Qt + 1) * 512)
                        S_.op("scalar", lambda e: e.activation(out=lnd2, in_=acc[64:128, cs], func=AF.Ln),
                              reads=[B_acc], writes=[B_lnd2])
                        S_.op("scalar", lambda e: e.activation(out=rec[0], in_=lnd2, func=AF.Exp, scale=-1.0),
                              reads=[B_lnd2], writes=[B_rec[0]])
                        m = next_m()
                        S_.op("vector", lambda e: e.tensor_tensor(out=mstg[m], in0=acc[0:64, cs], in1=rec[0],
                                                                 op=ALU.mult),
                              reads=[B_acc, B_rec[0]], writes=[B_mstg[m]])
                        store_mix(m, 512 + 64 * head, cs)

                    for Qt in range(S // 512):
                        deferred.append([Qt + 2, lambda Qt=Qt, f_=fin_tile: f_(Qt)])
            jobs = ([(load_diff, comp_diff, h) for h in range(4)] + [(load_moba, comp_moba, h) for h in range(4)] +
                    [(load_pair, comp_pair, p_) for p_ in range(4)])
            ctx_ = jobs[0][0](jobs[0][2])
            for ji, (lf, cf_, arg) in enumerate(jobs):
                nxt_ = jobs[ji + 1][0](jobs[ji + 1][2]) if ji + 1 < len(jobs) else None
                cf_(arg, ctx_)
                ctx_ = nxt_
            for d_ in list(deferred):
                deferred.remove(d_)
                d_[1]()
            return (B_Q + B_K + B_KZ + B_V + B_accs + B_pT + B_pT2 + B_rec + B_o0 + [B_od, B_sqd, B_lnd, B_rsd] + B_mstg)

        def phase_C(l, xsrc, B_xsrc, is_last):
            xt = [Tv(i * 4096, 4096, F32).rearrange("p (c t) -> p c t", c=8) for i in range(2)]
            o = 8192
            mt = Tv(o, 2048).rearrange("p (c t) -> p c t", c=8)
            o += 2048
            h2e = Tv(o, 8 * (TC + 2)).rearrange("p (c t) -> p c t", c=8)
            h2 = h2e[:, :, 2:TC + 2]
            o += 8 * (TC + 2)
            cv = [Tv(o + i * 512, 512, F32) for i in range(4)]
            o += 2048
            rstd = Tv(o, 512, F32)
            o += 512
            lnt = Tv(o, 512, F32)
            o += 512
            cvb = [Tv(o + i * 512, 512, F32) for i in range(2)]
            B_cvb = [S_.buf("cvb%d" % i) for i in range(2)]
            o += 1024
            assert o <= 16448
            G0 = WDN_OFF + 22 * D
            gact = Rv(G0, 22 * TC).rearrange("p (j t) -> p j t", j=22)
            sq = Rv(G0 + 22 * TC, 8 * TC).rearrange("p (c t) -> p c t", c=8)
            htmp = Rv(G0 + 30 * TC, 4, F32)
            assert G0 + 30 * TC + 8 <= 75392
            B_xt = [S_.buf("cxt%d" % i, dma=True) for i in range(2)]
            B_mt = S_.buf("mt", dma=True)
            B_h2 = S_.buf("h2")
            B_cv = [S_.buf("cv%d" % i) for i in range(4)]
            B_rstd = S_.buf("crstd")
            B_lnt = S_.buf("clnt")
            B_g = S_.buf("gact")
            B_sq = S_.buf("csq")
            B_sqc = [S_.buf("csq%d" % c) for c in range(8)]
            B_ht = S_.buf("htmp")
            xv = xsrc.rearrange("(c p) t -> p c t", p=128)
            mv = mixT_d.rearrange("(c p) t -> p c t", p=128)
            ov = (outT if is_last else xres_d).rearrange("(c p) t -> p c t", p=128)
            cw = lambda j, ci: col(PC_CONVW + l * 132 + j * 44 + ci)
            cbias = lambda ci: col(PC_CONVB + l * 44 + ci)
            pcnt = {"a": 0, "u": 0, "cv": 0}

            def norm_stats(xi, gbase, out_fn, bank=2):
                for c in range(8):
                    if True:
                        S_.op("scalar", lambda e, c=c: e.activation(out=sq[:, c, :], in_=xt[xi][:, c, :], func=AF.Square),
                              reads=[B_xt[xi]], writes=[B_sqc[c]])
                    else:
                        S_.op("gpsimd", lambda e, c=c: e.tensor_tensor(out=sq[:, c, :], in0=xt[xi][:, c, :],
                                                                      in1=xt[xi][:, c, :], op=ALU.mult),
                              reads=[B_xt[xi]], writes=[B_sqc[c]])
                for c in range(8):
                    S_.op("tensor", lambda e, c=c: e.matmul(pss[bank][:, 0:TC], ones_bf, sq[:, c, :], start=(c == 0), stop=(c == 7)),
                          reads=[B_sqc[c], B_cb], writes=[PS[bank]])
                S_.op("scalar", lambda e: e.activation(out=lnt, in_=pss[bank][:, 0:TC], func=AF.Ln, scale=1.0 / D,
                                                      bias=ccol(CF_EPS)), reads=[PS[bank], B_cf], writes=[B_lnt])
                S_.op("scalar", lambda e: e.activation(out=rstd, in_=lnt, func=AF.Exp, scale=-0.5),
                      reads=[B_lnt], writes=[B_rstd])
                for c in range(8):
                    out_fn(c, col(gbase + c))

            def load_tile(t):
                i = t % 2
                cs = slice(t * TC, (t + 1) * TC)
                S_.dma("sync", B_xt[i], xt[i], xv[:, :, cs], reads=[B_xsrc], writes=[B_xt[i]])

            DBANK = [0, 1, 2, 7]
            B_gj = [S_.buf("gj%d" % j) for j in range(22)]

            def dacc(m):
                return pss[DBANK[m // 2]][:, TC * (m % 2):TC * (m % 2) + TC]

            def down_mm(j):
                for m in range(8):
                    S_.op("tensor", lambda e, m=m: e.matmul(
                        dacc(m), Wdn(j, 128 * m, 128), gact[:, j, :], start=(j == 0 and m % 2 == 0), stop=(j == 21),
                        skip_group_check=True), reads=[P_wdn[j // 4], B_gj[j]], writes=[PS[DBANK[m // 2]]])

            def load_mt(t):
                cs_ = slice(t * TC, (t + 1) * TC)
                S_.dma("sync", B_mt, mt, mv[:, :, cs_], reads=[B_mixT], writes=[B_mt])

            def prologue(t, obanks, sbank):
                xi = t % 2
                if t == 0:
                    S_.op("gpsimd", lambda e: e.memset(h2e[:, :, 0:2], 0.0), writes=[B_h2])
                else:
                    S_.op("scalar", lambda e: e.activation(out=h2e[:, :, 0:2], in_=h2e[:, :, TC:TC + 2], func=AF.Copy),
                          reads=[B_h2], writes=[B_h2])
                for m in range(8):
                    pa = obanks[m % 2]
                    for k in range(8):
                        S_.op("tensor", lambda e, k=k, m=m, pa=pa: e.matmul(
                            pss[pa][:, 0:TC], WO[:, k, 128 * m:128 * m + 128], mt[:, k, :], start=(k == 0), stop=(k == 7)),
                            reads=[B_wo, B_mt], writes=[PS[pa]])
                    S_.op("vector", lambda e, m=m, pa=pa: e.tensor_tensor(out=xt[xi][:, m, :], in0=pss[pa][:, 0:TC],
                                                                         in1=xt[xi][:, m, :], op=ALU.add),
                          reads=[PS[pa], B_xt[xi]], writes=[B_xt[xi]])
                if t + 1 < NT_C:
                    load_mt(t + 1)
                norm_stats(xi, PC_GFFN + 8 * l, lambda c, g: S_.op(
                    "vector", lambda e, c=c, g=g: e.scalar_tensor_tensor(out=h2[:, c, :], in0=xt[xi][:, c, :], scalar=g,
                                                                         in1=rstd, op0=ALU.mult, op1=ALU.mult),
                    reads=[B_xt[xi], B_rstd, B_pc], writes=[B_h2]), bank=sbank)

            load_tile(0)
            load_mt(0)
            prologue(0, (0, 1), 2)
            for t in range(NT_C):
                xi = t % 2
                cs = slice(t * TC, (t + 1) * TC)
                if t + 1 < NT_C:
                    load_tile(t + 1)
                for j in range(22):
                    pu = 3 + 2 * (pcnt["u"] % 2)
                    pcnt["u"] += 1
                    cvi = []
                    for half, ci in enumerate((j, 22 + j)):
                        pb = pu + half
                        for k in range(8):
                            S_.op("tensor", lambda e, k=k, ci=ci, pb=pb: e.matmul(
                                pss[pb][:, 0:TC + 2], Wup(k, 128 * ci, 128), h2e[:, k, :], start=(k == 0), stop=(k == 7)),
                                reads=[(P_wupg if ci < 22 else P_wupv)[(ci % 22) // 4], B_h2], writes=[PS[pb]])
                        c_ = pcnt["cv"] % 4
                        pcnt["cv"] += 1
                        cvi.append(c_)
                        S_.op("scalar", lambda e, ci=ci, pb=pb, c_=c_: e.activation(
                            out=cv[c_], in_=pss[pb][:, 2:TC + 2], func=AF.Identity, scale=cw(2, ci), bias=cbias(ci)),
                            reads=[PS[pb], B_pc], writes=[B_cv[c_]])
                        S_.op("vector", lambda e, ci=ci, pb=pb, c_=c_: e.scalar_tensor_tensor(
                            out=cv[c_], in0=pss[pb][:, 1:TC + 1], scalar=cw(1, ci), in1=cv[c_],
                            op0=ALU.mult, op1=ALU.add), reads=[PS[pb], B_cv[c_], B_pc], writes=[B_cv[c_]])
                        S_.op("vector", lambda e, ci=ci, pb=pb, c_=c_: e.scalar_tensor_tensor(
                            out=cv[c_], in0=pss[pb][:, 0:TC], scalar=cw(0, ci), in1=cv[c_],
                            op0=ALU.mult, op1=ALU.add), reads=[PS[pb], B_cv[c_], B_pc], writes=[B_cv[c_]])
                    if j == 21 and t + 1 < NT_C:
                        prologue(t + 1, (8 - pu, 9 - pu), 8 - pu)
                    cg, cvv = cvi
                    S_.op("scalar", lambda e, cg=cg: e.activation(out=cv[cg], in_=cv[cg], func=AF.Silu),
                          reads=[B_cv[cg]], writes=[B_cv[cg]])
                    S_.op("gpsimd", lambda e, cg=cg, cvv=cvv, j=j: e.tensor_tensor(out=gact[:, j, :], in0=cv[cg], in1=cv[cvv],
                                                                                  op=ALU.mult),
                          reads=[B_cv[cg], B_cv[cvv]], writes=[B_gj[j]])
                    if j >= 2:
                        down_mm(j - 2)
                down_mm(20)
                down_mm(21)
                for m in range(8):
                    S_.op("vector", lambda e, m=m: e.tensor_tensor(out=xt[xi][:, m, :], in0=dacc(m),
                                                                  in1=xt[xi][:, m, :], op=ALU.add),
                          reads=[PS[DBANK[m // 2]], B_xt[xi]], writes=[B_xt[xi]])
                if is_last and final:
                    norm_stats(xi, PC_GFIN, lambda c, g: S_.op(
                        "vector", lambda e, c=c, g=g: e.scalar_tensor_tensor(out=xt[xi][:, c, :], in0=xt[xi][:, c, :], scalar=g,
                                                                             in1=rstd, op0=ALU.mult, op1=ALU.mult),
                        reads=[B_xt[xi], B_rstd, B_pc], writes=[B_xt[xi]]))
                Bo = B_out if is_last else B_xres
                S_.dma("sync", B_xt[xi], ov[:, :, cs], xt[xi], reads=[B_xt[xi]], writes=[Bo])
            return B_xt + [B_mt, B_h2] + B_cv + B_cvb + [B_rstd, B_lnt, B_g, B_ht] + B_gj + B_sqc

        B_out = S_.buf("out")
        B_xin = S_.buf("xin")
        tb_bufs = tables_init()
        tables_tile(0)
        tables_tile(1)
        LAZY = {"on": True}
        prevC = []
        prevR = []
        nl = len(layers)
        for li, l in enumerate(layers):
            xsrc, B_xsrc = (xT, B_xin) if (li == 0 and first) else (xres_d, B_xres)
            S_.wait_all("gpsimd", prevR)
            load_win(l)
            S_.wait_all("sync", prevR + prevC)
            S_.wait_all("vector", prevR)
            S_.wait_all("scalar", prevR)
            S_.wait_all("tensor", prevR)
            if dbg == "T":
                break
            bufsA = phase_A(l, xsrc, B_xsrc)
            if LAZY["on"]:
                LAZY["on"] = False
                bufsA = bufsA + tb_bufs
            if dbg == "A":
                break
            for eng in ("sync", "gpsimd", "vector", "scalar", "tensor"):
                S_.wait_all(eng, bufsA + P_win)
            load_wo(l)
            bufsB = phase_B(l)
            if dbg == "B":
                break
            for eng in ("sync", "gpsimd", "vector", "scalar", "tensor"):
                S_.wait_all(eng, bufsB)
            load_ffn(l)
            is_last = (li == nl - 1)
            bufsC = phase_C(l, xsrc, B_xsrc, is_last)
            prevR = bufsC + P_wupg + P_wupv + P_wdn
            prevC = bufsC
        S_.wait_all("sync", [B_out, B_xres])
        S_.final_wait("sync")
        S_.emit()
    return nc


def _prep_shared(inputs):
    w_in = np.asarray(inputs["w_in"], np.float32)
    qa, ka, va = w_in[..., 0:256], w_in[..., 256:512], w_in[..., 512:768]
    qb, kb, vb = w_in[..., 768:1024], w_in[..., 1024:1280], w_in[..., 1280:1536]
    qc, kc, vc = w_in[..., 1536:2048], w_in[..., 2048:2560], w_in[..., 2560:3072]
    w_in_r = np.ascontiguousarray(np.concatenate([ka, kb, kc, qa, qb, qc, va, vb, vc], axis=-1))
    pc = np.zeros((128, PC_N), np.float32)
    g_mix = np.asarray(inputs["g_mix"], np.float32)
    g_ffn = np.asarray(inputs["g_ffn"], np.float32)
    g_fin = np.asarray(inputs["g_final"], np.float32)
    conv_w = np.asarray(inputs["conv_w"], np.float32)
    conv_b = np.asarray(inputs["conv_b"], np.float32)
    g_diff = np.asarray(inputs["g_diff"], np.float32)
    for l in range(DEPTH):
        pc[:, PC_GMIX + 8 * l:PC_GMIX + 8 * l + 8] = g_mix[l].reshape(8, 128).T
        pc[:, PC_GFFN + 8 * l:PC_GFFN + 8 * l + 8] = g_ffn[l].reshape(8, 128).T
        for j in range(3):
            pc[:, PC_CONVW + l * 132 + j * 44:PC_CONVW + l * 132 + j * 44 + 44] = conv_w[l, j].reshape(44, 128).T
        pc[:, PC_CONVB + l * 44:PC_CONVB + l * 44 + 44] = conv_b[l].reshape(44, 128).T
        pc[:, PC_GDIFF + l] = np.concatenate([g_diff[l], g_diff[l]])
    pc[:, PC_GFIN:PC_GFIN + 8] = g_fin.reshape(8, 128).T
    lam = np.stack([np.asarray(inputs[k], np.float32) for k in ("lambda_q1", "lambda_k1", "lambda_q2", "lambda_k2")],
                   axis=1)
    cbm, cfm = _host_consts()
    return {
        "w_in": w_in_r,
        "w_out": np.ascontiguousarray(np.asarray(inputs["w_out"], np.float32)),
        "w_up": np.ascontiguousarray(np.asarray(inputs["w_up"], np.float32)),
        "w_down": np.ascontiguousarray(np.asarray(inputs["w_down"], np.float32)),
        "pcols": pc,
        "lam_in": np.ascontiguousarray(lam.reshape(1, -1)),
        "cb": cbm,
        "cf": cfm,
    }


_NC_CACHE = {}
import os
_DBG = os.environ.get("KDBG")


def kernel(**inputs):
    x = np.asarray(inputs["x"], np.float32)
    positions = np.asarray(inputs["positions"], np.int32)
    Bn, S, _ = x.shape
    shared = _prep_shared(inputs)
    shared["kind"] = _kind(S)
    key = (S,)
    if key not in _NC_CACHE:
        _NC_CACHE[key] = build(S=S, layers=(0, 1), final=True, first=True, dbg=_DBG)
    nc = _NC_CACHE[key]
    in_maps = []
    for b in range(Bn):
        m = dict(shared)
        m["xT"] = np.ascontiguousarray(x[b].T)
        m["pos"] = np.ascontiguousarray(positions[b].reshape(1, S))
        in_maps.append(m)
    res = run_bass_kernel_spmd(nc, in_maps, core_ids=list(range(Bn)))
    out = np.stack([np.ascontiguousarray(r["outT"].T) for r in res.results], axis=0)
    return out.astype(np.float32)
```

```python
import math
import os
from contextlib import ExitStack

import numpy as np
import ml_dtypes
import concourse.bass as bass
import concourse.mybir as mybir
from concourse.bass_utils import run_bass_kernel_spmd

F32 = mybir.dt.float32
BF16 = mybir.dt.bfloat16
I32 = mybir.dt.int32
ALU = mybir.AluOpType
AF = mybir.ActivationFunctionType

D = 1024
DFF = 2816
NFF = 2 * DFF
NCH_FF = NFF // 128
DEPTH = 2
EPS = 1e-6
THETA = 500000.0
NEGM = -30000.0
TA = 512
TC = 256


class Buf:
    def __init__(self, name, sem=None):
        self.name = name
        self.sem = sem
        self.writers = []
        self.readers = []
        self.excl = False


class _Rec:
    def __init__(self):
        self.call = None

    def __getattr__(self, name):
        def f(*a, **k):
            self.call = (name, a, k)
            return self
        return f


class Sched:
    def __init__(self, nc, es):
        self.nc = nc
        self.es = es
        self.names = ["tensor", "vector", "scalar", "gpsimd", "sync"]
        self.sem = {}
        self.cnt = {}
        self.isdma = {}
        for n in self.names:
            self.sem[n] = es.enter_context(nc.semaphore("e_" + n))
            self.cnt[n] = 0
            self.isdma[n] = False
        self.prog = {n: [] for n in self.names}
        self.waited = {n: {} for n in self.names}
        self.nslot = 0

    def buf(self, name, dma=False):
        b = Buf(name)
        if dma:
            self.nslot += 1
            k = "d%d" % self.nslot
            self.sem[k] = self.es.enter_context(self.nc.semaphore(k))
            self.cnt[k] = 0
            self.isdma[k] = True
            b.sem = k
        return b

    def _waits(self, eng, deps):
        for (k, v) in deps:
            if k == "tensor" and eng == "tensor":
                continue
            if self.isdma[k]:
                v = max(v, self.cnt[k])
            if self.waited[eng].get(k, 0) >= v:
                continue
            self.waited[eng][k] = v
            sem = self.sem[k]
            self.prog[eng].append(lambda e, sem=sem, v=v: e.wait_ge(sem, v))

    def _deps(self, reads, writes):
        deps = []
        for b in reads:
            deps += b.writers
            if b.excl:
                deps += b.readers
        for b in writes:
            deps += b.writers + b.readers
        return deps

    def _commit(self, tok, reads, writes):
        for b in reads:
            b.readers.append(tok)
        for b in writes:
            b.writers = [tok]
            b.readers = []

    def op(self, eng, fn, reads=(), writes=()):
        self._waits(eng, self._deps(reads, writes))
        self.cnt[eng] += 1
        sem = self.sem[eng]
        rec = _Rec()
        fn(rec)
        name, a, k = rec.call
        self.prog[eng].append(lambda e, name=name, a=a, k=k, sem=sem: getattr(e, name)(*a, **k).then_inc(sem, 1))
        tok = (eng, self.cnt[eng])
        self._commit(tok, reads, writes)
        return tok

    def dma(self, queue, sbuf, out, in_, reads=(), writes=(), **kw):
        self._waits(queue, self._deps(reads, writes))
        k = sbuf.sem
        self.cnt[k] += 16
        sem = self.sem[k]
        self.prog[queue].append(
            lambda e, sem=sem, out=out, in_=in_, kw=kw: e.dma_start(out=out, in_=in_, **kw).then_inc(sem, 16))
        tok = (k, self.cnt[k])
        self._commit(tok, reads, writes)
        return tok

    def wait_all(self, eng, bufs):
        deps = []
        for b in bufs:
            deps += b.writers + b.readers
        self._waits(eng, deps)

    def final_wait(self, eng="sync"):
        for k in list(self.sem.keys()):
            if self.cnt[k] > 0 and k != eng:
                v = self.cnt[k]
                if self.waited[eng].get(k, 0) >= v:
                    continue
                self.waited[eng][k] = v
                sem = self.sem[k]
                self.prog[eng].append(lambda e, sem=sem, v=v: e.wait_ge(sem, v))

    def emit(self):
        nc = self.nc
        with nc.Block() as block:
            @block.tensor
            def _(e):
                for f in self.prog["tensor"]:
                    f(e)

            @block.vector
            def _(e):
                for f in self.prog["vector"]:
                    f(e)

            @block.scalar
            def _(e):
                for f in self.prog["scalar"]:
                    f(e)

            @block.gpsimd
            def _(e):
                for f in self.prog["gpsimd"]:
                    f(e)

            @block.sync
            def _(e):
                for f in self.prog["sync"]:
                    f(e)


CB_IDENT = 0
CB_TRI = 128
CB_OWN4 = 256
CB_PREV4 = 768
CB_ONES = 1280
CB_PERMA = 1408
CB_PERMB = 1536
CB_N = 1664
CF_INV = 0
CF_EPS = 4
CF_M0 = 5
CF_M1 = 6
CF_MASKTAB = 8
CF_OWNTAB = 8 + 256
CF_N = 8 + 512


def _host_consts():
    cb = np.zeros((128, CB_N), np.float32)
    cb[:, CB_IDENT:CB_IDENT + 128] = np.eye(128)
    p = np.arange(128)[:, None]
    f = np.arange(128)[None, :]
    tri = np.where(p <= f, 0.0, NEGM)
    prev = np.where(p >= f, 0.0, NEGM)
    cb[:, CB_TRI:CB_TRI + 128] = tri
    cb[:, CB_OWN4:CB_OWN4 + 512] = np.tile(tri, (1, 4))
    cb[:, CB_PREV4:CB_PREV4 + 512] = np.tile(prev, (1, 4))
    cb[:, CB_ONES:CB_ONES + 128] = 1.0
    cf = np.zeros((128, CF_N), np.float32)
    permA = np.zeros((128, 128), np.float32)
    permB = np.zeros((128, 128), np.float32)
    for m in range(128):
        r = m % 32
        if r < 4:
            permA[m + 4, m] = 1.0
            cf[m, CF_INV + 0] = THETA ** (-(r) / 4.0)
            cf[m, CF_INV + 1] = -(THETA ** (-(r) / 4.0))
        elif r < 8:
            permA[m - 4, m] = 1.0
            cf[m, CF_INV + 0] = THETA ** (-(r - 4) / 4.0)
            cf[m, CF_INV + 1] = THETA ** (-(r - 4) / 4.0)
        r = m % 64
        if r < 8:
            permB[m + 8, m] = 1.0
            cf[m, CF_INV + 2] = THETA ** (-(r) / 8.0)
            cf[m, CF_INV + 3] = -(THETA ** (-(r) / 8.0))
        elif r < 16:
            permB[m - 8, m] = 1.0
            cf[m, CF_INV + 2] = THETA ** (-(r - 8) / 8.0)
            cf[m, CF_INV + 3] = THETA ** (-(r - 8) / 8.0)
    cb[:, CB_PERMA:CB_PERMA + 128] = permA
    cb[:, CB_PERMB:CB_PERMB + 128] = permB
    cf[:, CF_EPS] = EPS
    cf[:, CF_M0] = ((np.arange(128) % 64) < 32).astype(np.float32)
    cf[:, CF_M1] = ((np.arange(128) % 64) >= 32).astype(np.float32)
    for qb in range(16):
        for j in range(16):
            cf[:, CF_MASKTAB + qb * 16 + j] = 0.0 if j < qb else -1e30
            cf[:, CF_OWNTAB + qb * 16 + j] = 1.0 if j == qb else 0.0
    return cb.astype(ml_dtypes.bfloat16), cf


def _kind(S):
    k = np.zeros((16, S), np.float32)
    for j in range(S // 256):
        k[j, 256 * j:256 * (j + 1)] = 1.0
    return k.astype(ml_dtypes.bfloat16)


PC_GMIX = 0
PC_GFFN = 16
PC_GFIN = 32
PC_CONVW = 40
PC_CONVB = 40 + 264
PC_GDIFF = 40 + 264 + 88
PC_N = PC_GDIFF + 2


def build(S=4096, layers=(0, 1), final=True, first=True, dbg=None):
    NT_A = S // TA
    NT_C = S // TC
    NKT = S // 128
    nc = bass.Bass("TRN2", target_bir_lowering=False)
    dt_in = lambda n, sh, dt: nc.dram_tensor(n, sh, dt, kind="ExternalInput").ap()
    xT = dt_in("xT", [D, S], F32)
    pos = dt_in("pos", [1, S], I32)
    w_in = dt_in("w_in", [DEPTH, D, 3 * D], F32)
    w_out = dt_in("w_out", [DEPTH, D, D], F32)
    w_up = dt_in("w_up", [DEPTH, D, NFF], F32)
    w_down = dt_in("w_down", [DEPTH, DFF, D], F32)
    pcols = dt_in("pcols", [128, PC_N], F32)
    lam_in = dt_in("lam_in", [1, DEPTH * 4 * 32], F32)
    cb_in = dt_in("cb", [128, CB_N], BF16)
    cf_in = dt_in("cf", [128, CF_N], F32)
    kind_in = dt_in("kind", [16, S], BF16)
    outT = nc.dram_tensor("outT", [D, S], F32, kind="ExternalOutput").ap()
    scr = lambda n, sh, dt: nc.dram_tensor(n, sh, dt).ap()
    qA_d = scr("qA_d", [8, 64, S], BF16)
    kA_d = scr("kA_d", [4, 64, S], BF16)
    qB_d = scr("qB_d", [4, 80, S], BF16)
    kB_d = scr("kB_d", [4, 64, S], BF16)
    qC_d = scr("qC_d", [8, 64, S], BF16)
    kC_d = scr("kC_d", [8, 64, S], BF16)
    vaug_d = scr("vaug_d", [S, 16, 128], BF16)
    mixT_d = scr("mixT_d", [D, S], BF16)
    xres_d = scr("xres_d", [D, S], F32)
    tabs_d = scr("tabs_d", [4, 128, S], BF16)

    with ExitStack() as es:
        S_ = Sched(nc, es)
        sbt = lambda n, sh, dt: es.enter_context(nc.sbuf_tensor(n, sh, dt))
        R = sbt("R", [128, 75392], BF16)
        WO = sbt("WO", [128, 8, D], BF16)
        T = sbt("T", [128, 16448], BF16)
        cb = sbt("cbs", [128, CB_N], BF16)
        cf = sbt("cfs", [128, CF_N], F32)
        pc = sbt("pcs", [128, PC_N], F32)
        lamv = sbt("lamv", [128, DEPTH * 4 * 32], F32)
        small = sbt("small", [128, 64], F32)
        kmean = sbt("kmean", [128, 2, 16], F32)
        psbig = es.enter_context(nc.psum_tensor("psbig", [128, 4096], F32))
        pss = [psbig[:, 512 * i:512 * (i + 1)] for i in range(8)]
        PS = [S_.buf("ps%d" % i) for i in range(8)]
        for b_ in PS:
            b_.excl = True

        B_cb = S_.buf("cb", dma=True)
        B_cf = S_.buf("cf", dma=True)
        B_pc = S_.buf("pc", dma=True)
        B_lamv = S_.buf("lamv", dma=True)
        B_small = S_.buf("small")
        B_kmean = S_.buf("kmean")
        B_halo = [S_.buf("halo%d" % i) for i in range(NCH_FF)]
        B_vstg = [S_.buf("vstg%d" % i, dma=True) for i in range(2)]
        B_qA = [S_.buf("qA%d" % i) for i in range(8)]
        B_kA = [S_.buf("kA%d" % i) for i in range(4)]
        B_qB = [S_.buf("qB%d" % i) for i in range(4)]
        B_kB = [S_.buf("kB%d" % i) for i in range(4)]
        B_qC = [S_.buf("qC%d" % i) for i in range(8)]
        B_kC = [S_.buf("kC%d" % i) for i in range(8)]
        B_vaug = S_.buf("vaug")
        B_mixT = S_.buf("mixT")
        B_xres = S_.buf("xres")
        B_tabs = S_.buf("tabs")
        B_tabs_t = [S_.buf("tabs%d" % i) for i in range(NT_A)]
        B_R = S_.buf("Rarena")
        B_T = S_.buf("Tarena")

        ident = cb[:, CB_IDENT:CB_IDENT + 128]
        ones_bf = cb[:, CB_ONES:CB_ONES + 128]

        def Rv(off, n, dt=BF16, parts=slice(0, 128)):
            a = R[parts, off:off + n]
            return a if dt == BF16 else a.bitcast(dt)

        def Tv(off, n, dt=BF16, parts=slice(0, 128)):
            a = T[parts, off:off + n]
            return a if dt == BF16 else a.bitcast(dt)

        S_.dma("sync", B_cb, cb[:], cb_in, writes=[B_cb])
        S_.dma("sync", B_cf, cf[:], cf_in, writes=[B_cf])
        S_.dma("sync", B_pc, pc[:], pcols, writes=[B_pc])
        S_.dma("sync", B_lamv, lamv[:], bass.AP(lam_in.tensor, 0, [[0, 128], [1, DEPTH * 4 * 32]]), writes=[B_lamv])
        S_.op("gpsimd", lambda e: e.memset(kmean[:], 0.0), writes=[B_kmean])
        S_.op("gpsimd", lambda e: e.memset(small[:], 0.0), writes=[B_small])

        def col(off, parts=slice(0, 128)):
            return pc[parts, off:off + 1]

        def ccol(off, parts=slice(0, 128)):
            return cf[parts, off:off + 1]

        def lam_setup(l):
            lam_init = 0.8 - 0.6 * math.exp(-0.3 * l)
            base = l * 128
            prod = small[:, 32:64]
            for i, (a, b) in enumerate(((0, 1), (2, 3))):
                S_.op("vector", lambda e, a=a, b=b: e.tensor_tensor(
                    out=prod, in0=lamv[:, base + 32 * a:base + 32 * a + 32],
                    in1=lamv[:, base + 32 * b:base + 32 * b + 32], op=ALU.mult),
                    reads=[B_lamv], writes=[B_small])
                S_.op("vector", lambda e, i=i: e.tensor_reduce(
                    out=small[:, 16 + i:17 + i], in_=prod, axis=mybir.AxisListType.X, op=ALU.add),
                    reads=[B_small], writes=[B_small])
            S_.op("scalar", lambda e: e.activation(out=small[:, 18:20], in_=small[:, 16:18], func=AF.Exp),
                  reads=[B_small], writes=[B_small])
            S_.op("vector", lambda e: e.tensor_tensor(out=small[:, 4 * l:4 * l + 1], in0=small[:, 18:19],
                                                     in1=small[:, 19:20], op=ALU.subtract),
                  reads=[B_small], writes=[B_small])
            S_.op("vector", lambda e: e.tensor_scalar(out=small[:, 4 * l + 1:4 * l + 2], in0=small[:, 4 * l:4 * l + 1],
                                                     scalar1=lam_init, scalar2=-1.0, op0=ALU.add, op1=ALU.mult),
                  reads=[B_small], writes=[B_small])
            S_.op("vector", lambda e: e.tensor_scalar(out=small[:, 4 * l + 2:4 * l + 3],
                                                     in0=pc[:, PC_GDIFF + l:PC_GDIFF + l + 1],
                                                     scalar1=1.0 - lam_init, scalar2=None, op0=ALU.mult),
                  reads=[B_pc, B_small], writes=[B_small])

        for l in layers:
            lam_setup(l)

        TBS = {}

        def tables_init():
            pos_i = Tv(0, 2 * TA, I32)
            ang = Tv(2 * TA, 2 * TA, F32)
            kk = Tv(4 * TA, 2 * TA, F32)
            tb = [Tv(6 * TA + i * TA, TA, BF16) for i in range(2)]
            Bp = S_.buf("pos_i", dma=True)
            Ba = S_.buf("ang")
            Bk = S_.buf("kk")
            Btb = [S_.buf("tb%d" % i, dma=True) for i in range(2)]
            TWO_PI = 2 * np.pi
            c1 = float(np.float32(6.28125))
            c2 = float(np.float32(TWO_PI - 6.28125))
            c3 = float(TWO_PI - 6.28125 - float(np.float32(TWO_PI - 6.28125)))
            MAGIC = 12582912.0
            TBS["posf"] = Tv(8 * TA, 2 * TA, F32)
            TBS["Bpf"] = S_.buf("posf")
            TBS.update(dict(pos_i=pos_i, ang=ang, kk=kk, tb=tb, Bp=Bp, Ba=Ba, Bk=Bk, Btb=Btb, TWO_PI=TWO_PI,
                            c1=c1, c2=c2, c3=c3, MAGIC=MAGIC, n=0))
            return [Bp, Ba, Bk, TBS["Bpf"]] + Btb

        def tables_tile(t):
            pos_i, ang, kk, tb, Bp, Ba, Bk, Btb = (TBS[k_] for k_ in ("pos_i", "ang", "kk", "tb", "Bp", "Ba", "Bk", "Btb"))
            TWO_PI, c1, c2, c3, MAGIC = (TBS[k_] for k_ in ("TWO_PI", "c1", "c2", "c3", "MAGIC"))
            n = TBS["n"]
            for t in [t]:
                cs = slice(t * TA, (t + 1) * TA)
                S_.dma("sync", Bp, pos_i, bass.AP(pos.tensor, t * TA, [[0, 128], [1, TA]]), writes=[Bp])
                S_.op("vector", lambda e: e.tensor_copy(out=TBS["posf"], in_=pos_i), reads=[Bp], writes=[TBS["Bpf"]])
                for k in range(4):
                    shift = (np.pi / 2) if k in (0, 2) else 0.0
                    S_.op("vector", lambda e, k=k, shift=shift: e.tensor_scalar(
                        out=ang, in0=TBS["posf"], scalar1=ccol(CF_INV + k), scalar2=shift, op0=ALU.mult, op1=ALU.add),
                        reads=[TBS["Bpf"], B_cf], writes=[Ba])
                    S_.op("vector", lambda e: e.tensor_scalar(out=kk, in0=ang, scalar1=1.0 / TWO_PI, scalar2=MAGIC,
                                                             op0=ALU.mult, op1=ALU.add), reads=[Ba], writes=[Bk])
                    S_.op("vector", lambda e: e.tensor_scalar(out=kk, in0=kk, scalar1=MAGIC, scalar2=None,
                                                             op0=ALU.subtract), reads=[Bk], writes=[Bk])
                    for cc in (c1, c2):
                        S_.op("vector", lambda e, cc=cc: e.scalar_tensor_tensor(
                            out=ang, in0=kk, scalar=-cc, in1=ang, op0=ALU.mult, op1=ALU.add),
                            reads=[Bk, Ba], writes=[Ba])
                    S_.op("vector", lambda e: e.tensor_scalar(out=ang, in0=ang, scalar1=3.1415925, scalar2=-3.1415925,
                                                             op0=ALU.min, op1=ALU.max), reads=[Ba], writes=[Ba])
                    i = n % 2
                    n += 1
                    S_.op("scalar", lambda e, i=i: e.activation(out=tb[i], in_=ang, func=AF.Sin),
                          reads=[Ba], writes=[Btb[i]])
                    S_.dma("sync", Btb[i], tabs_d[k, :, cs], tb[i], reads=[Btb[i]], writes=[B_tabs_t[t]])
            TBS["n"] = n

        WIN_OFF = 0
        WUP_OFF = 0
        WDN_OFF = 8 * NFF
        B_win = S_.buf("win", dma=True)
        B_wup = S_.buf("wup", dma=True)
        B_wdn = S_.buf("wdn", dma=True)
        B_wo = S_.buf("wo", dma=True)

        def Win(k, c0, n):
            return R[:, WIN_OFF + k * 3072 + c0:WIN_OFF + k * 3072 + c0 + n]

        def Wup(k, c0, n):
            return R[:, WUP_OFF + k * NFF + c0:WUP_OFF + k * NFF + c0 + n]

        def Wdn(j, c0, n):
            return R[:, WDN_OFF + j * D + c0:WDN_OFF + j * D + c0 + n]

        P_win = [S_.buf("win%d" % i, dma=True) for i in range(6)]
        P_wupg = [S_.buf("wupg%d" % i, dma=True) for i in range(6)]
        P_wupv = [S_.buf("wupv%d" % i, dma=True) for i in range(6)]
        P_wdn = [S_.buf("wdn%d" % i, dma=True) for i in range(6)]

        def _piece(Bw, dst, src):
            S_.dma("gpsimd", Bw, dst, src, writes=[Bw])

        def load_win(l):
            dst = R[:, WIN_OFF:WIN_OFF + 8 * 3072].rearrange("p (k n) -> p k n", k=8)
            sv = w_in[l].rearrange("(k p) n -> p k n", p=128)
            for i in range(6):
                _piece(P_win[i], dst[:, :, 512 * i:512 * i + 512], sv[:, :, 512 * i:512 * i + 512])

        def load_ffn(l):
            dup = R[:, WUP_OFF:WUP_OFF + 8 * NFF].rearrange("p (k n) -> p k n", k=8)
            sup = w_up[l].rearrange("(k p) n -> p k n", p=128)
            ddn = R[:, WDN_OFF:WDN_OFF + 22 * D].rearrange("p (k n) -> p k n", k=22)
            sdn = w_down[l].rearrange("(k p) n -> p k n", p=128)
            for i in range(6):
                w_ = 512 if i < 5 else 256
                _piece(P_wupg[i], dup[:, :, 512 * i:512 * i + w_], sup[:, :, 512 * i:512 * i + w_])
                _piece(P_wupv[i], dup[:, :, DFF + 512 * i:DFF + 512 * i + w_], sup[:, :, DFF + 512 * i:DFF + 512 * i + w_])
                j0, j1 = 4 * i, min(4 * i + 4, 22)
                _piece(P_wdn[i], ddn[:, j0:j1, :], sdn[:, j0:j1, :])

        def load_wo(l):
            sv = w_out[l].rearrange("(k p) n -> p k n", p=128)
            first = True
            for c0 in range(0, D, 512):
                S_.dma("gpsimd", B_wo, WO[:, :, c0:c0 + 512], sv[:, :, c0:c0 + 512], writes=[B_wo] if first else [])
                first = False
            B_wo.writers = [(B_wo.sem, S_.cnt[B_wo.sem])]

        CH = ([("kA", i, 128 * i) for i in range(2)] + [("kB", i, 256 + 128 * i) for i in range(2)] +
              [("kC", i, 512 + 128 * i) for i in range(4)] + [("qA", i, 1024 + 128 * i) for i in range(2)] +
              [("qB", i, 1280 + 128 * i) for i in range(2)] + [("qC", i, 1536 + 128 * i) for i in range(4)])
        VCOL = 2048

        def phase_A(l, xsrc, B_xsrc):
            A0 = 8 * 3072
            o2 = 67584 + 4096
            xt = [Rv(A0 + i * 8192, 8192, F32).rearrange("p (c t) -> p c t", c=8) for i in range(2)]
            o = A0 + 16384
            tab = [Rv(o + i * 2048, 2048).rearrange("p (k t) -> p k t", k=4) for i in range(2)]
            o += 4096
            hT = Rv(o, 4096).rearrange("p (c t) -> p c t", c=8)
            o += 4096
            sq = Rv(o, 4096).rearrange("p (c t) -> p c t", c=8)
            o += 4096
            rstd = Rv(o, 1024, F32)
            o += 1024
            lnt = Rv(o, 1024, F32)
            o += 1024
            t1 = [Rv(o + i * 1024, 1024, F32) for i in range(2)]
            o += 2048
            t2 = [Rv(o + i * 1024, 1024, F32) for i in range(2)]
            o += 2048
            r32 = [Rv(o + i * 1024, 1024, F32) for i in range(4)]
            o += 4096
            qb16 = [Rv(o + i * 512, 512) for i in range(2)]
            o += 1024
            stg = [Rv(o + i * 512, 512) for i in range(4)]
            o += 2048
            gm = Rv(o, 256, F32)
            o += 256
            t8 = Rv(o, 64, F32)
            o += 64
            thr = Rv(o, 8, F32)
            o += 8
            btoks = [Rv(o, 64), Rv(o2 + 2048 + 64, 64)]
            B_btok = [S_.buf("btok0"), S_.buf("btok1")]
            o += 64
            stgB = Rv(o, 512, parts=slice(0, 64))
            o += 512
            assert o <= 67584
            qh = [Rv(o2 + i * 512, 512) for i in range(2)]
            ql = [Rv(o2 + 1024 + i * 512, 512) for i in range(2)]
            kmh = Rv(o2 + 2048, 32).rearrange("p (c j) -> p c j", c=2)
            kml = Rv(o2 + 2048 + 32, 32).rearrange("p (c j) -> p c j", c=2)
            assert o2 + 2048 + 128 <= 75392
            B_qh = [S_.buf("qh%d" % i, dma=True) for i in range(2)]
            B_ql = [S_.buf("ql%d" % i) for i in range(2)]
            B_kmhl = S_.buf("kmhl")
            vstg = [Rv(67584 + i * 2048, 2048).rearrange("p (h c) -> p h c", h=16) for i in range(2)]
            for i in range(2):
                S_.op("gpsimd", lambda e, i=i: e.memset(vstg[i], 1.0), writes=[B_vstg[i]])
            B_xt = [S_.buf("xt%d" % i, dma=True) for i in range(2)]
            B_tab = [S_.buf("tab%d" % i, dma=True) for i in range(2)]
            B_hT = S_.buf("hT")
            B_sq = S_.buf("sq")
            B_sqc = [S_.buf("sq%d" % c) for c in range(8)]
            B_rstd = S_.buf("rstd")
            B_lnt = S_.buf("lnt")
            B_t1 = [S_.buf("t1%d" % i) for i in range(2)]
            B_t2 = [S_.buf("t2%d" % i) for i in range(2)]
            B_r32 = [S_.buf("r32%d" % i) for i in range(4)]
            B_qb16 = [S_.buf("qb16%d" % i) for i in range(2)]
            B_stg = [S_.buf("stg%d" % i, dma=True) for i in range(4)]
            B_gm = S_.buf("gm")
            B_stgB = S_.buf("stgB", dma=True)
            xv = xsrc.rearrange("(c p) t -> p c t", p=128)
            tv = tabs_d.rearrange("k p t -> p k t")

            def gate_sub(tg, s4):
                qblk = (tg * TA + s4 * 128) // 256
                if s4 == 0:
                    S_.op("vector", lambda e: e.tensor_copy(out=kmh, in_=kmean[:]), reads=[B_kmean], writes=[B_kmhl])
                    S_.op("vector", lambda e: e.tensor_tensor(out=kml, in0=kmean[:], in1=kmh, op=ALU.subtract),
                          reads=[B_kmean, B_kmhl], writes=[B_kmhl])
                for h in range(4):
                    cbk, hh = h // 2, h % 2
                    pr_ = slice(64 * hh, 64 * hh + 64)
                    tk_ = slice(128 * s4, 128 * s4 + 128)
                    for mi, (qa_, ka_) in enumerate(((qh, kmh), (qh, kml), (ql, kmh))):
                        gb_ = 7 - hh
                        S_.op("tensor", lambda e, h=h, mi=mi, qa_=qa_, ka_=ka_: e.matmul(
                            pss[gb_][:, 16 * h:16 * h + 16], qa_[cbk][pr_, tk_], ka_[pr_, cbk, :],
                            start=(mi == 0), stop=(mi == 2)),
                            reads=[B_qh[cbk], B_ql[cbk], B_kmhl], writes=[PS[gb_]])
                for h in range(4):
                    S_.op("vector", lambda e, h=h: e.tensor_tensor(
                        out=gm[:, 16 * h:16 * h + 16], in0=pss[7 - (h % 2)][:, 16 * h:16 * h + 16],
                        in1=cf[:, CF_MASKTAB + 16 * qblk:CF_MASKTAB + 16 * qblk + 16],
                        op=ALU.add), reads=[PS[7 - (h % 2)], B_cf], writes=[B_gm])
                for h in range(4):
                    S_.op("vector", lambda e, h=h: e.max(out=t8[:, 8 * h:8 * h + 8], in_=gm[:, 16 * h:16 * h + 16]),
                          reads=[B_gm], writes=[B_gm])
                S_.op("vector", lambda e: e.tensor_scalar(
                    out=thr[:, 0:4], in0=t8[:, 2:32:8], scalar1=-1e29, scalar2=None, op0=ALU.max),
                    reads=[B_gm], writes=[B_gm])
                for h in range(4):
                    S_.op("vector", lambda e, h=h: e.tensor_scalar(
                        out=gm[:, 64 + 16 * h:64 + 16 * h + 16], in0=gm[:, 16 * h:16 * h + 16],
                        scalar1=thr[:, h:h + 1], scalar2=None, op0=ALU.is_ge), reads=[B_gm], writes=[B_gm])
                for h in range(4):
                    S_.op("vector", lambda e, h=h: e.tensor_tensor(
                        out=gm[:, 64 + 16 * h:64 + 16 * h + 16], in0=gm[:, 64 + 16 * h:64 + 16 * h + 16],
                        in1=cf[:, CF_OWNTAB + 16 * qblk:CF_OWNTAB + 16 * qblk + 16], op=ALU.max),
                        reads=[B_gm, B_cf], writes=[B_gm])
                S_.op("vector", lambda e: e.tensor_scalar(out=btoks[s4 % 2], in0=gm[:, 64:128], scalar1=-NEGM, scalar2=NEGM,
                                                         op0=ALU.mult, op1=ALU.add), reads=[B_gm], writes=[B_btok[s4 % 2]])

            def gate_sub2(tg, s4):
                S_.op("tensor", lambda e: e.matmul(pss[5][0:64, 256:384], btoks[s4 % 2], ident, start=True, stop=True),
                      reads=[B_btok[s4 % 2], B_cb], writes=[PS[5]])
                S_.op("vector", lambda e: e.tensor_copy(out=stgB[:, 128 * s4:128 * s4 + 128],
                                                       in_=pss[5][0:64, 256:384]),
                      reads=[PS[5]], writes=[B_stgB])
                if s4 == TA // 128 - 1:
                    csg = slice(tg * TA, (tg + 1) * TA)
                    for h in range(4):
                        S_.dma("sync", B_stgB, qB_d[h, 64:80, csg], stgB[16 * h:16 * h + 16, :],
                               reads=[B_stgB], writes=[B_qB[h]])

            def load_tile(t):
                i = t % 2
                cs = slice(t * TA, (t + 1) * TA)
                S_.dma("sync", B_xt[i], xt[i], xv[:, :, cs], reads=[B_xsrc], writes=[B_xt[i]])
                S_.dma("sync", B_tab[i], tab[i], tv[:, :, cs], reads=[B_tabs_t[t]], writes=[B_tab[i]])

            load_tile(0)
            cnt = {"ps": 0, "n": 0, "stg": 0}
            for t in range(min(NT_A, int(os.environ.get('KDBGT', '99')))):
                i = t % 2
                cs = slice(t * TA, (t + 1) * TA)
                if t + 1 < NT_A:
                    load_tile(t + 1)
                def stats(ti):
                    ii = ti % 2
                    for c in range(8):
                        if True:
                            S_.op("scalar", lambda e, c=c: e.activation(out=sq[:, c, :], in_=xt[ii][:, c, :], func=AF.Square),
                                  reads=[B_xt[ii]], writes=[B_sqc[c]])
                        else:
                            S_.op("gpsimd", lambda e, c=c: e.tensor_tensor(out=sq[:, c, :], in0=xt[ii][:, c, :],
                                                                          in1=xt[ii][:, c, :], op=ALU.mult),
                                  reads=[B_xt[ii]], writes=[B_sqc[c]])
                    for c in range(8):
                        S_.op("tensor", lambda e, c=c: e.matmul(pss[6][:, 0:TA], ones_bf, sq[:, c, :],
                                                               start=(c == 0), stop=(c == 7)),
                              reads=[B_sqc[c], B_cb], writes=[PS[6]])
                    S_.op("scalar", lambda e: e.activation(out=lnt, in_=pss[6][:, 0:TA], func=AF.Ln, scale=1.0 / D,
                                                          bias=ccol(CF_EPS)), reads=[PS[6], B_cf], writes=[B_lnt])
                    S_.op("scalar", lambda e: e.activation(out=rstd, in_=lnt, func=AF.Exp, scale=-0.5),
                          reads=[B_lnt], writes=[B_rstd])

                if t == 0:
                    stats(0)
                for c in range(8):
                    S_.op("vector", lambda e, c=c: e.scalar_tensor_tensor(
                        out=hT[:, c, :], in0=xt[i][:, c, :], scalar=col(PC_GMIX + 8 * l + c), in1=rstd,
                        op0=ALU.mult, op1=ALU.mult), reads=[B_xt[i], B_rstd, B_pc], writes=[B_hT])
                pend = {"f": None}

                def chunk_tail(kind, ci, pq, psw, n):
                    typA = kind in ("kA", "qA")
                    perm = cb[:, CB_PERMA:CB_PERMA + 128] if typA else cb[:, CB_PERMB:CB_PERMB + 128]
                    S_.op("tensor", lambda e, psw=psw, n=n, perm=perm: e.matmul(
                        pss[psw][:, 0:TA], perm, qb16[n], start=True, stop=True),
                        reads=[B_qb16[n], B_cb], writes=[PS[psw]])
                    tc_, ts_ = (0, 1) if typA else (2, 3)
                    S_.op("vector", lambda e, pq=pq, n=n, tc_=tc_: e.tensor_tensor(
                        out=t1[n], in0=pss[pq][:, 0:TA], in1=tab[i][:, tc_, :], op=ALU.mult),
                        reads=[PS[pq], B_tab[i]], writes=[B_t1[n]])
                    S_.op("vector", lambda e, psw=psw, n=n, ts_=ts_: e.tensor_tensor(
                        out=t2[n], in0=pss[psw][:, 0:TA], in1=tab[i][:, ts_, :], op=ALU.mult),
                        reads=[PS[psw], B_tab[i]], writes=[B_t2[n]])


                    def nstg():
                        s = cnt["stg"] % 4
                        cnt["stg"] += 1
                        return s

                    if kind in ("kA", "kC", "qC"):
                        s = nstg()
                        S_.op("gpsimd", lambda e, n=n, s=s: e.tensor_tensor(out=stg[s], in0=t1[n], in1=t2[n], op=ALU.add),
                              reads=[B_t1[n], B_t2[n]], writes=[B_stg[s]])
                        dd, Bd = {"kA": (kA_d, B_kA), "kC": (kC_d, B_kC), "qC": (qC_d, B_qC)}[kind]
                        for hh in range(2):
                            h = 2 * ci + hh
                            S_.dma("sync", B_stg[s], dd[h, :, cs], stg[s][64 * hh:64 * hh + 64, :],
                                   reads=[B_stg[s]], writes=[Bd[h]])
                    elif kind == "qA":
                        ri = 2
                        S_.op("gpsimd", lambda e, n=n: e.tensor_tensor(out=r32[ri], in0=t1[n], in1=t2[n], op=ALU.add),
                              reads=[B_t1[n], B_t2[n]], writes=[B_r32[ri]])
                        for mp in range(2):
                            s = nstg()
                            S_.op("scalar", lambda e, s=s, mp=mp: e.activation(
                                out=stg[s], in_=r32[ri], func=AF.Copy, scale=ccol(CF_M0 + mp)),
                                reads=[B_r32[ri], B_cf], writes=[B_stg[s]])
                            for hh in range(2):
                                h = 2 * ci + hh
                                S_.dma("sync", B_stg[s], qA_d[2 * h + mp, :, cs], stg[s][64 * hh:64 * hh + 64, :],
                                       reads=[B_stg[s]], writes=[B_qA[2 * h + mp]])
                    else:
                        ri = ci if kind == "kB" else 2 + ci
                        S_.op("gpsimd", lambda e, n=n, ri=ri: e.tensor_tensor(out=r32[ri], in0=t1[n], in1=t2[n], op=ALU.add),
                              reads=[B_t1[n], B_t2[n]], writes=[B_r32[ri]])
                        if kind == "kB":
                            s = nstg()
                            S_.op("gpsimd", lambda e, s=s, ri=ri: e.tensor_copy(out=stg[s], in_=r32[ri]),
                                  reads=[B_r32[ri]], writes=[B_stg[s]])
                            for hh in range(2):
                                h = 2 * ci + hh
                                S_.dma("sync", B_stg[s], kB_d[h, :, cs], stg[s][64 * hh:64 * hh + 64, :],
                                       reads=[B_stg[s]], writes=[B_kB[h]])
                        else:
                            S_.op("gpsimd", lambda e, ri=ri: e.tensor_copy(out=qh[ci], in_=r32[ri]),
                                  reads=[B_r32[ri]], writes=[B_qh[ci]])
                            S_.op("gpsimd", lambda e, ri=ri: e.tensor_tensor(out=ql[ci], in0=r32[ri], in1=qh[ci],
                                                                            op=ALU.subtract),
                                  reads=[B_r32[ri], B_qh[ci]], writes=[B_ql[ci]])
                            for hh in range(2):
                                h = 2 * ci + hh
                                S_.dma("sync", B_qh[ci], qB_d[h, 0:64, cs], qh[ci][64 * hh:64 * hh + 64, :],
                                       reads=[B_qh[ci]], writes=[B_qB[h]])
                        if kind == "kB":
                            nb = TA // 256
                            S_.op("vector", lambda e, ri=ri, ci=ci: e.tensor_reduce(
                                out=kmean[:, ci, t * nb:(t + 1) * nb],
                                in_=r32[ri].rearrange("p (b k) -> p b k", b=nb),
                                axis=mybir.AxisListType.X, op=ALU.add), reads=[B_r32[ri]], writes=[B_kmean])

                ALVL = int(os.environ.get("KDBGA", "9"))
                for chi, (kind, ci, wc) in enumerate(CH):
                    pq = cnt["ps"] % 4
                    cnt["ps"] += 1
                    psw = 4 + (cnt["ps"] % 2)
                    for k in range(8):
                        S_.op("tensor", lambda e, k=k, pq=pq, wc=wc: e.matmul(
                            pss[pq][:, 0:TA], Win(k, wc, 128), hT[:, k, :], start=(k == 0), stop=(k == 7)),
                            reads=[B_hT, P_win[wc // 512]], writes=[PS[pq]])
                    SLVL = int(os.environ.get("KDBGS", "9"))
                    if SLVL < 2:
                        continue
                    n = cnt["n"] % 2
                    cnt["n"] += 1
                    S_.op("scalar", lambda e, pq=pq, n=n: e.activation(out=qb16[n], in_=pss[pq][:, 0:TA], func=AF.Copy),
                          reads=[PS[pq]], writes=[B_qb16[n]])
                    if pend["f"] is not None:
                        pend["f"]()
                    pend["f"] = (lambda kind=kind, ci=ci, pq=pq, psw=psw, n=n: chunk_tail(kind, ci, pq, psw, n))
                    if LAZY["on"] and chi == 6 and t + 2 < NT_A:
                        tables_tile(t + 2)
                    if t >= 1 and chi in (2, 5, 8, 10, 12, 14):
                        gi_ = (2, 5, 8, 10, 12, 14).index(chi)
                        if gi_ >= 2:
                            gate_sub2(t - 1, gi_ - 2)
                        if gi_ <= 3:
                            gate_sub(t - 1, gi_)
                    if chi == 13 and t + 1 < NT_A:
                        stats(t + 1)
                if pend["f"] is not None:
                    pend["f"]()
                    pend["f"] = None

                for s4 in range(TA // 128):
                    vi = (t * 4 + s4) % 2
                    for g in range(2):
                        pq = cnt["ps"] % 4
                        cnt["ps"] += 1
                        for k in range(8):
                            S_.op("tensor", lambda e, k=k, pq=pq, g=g, s4=s4: e.matmul(
                                pss[pq][:, 0:512], hT[:, k, 128 * s4:128 * s4 + 128], Win(k, VCOL + 512 * g, 512),
                                start=(k == 0), stop=(k == 7)), reads=[B_hT, P_win[4 + g]], writes=[PS[pq]])
                        S_.op("scalar", lambda e, pq=pq, g=g, vi=vi: e.activation(
                            out=vstg[vi][:, 8 * g:8 * g + 8, 0:64],
                            in_=pss[pq][:, 0:512].rearrange("p (h d) -> p h d", h=8), func=AF.Copy),
                            reads=[PS[pq]], writes=[B_vstg[vi]])
                    r0 = t * TA + s4 * 128
                    S_.dma("sync", B_vstg[vi], vaug_d[r0:r0 + 128, :, :], vstg[vi], reads=[B_vstg[vi]], writes=[B_vaug])
            for s4_ in range(TA // 128):
                gate_sub(NT_A - 1, s4_)
                gate_sub2(NT_A - 1, s4_)
            return (B_xt + B_tab + B_sqc + [B_hT, B_sq, B_rstd, B_lnt] + B_t1 + B_t2 + B_r32 + B_qb16 + B_stg + [B_gm, B_stgB])

        def phase_B(l):
            QS = [Rv(i * 4096, 4096) for i in range(2)]
            KS = [Rv(8192 + i * 4096, 4096) for i in range(2)]
            VS = [Rv(16384 + i * 4096, 4096).rearrange("p (b c) -> p b c", c=128) for i in range(2)]
            accs = [Rv(24576, 8192, F32), Rv(66560, 8192, F32)]
            pT = [Rv(32768 + i * 512, 512) for i in range(4)]
            pT2 = [Rv(32768 + 2048 + i * 1024, 1024) for i in range(3)]
            B_pT2 = [S_.buf("pT2_%d" % i) for i in range(3)]
            o = 32768 + 2048 + 3072
            rec = [Rv(o + i * 1024, 1024, F32, parts=slice(0, 64)) for i in range(2)]
            o += 2048
            o0 = [Rv(o + i * 1024, 1024, F32, parts=slice(0, 64)) for i in range(2)]
            o += 2048
            od = Rv(o, 1024, F32, parts=slice(0, 64))
            o += 1024
            sqd = Rv(o, 512, parts=slice(0, 64))
            o += 512
            lnd = Rv(o, 1024, F32, parts=slice(0, 64))
            o += 1024
            rsd = Rv(o, 1024, F32, parts=slice(0, 64))
            o += 1024
            mstg = [Rv(o + i * 512, 512, parts=slice(0, 64)) for i in range(3)]
            o += 1536
            lnd2 = Rv(o, 1024, F32, parts=slice(64, 128))
            B_lnd2 = S_.buf("lnd2")
            o += 1024
            assert o <= 50176
            KZ = [[Rv(50176 + (2 * i + m_) * 4096, 4096) for m_ in range(2)] for i in range(2)]
            B_KZ = [S_.buf("KZ%d" % i, dma=True) for i in range(2)]
            for i_ in range(2):
                S_.op("gpsimd", lambda e: e.memset(KZ[i_][0][64:128, :], 0.0), writes=[B_KZ[i_]])
                S_.op("gpsimd", lambda e: e.memset(KZ[i_][1][0:64, :], 0.0), writes=[B_KZ[i_]])
            B_Q = [S_.buf("Q%d" % i, dma=True) for i in range(2)]
            B_K = [S_.buf("K%d" % i, dma=True) for i in range(2)]
            B_V = [S_.buf("V%d" % i, dma=True) for i in range(2)]
            B_accs = [S_.buf("acc0"), S_.buf("acc1")]
            B_pT = [S_.buf("pT%d" % i) for i in range(4)]
            B_rec = [S_.buf("rec%d" % i) for i in range(2)]
            B_o0 = [S_.buf("o0%d" % i) for i in range(2)]
            B_od = S_.buf("od")
            B_sqd = S_.buf("sqd")
            B_lnd = S_.buf("lnd")
            B_rsd = S_.buf("rsd")
            B_mstg = [S_.buf("mstg%d" % i, dma=True) for i in range(3)]
            st = {"s": 0, "p": 0, "o": 0, "m": 0, "qk": 0, "v": 0}

            def next_s():
                st["s"] = (st["s"] + 1) % 4
                return st["s"]

            def next_p():
                st["p"] = (st["p"] + 1) % 4
                return st["p"]

            def next_m():
                st["m"] = (st["m"] + 1) % 3
                return st["m"]

            deferred = []

            def run_pipeline(steps, lag=2):
                nst = len(steps)
                for i in range(nst + lag):
                    if i < nst:
                        steps[i][0]()
                        steps[i][1]()
                    j = i - lag
                    if 0 <= j < nst:
                        steps[j][2]()
                    for d_ in list(deferred):
                        d_[0] -= 1
                        if d_[0] <= 0:
                            deferred.remove(d_)
                            d_[1]()
                for d_ in list(deferred):
                    deferred.remove(d_)
                    d_[1]()

            def store_mix(m, row0, cs):
                S_.dma("sync", B_mstg[m], mixT_d[row0:row0 + 64, cs], mstg[m], reads=[B_mstg[m]], writes=[B_mixT])

            def causal_head(kap, qap, kq_bufs, vslot, nmaps, scale, fin):
                steps = []
                for Qt in range(S // 512):
                    cs = slice(Qt * 512, (Qt + 1) * 512)
                    ob = 4 + 2 * (st["o"] % 2)
                    st["o"] += 1
                    last_kt = 4 * Qt + 3
                    for kp in range((last_kt + 1) // 2):
                        for mp in range(nmaps):
                            ctx = {}

                            def qk_fn(kp=kp, mp=mp, ctx=ctx, Qt=Qt):
                                st["r"] = (st.get("r", 0) + 1) % 2
                                r = st["r"]
                                ctx["r"] = r
                                pb = 64 * mp
                                for u in range(2):
                                    kt = 2 * kp + u
                                    j = kt - 4 * Qt
                                    c0 = 128 * max(j, 0)
                                    b0 = 1024 * r + 512 * u
                                    S_.op("tensor", lambda e: e.matmul(
                                        psbig[:, b0 + c0:b0 + 512], kap(mp)[:, 128 * kt:128 * kt + 128],
                                        qap[:, Qt * 512 + c0:(Qt + 1) * 512], start=True, stop=(j < 0)),
                                        reads=kq_bufs, writes=[PS[2 * r + u]])
                                    if j >= 0:
                                        S_.op("tensor", lambda e: e.matmul(
                                            psbig[:, b0 + c0:b0 + c0 + 128], ident, cb[:, CB_TRI:CB_TRI + 128],
                                            start=False, stop=True), reads=[B_cb], writes=[PS[2 * r + u]])

                            def exp_fn(kp=kp, ctx=ctx, Qt=Qt):
                                r = ctx["r"]
                                st["p2"] = (st.get("p2", 0) + 1) % 3
                                p = st["p2"]
                                ctx["p"] = p
                                if 2 * kp + 1 < 4 * Qt:
                                    S_.op("scalar", lambda e: e.activation(
                                        out=pT2[p], in_=psbig[:, 1024 * r:1024 * r + 1024], func=AF.Exp, scale=scale),
                                        reads=[PS[2 * r], PS[2 * r + 1]], writes=[B_pT2[p]])
                                else:
                                    for u in range(2):
                                        c0 = 128 * max(2 * kp + u - 4 * Qt, 0)
                                        S_.op("scalar", lambda e: e.activation(
                                            out=pT2[p][:, 512 * u + c0:512 * u + 512],
                                            in_=psbig[:, 1024 * r + 512 * u + c0:1024 * r + 512 * u + 512],
                                            func=AF.Exp, scale=scale), reads=[PS[2 * r + u]], writes=[B_pT2[p]])

                            def pv_fn(kp=kp, mp=mp, ctx=ctx, ob=ob, last_kt=last_kt, Qt=Qt, cs=cs):
                                p = ctx["p"]
                                for u in range(2):
                                    kt = 2 * kp + u
                                    c0 = 128 * max(kt - 4 * Qt, 0)
                                    S_.op("tensor", lambda e: e.matmul(
                                        pss[ob + mp][:, c0:512], VS[vslot][:, kt, :], pT2[p][:, 512 * u + c0:512 * u + 512],
                                        start=(kt == 0), stop=(kt == last_kt)),
                                        reads=[B_V[vslot], B_pT2[p]], writes=[PS[ob + mp]])
                                if 2 * kp + 1 == last_kt and mp == nmaps - 1:
                                    fin(ob, cs)

                            steps.append((qk_fn, exp_fn, pv_fn))
                run_pipeline(steps, lag=1)

            def fin_diff(h):
                def fin(ob, cs):
                    for mp in range(2):
                        S_.op("vector", lambda e, mp=mp: e.reciprocal(out=rec[mp], in_=pss[ob + mp][64:128, :]),
                              reads=[PS[ob + mp]], writes=[B_rec[mp]])
                        S_.op("vector", lambda e, mp=mp: e.tensor_tensor(out=o0[mp], in0=pss[ob + mp][0:64, :], in1=rec[mp],
                                                                        op=ALU.mult),
                              reads=[PS[ob + mp], B_rec[mp]], writes=[B_o0[mp]])
                    S_.op("vector", lambda e: e.scalar_tensor_tensor(
                        out=od, in0=o0[1], scalar=small[0:64, 4 * l + 1:4 * l + 2], in1=o0[0], op0=ALU.mult, op1=ALU.add),
                        reads=[B_o0[0], B_o0[1], B_small], writes=[B_od])
                    S_.op("vector", lambda e: e.tensor_tensor(out=sqd, in0=od, in1=od, op=ALU.mult),
                          reads=[B_od], writes=[B_sqd])
                    deferred.append([min(12, 4 * (cs.start // 512 + 2) - 1), lambda: fin2(cs)])

                def fin2(cs):
                    s = next_s()
                    S_.op("tensor", lambda e: e.matmul(pss[s][0:64, :], cb[0:64, CB_ONES:CB_ONES + 64], sqd,
                                                      start=True, stop=True), reads=[B_sqd, B_cb], writes=[PS[s]])
                    S_.op("scalar", lambda e: e.activation(out=lnd, in_=pss[s][0:64, :], func=AF.Ln, scale=1.0 / 64,
                                                          bias=ccol(CF_EPS, slice(0, 64))),
                          reads=[PS[s], B_cf], writes=[B_lnd])
                    S_.op("scalar", lambda e: e.activation(out=rsd, in_=lnd, func=AF.Exp, scale=-0.5),
                          reads=[B_lnd], writes=[B_rsd])
                    m = next_m()
                    S_.op("vector", lambda e: e.scalar_tensor_tensor(
                        out=mstg[m], in0=od, scalar=small[0:64, 4 * l + 2:4 * l + 3], in1=rsd, op0=ALU.mult, op1=ALU.mult),
                        reads=[B_od, B_rsd, B_small], writes=[B_mstg[m]])
                    store_mix(m, 64 * h, cs)
                return fin

            def fin_plain(row0):
                def fin(ob, cs):
                    S_.op("vector", lambda e: e.reciprocal(out=rec[0], in_=pss[ob][64:128, :]),
                          reads=[PS[ob]], writes=[B_rec[0]])
                    m = next_m()
                    S_.op("vector", lambda e: e.tensor_tensor(out=mstg[m], in0=pss[ob][0:64, :], in1=rec[0], op=ALU.mult),
                          reads=[PS[ob], B_rec[0]], writes=[B_mstg[m]])
                    store_mix(m, row0, cs)
                return fin

            def load_v(vslot, h, dil):
                src = vaug_d[:, h, :]
                if dil == 1:
                    sv = src.rearrange("(n i) c -> i n c", i=128)
                    S_.dma("sync", B_V[vslot], VS[vslot][:, 0:NKT, :], sv, reads=[B_vaug], writes=[B_V[vslot]])
                else:
                    nb = S // (128 * dil)
                    sv = src.rearrange("(n i r) c -> i r n c", i=128, r=dil)
                    dv = VS[vslot][:, 0:NKT, :].rearrange("p (r n) c -> p r n c", r=dil)
                    first = True
                    for r in range(dil):
                        S_.dma("sync", B_V[vslot], dv[:, r, :, :], sv[:, r, :, :], reads=[B_vaug],
                               writes=[B_V[vslot]] if first else [])
                        first = False
                    B_V[vslot].writers = [(B_V[vslot].sem, S_.cnt[B_V[vslot].sem])]

            def nqk():
                st["qk"] += 1
                return st["qk"] % 2

            def nv():
                st["v"] += 1
                return st["v"] % 2

            def load_diff(h):
                qk = nqk()
                vs = nv()
                first = True
                for mp in range(2):
                    S_.dma("sync", B_Q[qk], QS[qk][64 * mp:64 * mp + 64, 0:S], qA_d[2 * h + mp], reads=[B_qA[2 * h + mp]],
                           writes=[B_Q[qk]] if first else [])
                    S_.dma("sync", B_KZ[qk], KZ[qk][mp][64 * mp:64 * mp + 64, 0:S], kA_d[h], reads=[B_kA[h]],
                           writes=[B_KZ[qk]] if first else [])
                    first = False
                B_Q[qk].writers = [(B_Q[qk].sem, S_.cnt[B_Q[qk].sem])]
                B_KZ[qk].writers = [(B_KZ[qk].sem, S_.cnt[B_KZ[qk].sem])]
                load_v(vs, h, 1)
                return (qk, vs)

            def comp_diff(h, ctx_):
                qk, vs = ctx_
                causal_head(lambda mp, qk=qk: KZ[qk][mp][0:128, :], QS[qk][0:128, :], [B_KZ[qk], B_Q[qk]], vs, 2,
                            32 ** -0.5, fin_diff(h))

            def load_moba(h):
                qk = nqk()
                vs = nv()
                S_.dma("sync", B_Q[qk], QS[qk][0:80, 0:S], qB_d[h], reads=[B_qB[h]], writes=[B_Q[qk]])
                S_.dma("sync", B_K[qk], KS[qk][0:64, 0:S], kB_d[h], reads=[B_kB[h]], writes=[B_K[qk]])
                S_.dma("sync", B_K[qk], KS[qk][64:80, 0:S], kind_in, reads=[], writes=[])
                B_K[qk].writers = [(B_K[qk].sem, S_.cnt[B_K[qk].sem])]
                load_v(vs, 4 + h, 1)
                return (qk, vs)

            def comp_moba(h, ctx_):
                qk, vs = ctx_
                causal_head(lambda mp, qk=qk: KS[qk][0:80, :], QS[qk][0:80, :], [B_K[qk], B_Q[qk]], vs, 1,
                            0.125, fin_plain(256 + 64 * h))

            def load_pair(pr):
                qk = nqk()
                first = True
                for hh in range(2):
                    S_.dma("sync", B_Q[qk], QS[qk][64 * hh:64 * hh + 64, 0:S], qC_d[2 * pr + hh], reads=[B_qC[2 * pr + hh]],
                           writes=[B_Q[qk]] if first else [])
                    S_.dma("sync", B_KZ[qk], KZ[qk][hh][64 * hh:64 * hh + 64, 0:S], kC_d[2 * pr + hh], reads=[B_kC[2 * pr + hh]],
                           writes=[B_KZ[qk]] if first else [])
                    first = False
                B_Q[qk].writers = [(B_Q[qk].sem, S_.cnt[B_Q[qk].sem])]
                B_KZ[qk].writers = [(B_KZ[qk].sem, S_.cnt[B_KZ[qk].sem])]
                return qk

            def comp_pair(pr, qk):
                for hh in range(2):
                    head = 2 * pr + hh
                    pb = 64 * hh
                    acc = accs[head % 2]
                    B_acc = B_accs[head % 2]
                    for di, dil in enumerate((1, 4, 16)):
                        vs = nv()
                        load_v(vs, 8 + head, dil)
                        nb = S // (128 * dil)
                        groups = []
                        ngr = (S // 128) // 4
                        for g in range(ngr):
                            if dil == 1:
                                blocks = [(0, 4 * g + b) for b in range(4)]
                                astart, ost, ist = 512 * g, 128, 1
                            elif dil == 4:
                                r, n0 = g // (nb // 4), 4 * (g % (nb // 4))
                                blocks = [(r, n0 + b) for b in range(4)]
                                astart, ost, ist = 512 * n0 + r, 512, 4
                            else:
                                n, r0 = g // (dil // 4), 4 * (g % (dil // 4))
                                blocks = [(r0 + b, n) for b in range(4)]
                                astart, ost, ist = 2048 * n + r0, 1, 16
                            groups.append((blocks, astart, ost, ist))
                        steps = []
                        for (blocks, astart, ost, ist) in groups:
                            ctx = {}

                            def tok(r, n, dil=dil):
                                t0 = 128 * n * dil + r
                                return slice(t0, t0 + 127 * dil + 1, dil) if dil > 1 else slice(t0, t0 + 128)

                            def qk_fn(blocks=blocks, ctx=ctx, tok=tok):
                                so = next_s()
                                sp = next_s()
                                ctx["so"], ctx["sp"] = so, sp
                                anyprev = any(n > 0 for (_, n) in blocks)
                                S_.op("tensor", lambda e: e.matmul(pss[so][:, :], ident, cb[:, CB_OWN4:CB_OWN4 + 512],
                                                                  start=True, stop=False), reads=[B_cb], writes=[PS[so]])
                                for b, (r, n) in enumerate(blocks):
                                    S_.op("tensor", lambda e, b=b, r=r, n=n: e.matmul(
                                        pss[so][:, 128 * b:128 * b + 128], KZ[qk][hh][0:128, tok(r, n)],
                                        QS[qk][0:128, tok(r, n)], start=False, stop=True),
                                        reads=[B_KZ[qk], B_Q[qk]], writes=[PS[so]])
                                ctx["anyprev"] = anyprev
                                if anyprev:
                                    S_.op("tensor", lambda e: e.matmul(pss[sp][:, :], ident, cb[:, CB_PREV4:CB_PREV4 + 512],
                                                                      start=True, stop=False), reads=[B_cb], writes=[PS[sp]])
                                    for b, (r, n) in enumerate(blocks):
                                        if n == 0:
                                            continue
                                        S_.op("tensor", lambda e, b=b, r=r, n=n: e.matmul(
                                            pss[sp][:, 128 * b:128 * b + 128], KZ[qk][hh][0:128, tok(r, n - 1)],
                                            QS[qk][0:128, tok(r, n)], start=False, stop=True),
                                            reads=[B_KZ[qk], B_Q[qk]], writes=[PS[sp]])

                            def exp_fn(ctx=ctx):
                                po = next_p()
                                ctx["po"] = po
                                S_.op("scalar", lambda e: e.activation(out=pT[po], in_=pss[ctx["so"]][:, :], func=AF.Exp,
                                                                      scale=0.125), reads=[PS[ctx["so"]]], writes=[B_pT[po]])
                                if ctx["anyprev"]:
                                    pp = next_p()
                                    ctx["pp"] = pp
                                    S_.op("scalar", lambda e: e.activation(out=pT[pp], in_=pss[ctx["sp"]][:, :], func=AF.Exp,
                                                                          scale=0.125),
                                          reads=[PS[ctx["sp"]]], writes=[B_pT[pp]])

                            def pv_fn(blocks=blocks, ctx=ctx, astart=astart, ost=ost, ist=ist, di=di, vs=vs, dil=dil, nb=nb):
                                ob = 4 + (st["o"] % 4)
                                st["o"] += 1
                                firstmm = True
                                for b, (r, n) in enumerate(blocks):
                                    S_.op("tensor", lambda e, b=b, r=r, n=n, firstmm=firstmm: e.matmul(
                                        pss[ob][:, 128 * b:128 * b + 128], VS[vs][:, r * nb + n, :],
                                        pT[ctx["po"]][:, 128 * b:128 * b + 128], start=firstmm, stop=(n == 0)),
                                        reads=[B_V[vs], B_pT[ctx["po"]]], writes=[PS[ob]])
                                    firstmm = False
                                    if n > 0:
                                        S_.op("tensor", lambda e, b=b, r=r, n=n: e.matmul(
                                            pss[ob][:, 128 * b:128 * b + 128], VS[vs][:, r * nb + n - 1, :],
                                            pT[ctx["pp"]][:, 128 * b:128 * b + 128], start=False, stop=True),
                                            reads=[B_V[vs], B_pT[ctx["pp"]]], writes=[PS[ob]])
                                if dil == 1:
                                    av = acc[:, astart:astart + 512].rearrange("p (b i) -> p b i", b=4)
                                elif dil == 4:
                                    n0_ = blocks[0][1]
                                    av = acc[:, 512 * n0_:512 * n0_ + 2048].rearrange("p (b i r) -> p b i r", b=4, r=4)[:, :, :, blocks[0][0]]
                                else:
                                    n_ = blocks[0][1]
                                    r0_ = blocks[0][0]
                                    av = acc[:, 2048 * n_:2048 * n_ + 2048].rearrange("p (i r) -> p r i", r=16)[:, r0_:r0_ + 4, :]
                                pv = pss[ob][:, :].rearrange("p (b i) -> p b i", b=4)
                                if di == 0:
                                    S_.op("vector", lambda e: e.tensor_copy(out=av, in_=pv), reads=[PS[ob]], writes=[B_acc])
                                else:
                                    S_.op("vector", lambda e: e.tensor_tensor(out=av, in0=pv, in1=av, op=ALU.add),
                                          reads=[PS[ob], B_acc], writes=[B_acc])

                            steps.append((qk_fn, exp_fn, pv_fn))
                        run_pipeline(steps, lag=1)
                    def fin_tile(Qt, acc=acc, B_acc=B_acc, head=head):
                        cs = slice(Qt * 512, (Qt + 1) * 512)
                        S_.op("scalar", lambda e: e.activation(out=lnd2, in_=acc[64:128, cs], func=AF.Ln),
                              reads=[B_acc], writes=[B_lnd2])
                        S_.op("scalar", lambda e: e.activation(out=rec[0], in_=lnd2, func=AF.Exp, scale=-1.0),
                              reads=[B_lnd2], writes=[B_rec[0]])
                        m = next_m()
                        S_.op("vector", lambda e: e.tensor_tensor(out=mstg[m], in0=acc[0:64, cs], in1=rec[0],
                                                                 op=ALU.mult),
                              reads=[B_acc, B_rec[0]], writes=[B_mstg[m]])
                        store_mix(m, 512 + 64 * head, cs)

                    for Qt in range(S // 512):
                        deferred.append([Qt + 2, lambda Qt=Qt, f_=fin_tile: f_(Qt)])
            jobs = ([(load_diff, comp_diff, h) for h in range(4)] + [(load_moba, comp_moba, h) for h in range(4)] +
                    [(load_pair, comp_pair, p_) for p_ in range(4)])
            ctx_ = jobs[0][0](jobs[0][2])
            for ji, (lf, cf_, arg) in enumerate(jobs):
                nxt_ = jobs[ji + 1][0](jobs[ji + 1][2]) if ji + 1 < len(jobs) else None
                cf_(arg, ctx_)
                ctx_ = nxt_
            for d_ in list(deferred):
                deferred.remove(d_)
                d_[1]()
            return (B_Q + B_K + B_KZ + B_V + B_accs + B_pT + B_pT2 + B_rec + B_o0 + [B_od, B_sqd, B_lnd, B_rsd] + B_mstg)

        def phase_C(l, xsrc, B_xsrc, is_last):
            xt = [Tv(i * 4096, 4096, F32).rearrange("p (c t) -> p c t", c=8) for i in range(2)]
            o = 8192
            mt = Tv(o, 2048).rearrange("p (c t) -> p c t", c=8)
            o += 2048
            h2e = Tv(o, 8 * (TC + 2)).rearrange("p (c t) -> p c t", c=8)
            h2 = h2e[:, :, 2:TC + 2]
            o += 8 * (TC + 2)
            cv = [Tv(o + i * 512, 512, F32) for i in range(4)]
            o += 2048
            rstd = Tv(o, 512, F32)
            o += 512
            lnt = Tv(o, 512, F32)
            o += 512
            cvb = [Tv(o + i * 512, 512, F32) for i in range(2)]
            B_cvb = [S_.buf("cvb%d" % i) for i in range(2)]
            o += 1024
            assert o <= 16448
            G0 = WDN_OFF + 22 * D
            gact = Rv(G0, 22 * TC).rearrange("p (j t) -> p j t", j=22)
            sq = Rv(G0 + 22 * TC, 8 * TC).rearrange("p (c t) -> p c t", c=8)
            htmp = Rv(G0 + 30 * TC, 4, F32)
            assert G0 + 30 * TC + 8 <= 75392
            B_xt = [S_.buf("cxt%d" % i, dma=True) for i in range(2)]
            B_mt = S_.buf("mt", dma=True)
            B_h2 = S_.buf("h2")
            B_cv = [S_.buf("cv%d" % i) for i in range(4)]
            B_rstd = S_.buf("crstd")
            B_lnt = S_.buf("clnt")
            B_g = S_.buf("gact")
            B_sq = S_.buf("csq")
            B_sqc = [S_.buf("csq%d" % c) for c in range(8)]
            B_ht = S_.buf("htmp")
            xv = xsrc.rearrange("(c p) t -> p c t", p=128)
            mv = mixT_d.rearrange("(c p) t -> p c t", p=128)
            ov = (outT if is_last else xres_d).rearrange("(c p) t -> p c t", p=128)
            cw = lambda j, ci: col(PC_CONVW + l * 132 + j * 44 + ci)
            cbias = lambda ci: col(PC_CONVB + l * 44 + ci)
            pcnt = {"a": 0, "u": 0, "cv": 0}

            def norm_stats(xi, gbase, out_fn, bank=2):
                for c in range(8):
                    if True:
                        S_.op("scalar", lambda e, c=c: e.activation(out=sq[:, c, :], in_=xt[xi][:, c, :], func=AF.Square),
                              reads=[B_xt[xi]], writes=[B_sqc[c]])
                    else:
                        S_.op("gpsimd", lambda e, c=c: e.tensor_tensor(out=sq[:, c, :], in0=xt[xi][:, c, :],
                                                                      in1=xt[xi][:, c, :], op=ALU.mult),
                              reads=[B_xt[xi]], writes=[B_sqc[c]])
                for c in range(8):
                    S_.op("tensor", lambda e, c=c: e.matmul(pss[bank][:, 0:TC], ones_bf, sq[:, c, :], start=(c == 0), stop=(c == 7)),
                          reads=[B_sqc[c], B_cb], writes=[PS[bank]])
                S_.op("scalar", lambda e: e.activation(out=lnt, in_=pss[bank][:, 0:TC], func=AF.Ln, scale=1.0 / D,
                                                      bias=ccol(CF_EPS)), reads=[PS[bank], B_cf], writes=[B_lnt])
                S_.op("scalar", lambda e: e.activation(out=rstd, in_=lnt, func=AF.Exp, scale=-0.5),
                      reads=[B_lnt], writes=[B_rstd])
                for c in range(8):
                    out_fn(c, col(gbase + c))

            def load_tile(t):
                i = t % 2
                cs = slice(t * TC, (t + 1) * TC)
                S_.dma("sync", B_xt[i], xt[i], xv[:, :, cs], reads=[B_xsrc], writes=[B_xt[i]])

            DBANK = [0, 1, 2, 7]
            B_gj = [S_.buf("gj%d" % j) for j in range(22)]

            def dacc(m):
                return pss[DBANK[m // 2]][:, TC * (m % 2):TC * (m % 2) + TC]

            def down_mm(j):
                for m in range(8):
                    S_.op("tensor", lambda e, m=m: e.matmul(
                        dacc(m), Wdn(j, 128 * m, 128), gact[:, j, :], start=(j == 0 and m % 2 == 0), stop=(j == 21),
                        skip_group_check=True), reads=[P_wdn[j // 4], B_gj[j]], writes=[PS[DBANK[m // 2]]])

            def load_mt(t):
                cs_ = slice(t * TC, (t + 1) * TC)
                S_.dma("sync", B_mt, mt, mv[:, :, cs_], reads=[B_mixT], writes=[B_mt])

            def prologue(t, obanks, sbank):
                xi = t % 2
                if t == 0:
                    S_.op("gpsimd", lambda e: e.memset(h2e[:, :, 0:2], 0.0), writes=[B_h2])
                else:
                    S_.op("scalar", lambda e: e.activation(out=h2e[:, :, 0:2], in_=h2e[:, :, TC:TC + 2], func=AF.Copy),
                          reads=[B_h2], writes=[B_h2])
                for m in range(8):
                    pa = obanks[m % 2]
                    for k in range(8):
                        S_.op("tensor", lambda e, k=k, m=m, pa=pa: e.matmul(
                            pss[pa][:, 0:TC], WO[:, k, 128 * m:128 * m + 128], mt[:, k, :], start=(k == 0), stop=(k == 7)),
                            reads=[B_wo, B_mt], writes=[PS[pa]])
                    S_.op("vector", lambda e, m=m, pa=pa: e.tensor_tensor(out=xt[xi][:, m, :], in0=pss[pa][:, 0:TC],
                                                                         in1=xt[xi][:, m, :], op=ALU.add),
                          reads=[PS[pa], B_xt[xi]], writes=[B_xt[xi]])
                if t + 1 < NT_C:
                    load_mt(t + 1)
                norm_stats(xi, PC_GFFN + 8 * l, lambda c, g: S_.op(
                    "vector", lambda e, c=c, g=g: e.scalar_tensor_tensor(out=h2[:, c, :], in0=xt[xi][:, c, :], scalar=g,
                                                                         in1=rstd, op0=ALU.mult, op1=ALU.mult),
                    reads=[B_xt[xi], B_rstd, B_pc], writes=[B_h2]), bank=sbank)

            load_tile(0)
            load_mt(0)
            prologue(0, (0, 1), 2)
            for t in range(NT_C):
                xi = t % 2
                cs = slice(t * TC, (t + 1) * TC)
                if t + 1 < NT_C:
                    load_tile(t + 1)
                for j in range(22):
                    pu = 3 + 2 * (pcnt["u"] % 2)
                    pcnt["u"] += 1
                    cvi = []
                    for half, ci in enumerate((j, 22 + j)):
                        pb = pu + half
                        for k in range(8):
                            S_.op("tensor", lambda e, k=k, ci=ci, pb=pb: e.matmul(
                                pss[pb][:, 0:TC + 2], Wup(k, 128 * ci, 128), h2e[:, k, :], start=(k == 0), stop=(k == 7)),
                                reads=[(P_wupg if ci < 22 else P_wupv)[(ci % 22) // 4], B_h2], writes=[PS[pb]])
                        c_ = pcnt["cv"] % 4
                        pcnt["cv"] += 1
                        cvi.append(c_)
                        S_.op("scalar", lambda e, ci=ci, pb=pb, c_=c_: e.activation(
                            out=cv[c_], in_=pss[pb][:, 2:TC + 2], func=AF.Identity, scale=cw(2, ci), bias=cbias(ci)),
                            reads=[PS[pb], B_pc], writes=[B_cv[c_]])
                        S_.op("vector", lambda e, ci=ci, pb=pb, c_=c_: e.scalar_tensor_tensor(
                            out=cv[c_], in0=pss[pb][:, 1:TC + 1], scalar=cw(1, ci), in1=cv[c_],
                            op0=ALU.mult, op1=ALU.add), reads=[PS[pb], B_cv[c_], B_pc], writes=[B_cv[c_]])
                        S_.op("vector", lambda e, ci=ci, pb=pb, c_=c_: e.scalar_tensor_tensor(
                            out=cv[c_], in0=pss[pb][:, 0:TC], scalar=cw(0, ci), in1=cv[c_],
                            op0=ALU.mult, op1=ALU.add), reads=[PS[pb], B_cv[c_], B_pc], writes=[B_cv[c_]])
                    if j == 21 and t + 1 < NT_C:
                        prologue(t + 1, (8 - pu, 9 - pu), 8 - pu)
                    cg, cvv = cvi
                    S_.op("scalar", lambda e, cg=cg: e.activation(out=cv[cg], in_=cv[cg], func=AF.Silu),
                          reads=[B_cv[cg]], writes=[B_cv[cg]])
                    S_.op("gpsimd", lambda e, cg=cg, cvv=cvv, j=j: e.tensor_tensor(out=gact[:, j, :], in0=cv[cg], in1=cv[cvv],
                                                                                  op=ALU.mult),
                          reads=[B_cv[cg], B_cv[cvv]], writes=[B_gj[j]])
                    if j >= 2:
                        down_mm(j - 2)
                down_mm(20)
                down_mm(21)
                for m in range(8):
                    S_.op("vector", lambda e, m=m: e.tensor_tensor(out=xt[xi][:, m, :], in0=dacc(m),
                                                                  in1=xt[xi][:, m, :], op=ALU.add),
                          reads=[PS[DBANK[m // 2]], B_xt[xi]], writes=[B_xt[xi]])
                if is_last and final:
                    norm_stats(xi, PC_GFIN, lambda c, g: S_.op(
                        "vector", lambda e, c=c, g=g: e.scalar_tensor_tensor(out=xt[xi][:, c, :], in0=xt[xi][:, c, :], scalar=g,
                                                                             in1=rstd, op0=ALU.mult, op1=ALU.mult),
                        reads=[B_xt[xi], B_rstd, B_pc], writes=[B_xt[xi]]))
                Bo = B_out if is_last else B_xres
                S_.dma("sync", B_xt[xi], ov[:, :, cs], xt[xi], reads=[B_xt[xi]], writes=[Bo])
            return B_xt + [B_mt, B_h2] + B_cv + B_cvb + [B_rstd, B_lnt, B_g, B_ht] + B_gj + B_sqc

        B_out = S_.buf("out")
        B_xin = S_.buf("xin")
        tb_bufs = tables_init()
        tables_tile(0)
        tables_tile(1)
        LAZY = {"on": True}
        prevC = []
        prevR = []
        nl = len(layers)
        for li, l in enumerate(layers):
            xsrc, B_xsrc = (xT, B_xin) if (li == 0 and first) else (xres_d, B_xres)
            S_.wait_all("gpsimd", prevR)
            load_win(l)
            S_.wait_all("sync", prevR + prevC)
            S_.wait_all("vector", prevR)
            S_.wait_all("scalar", prevR)
            S_.wait_all("tensor", prevR)
            if dbg == "T":
                break
            bufsA = phase_A(l, xsrc, B_xsrc)
            if LAZY["on"]:
                LAZY["on"] = False
                bufsA = bufsA + tb_bufs
            if dbg == "A":
                break
            for eng in ("sync", "gpsimd", "vector", "scalar", "tensor"):
                S_.wait_all(eng, bufsA + P_win)
            load_wo(l)
            bufsB = phase_B(l)
            if dbg == "B":
                break
            for eng in ("sync", "gpsimd", "vector", "scalar", "tensor"):
                S_.wait_all(eng, bufsB)
            load_ffn(l)
            is_last = (li == nl - 1)
            bufsC = phase_C(l, xsrc, B_xsrc, is_last)
            prevR = bufsC + P_wupg + P_wupv + P_wdn
            prevC = bufsC
        S_.wait_all("sync", [B_out, B_xres])
        S_.final_wait("sync")
        S_.emit()
    return nc


def _prep_shared(inputs):
    w_in = np.asarray(inputs["w_in"], np.float32)
    qa, ka, va = w_in[..., 0:256], w_in[..., 256:512], w_in[..., 512:768]
    qb, kb, vb = w_in[..., 768:1024], w_in[..., 1024:1280], w_in[..., 1280:1536]
    qc, kc, vc = w_in[..., 1536:2048], w_in[..., 2048:2560], w_in[..., 2560:3072]
    w_in_r = np.ascontiguousarray(np.concatenate([ka, kb, kc, qa, qb, qc, va, vb, vc], axis=-1))
    pc = np.zeros((128, PC_N), np.float32)
    g_mix = np.asarray(inputs["g_mix"], np.float32)
    g_ffn = np.asarray(inputs["g_ffn"], np.float32)
    g_fin = np.asarray(inputs["g_final"], np.float32)
    conv_w = np.asarray(inputs["conv_w"], np.float32)
    conv_b = np.asarray(inputs["conv_b"], np.float32)
    g_diff = np.asarray(inputs["g_diff"], np.float32)
    for l in range(DEPTH):
        pc[:, PC_GMIX + 8 * l:PC_GMIX + 8 * l + 8] = g_mix[l].reshape(8, 128).T
        pc[:, PC_GFFN + 8 * l:PC_GFFN + 8 * l + 8] = g_ffn[l].reshape(8, 128).T
        for j in range(3):
            pc[:, PC_CONVW + l * 132 + j * 44:PC_CONVW + l * 132 + j * 44 + 44] = conv_w[l, j].reshape(44, 128).T
        pc[:, PC_CONVB + l * 44:PC_CONVB + l * 44 + 44] = conv_b[l].reshape(44, 128).T
        pc[:, PC_GDIFF + l] = np.concatenate([g_diff[l], g_diff[l]])
    pc[:, PC_GFIN:PC_GFIN + 8] = g_fin.reshape(8, 128).T
    lam = np.stack([np.asarray(inputs[k], np.float32) for k in ("lambda_q1", "lambda_k1", "lambda_q2", "lambda_k2")],
                   axis=1)
    cbm, cfm = _host_consts()
    return {
        "w_in": w_in_r,
        "w_out": np.ascontiguousarray(np.asarray(inputs["w_out"], np.float32)),
        "w_up": np.ascontiguousarray(np.asarray(inputs["w_up"], np.float32)),
        "w_down": np.ascontiguousarray(np.asarray(inputs["w_down"], np.float32)),
        "pcols": pc,
        "lam_in": np.ascontiguousarray(lam.reshape(1, -1)),
        "cb": cbm,
        "cf": cfm,
    }


_NC_CACHE = {}
import os
_DBG = os.environ.get("KDBG")


def kernel(**inputs):
    x = np.asarray(inputs["x"], np.float32)
    positions = np.asarray(inputs["positions"], np.int32)
    Bn, S, _ = x.shape
    shared = _prep_shared(inputs)
    shared["kind"] = _kind(S)
    key = (S,)
    if key not in _NC_CACHE:
        _NC_CACHE[key] = build(S=S, layers=(0, 1), final=True, first=True, dbg=_DBG)
    nc = _NC_CACHE[key]
    in_maps = []
    for b in range(Bn):
        m = dict(shared)
        m["xT"] = np.ascontiguousarray(x[b].T)
        m["pos"] = np.ascontiguousarray(positions[b].reshape(1, S))
        in_maps.append(m)
    res = run_bass_kernel_spmd(nc, in_maps, core_ids=list(range(Bn)))
    out = np.stack([np.ascontiguousarray(r["outT"].T) for r in res.results], axis=0)
    return out.astype(np.float32)
```
